# Optimizing a Trainium2 kernel written in Bass

```python
import math
import jax, jax.numpy as jnp
from jax import lax
import numpy as np


D_MODEL = 2048
BATCH = 2
SEQ = 4096
DEPTH = 1
DEC_BATCH = 4
DEC_SEQ = 4096
PAST_LEN = 128

GLA_HEADS = 4
GLA_DK = 128
GLA_DV = 256
GLA_GATE_RANK = 16
GLA_TAU = 16.0
GLA_CHUNK = 64
MLA_HEADS = 8
MLA_Q_RANK = 512
MLA_KV_RANK = 512
MLA_NOPE = 128
MLA_ROPE = 64
MLA_V = 128
ROPE_BASE = 10000.0
Q_BLOCK = 128
MEM_TOKENS = 256
MEM_HEADS = 4
MEM_HEAD_DIM = D_MODEL // MEM_HEADS
D_FF = 5632
CONV_WIDTH = 3
EPS = 1e-6

GLA_WIDTH = GLA_HEADS * GLA_DV
MLA_WIDTH = MLA_HEADS * MLA_V
MIX_WIDTH = GLA_WIDTH + MLA_WIDTH
IN_SPLITS = (GLA_HEADS * GLA_DK, GLA_HEADS * GLA_DK, GLA_WIDTH, GLA_WIDTH, GLA_GATE_RANK, GLA_GATE_RANK, MLA_Q_RANK, MLA_KV_RANK, MLA_ROPE)
IN_WIDTH = sum(IN_SPLITS)

kernel_name = 'hymba_gla_mla_memxattn_convffn_bidir_encoder'


def rmsnorm(x, g):
    xf = x.astype(jnp.float32)
    y = xf * lax.rsqrt(jnp.mean(xf * xf, axis=-1, keepdims=True) + EPS)
    return (y * g.astype(jnp.float32)).astype(x.dtype)


def split_offsets():
    offs, acc = [], 0
    for w in IN_SPLITS[:-1]:
        acc += w
        offs.append(acc)
    return offs


def rope(x, positions):
    half = x.shape[-1] // 2
    freqs = ROPE_BASE ** (-jnp.arange(half, dtype=jnp.float32) / half)
    ang = positions[:, None] * freqs[None, :]
    cos = jnp.cos(ang)[None, :, None, :]
    sin = jnp.sin(ang)[None, :, None, :]
    xf = x.astype(jnp.float32)
    x1, x2 = xf[..., :half], xf[..., half:]
    out = jnp.concatenate([x1 * cos - x2 * sin, x1 * sin + x2 * cos], axis=-1)
    return out.astype(x.dtype)


def gla_direction(q, k, v, log_a, strict):
    B, S, H, DK = q.shape
    DV = v.shape[-1]
    C = GLA_CHUNK
    N = S // C

    def chunks(t):
        return t.reshape(B, N, C, H, t.shape[-1]).transpose(1, 0, 3, 2, 4)

    qc, kc, vc, la = chunks(q), chunks(k), chunks(v), chunks(log_a)
    b = jnp.cumsum(la, axis=3)
    ref = b[:, :, :, C // 2:C // 2 + 1]
    qi = qc * jnp.exp(b - ref)
    ki = kc * jnp.exp(ref - b)
    att = jnp.einsum('nbhid,nbhjd->nbhij', qi, ki)
    mask = jnp.tril(jnp.ones((C, C), dtype=bool), k=-1 if strict else 0)
    att = jnp.where(mask, att, 0.0)
    o = jnp.einsum('nbhij,nbhje->nbhie', att, vc)
    b_last = b[:, :, :, -1:]
    kv = jnp.einsum('nbhjd,nbhje->nbhde', kc * jnp.exp(b_last - b), vc)
    decay = jnp.exp(b_last[:, :, :, 0])

    def step(state, inp):
        d, kv_n = inp
        return d[..., None] * state + kv_n, state

    _, s_prev = lax.scan(step, jnp.zeros((B, H, DK, DV), jnp.float32), (decay, kv))
    o = o + jnp.einsum('nbhid,nbhde->nbhie', qc * jnp.exp(b), s_prev)
    return o.transpose(1, 0, 3, 2, 4).reshape(B, S, H, DV)


def gla_mixer(gq, gk, gv, gr, ggf, ggb, w2f, bf, w2b, bb, out_norm):
    B, S, _ = gq.shape
    f32 = jnp.float32
    q = gq.reshape(B, S, GLA_HEADS, GLA_DK).astype(f32) * (GLA_DK ** -0.5)
    k = gk.reshape(B, S, GLA_HEADS, GLA_DK).astype(f32)
    v = gv.reshape(B, S, GLA_HEADS, GLA_DV).astype(f32)
    la_f = (jax.nn.log_sigmoid((ggf @ w2f + bf).astype(f32)) / GLA_TAU).reshape(B, S, GLA_HEADS, GLA_DK)
    la_b = (jax.nn.log_sigmoid((ggb @ w2b + bb).astype(f32)) / GLA_TAU).reshape(B, S, GLA_HEADS, GLA_DK)
    o_f = gla_direction(q, k, v, la_f, False)
    flip = lambda t: jnp.flip(t, axis=1)
    o_b = flip(gla_direction(flip(q), flip(k), flip(v), flip(la_b), True))
    o = o_f + o_b
    o = o * lax.rsqrt(jnp.mean(o * o, axis=-1, keepdims=True) + EPS) * out_norm.astype(f32)
    return o.reshape(B, S, GLA_WIDTH).astype(gq.dtype) * jax.nn.silu(gr)


def mla_mixer(cq, ckv, kr, q_norm, w_q_up, kv_norm, w_kv_up):
    B, S, _ = cq.shape
    f32 = jnp.float32
    pos = jnp.arange(S, dtype=f32)
    q = (rmsnorm(cq, q_norm) @ w_q_up).reshape(B, S, MLA_HEADS, MLA_NOPE + MLA_ROPE)
    q = jnp.concatenate([q[..., :MLA_NOPE], rope(q[..., MLA_NOPE:], pos)], axis=-1)
    kv = (rmsnorm(ckv, kv_norm) @ w_kv_up).reshape(B, S, MLA_HEADS, MLA_NOPE + MLA_V)
    k_nope, v = kv[..., :MLA_NOPE], kv[..., MLA_NOPE:]
    k_pe = rope(kr.reshape(B, S, 1, MLA_ROPE), pos)
    k = jnp.concatenate([k_nope, jnp.broadcast_to(k_pe, (B, S, MLA_HEADS, MLA_ROPE))], axis=-1)
    scale = (MLA_NOPE + MLA_ROPE) ** -0.5
    nb = S // Q_BLOCK
    qb = (q.astype(f32) * scale).reshape(B, nb, Q_BLOCK, MLA_HEADS, MLA_NOPE + MLA_ROPE).transpose(1, 0, 2, 3, 4)
    kf = k.astype(f32)
    vf = v.astype(f32)

    def block(qblk):
        s = jnp.einsum('bqhd,bkhd->bhqk', qblk, kf)
        p = jax.nn.softmax(s, axis=-1)
        return jnp.einsum('bhqk,bkhd->bqhd', p, vf)

    o = lax.map(block, qb)
    return o.transpose(1, 0, 2, 3, 4).reshape(B, S, MLA_WIDTH).astype(cq.dtype)


def memory_xattn(h, m, wq, wk, wv, wo):
    B, S, _ = h.shape
    M = m.shape[1]
    f32 = jnp.float32
    q = (h @ wq).reshape(B, S, MEM_HEADS, MEM_HEAD_DIM).astype(f32)
    k = (m @ wk).reshape(B, M, MEM_HEADS, MEM_HEAD_DIM).astype(f32)
    v = (m @ wv).reshape(B, M, MEM_HEADS, MEM_HEAD_DIM).astype(f32)
    p = jax.nn.softmax(jnp.einsum('bqhd,bkhd->bhqk', q, k) * (MEM_HEAD_DIM ** -0.5), axis=-1)
    o = jnp.einsum('bhqk,bkhd->bqhd', p, v).reshape(B, S, D_MODEL).astype(h.dtype)
    return o @ wo


def conv_ffn(h, w_up, conv_w, conv_b, w_down):
    S = h.shape[1]
    u = h @ w_up
    pad = CONV_WIDTH // 2
    up = jnp.pad(u, ((0, 0), (pad, CONV_WIDTH - 1 - pad), (0, 0)))
    u = sum(up[:, j:j + S] * conv_w[j] for j in range(CONV_WIDTH)) + conv_b
    g, val = u[..., :D_FF], u[..., D_FF:]
    return (jax.nn.gelu(g, approximate=True) * val) @ w_down


def apply_layer(x, mem, lp):
    h = rmsnorm(x, lp['norm_mix_pre'])
    gq, gk, gv, gr, ggf, ggb, cq, ckv, kr = jnp.split(h @ lp['w_in'], split_offsets(), axis=-1)
    o_gla = gla_mixer(gq, gk, gv, gr, ggf, ggb, lp['gla_gate_w2_fwd'], lp['gla_gate_b_fwd'],
                      lp['gla_gate_w2_bwd'], lp['gla_gate_b_bwd'], lp['gla_out_norm'])
    o_mla = mla_mixer(cq, ckv, kr, lp['mla_q_norm'], lp['mla_w_q_up'], lp['mla_kv_norm'], lp['mla_w_kv_up'])
    y = jnp.concatenate([o_gla, o_mla], axis=-1) @ lp['w_out']
    x = x + rmsnorm(y, lp['norm_mix_post'])
    h = rmsnorm(x, lp['norm_mem_pre'])
    m = rmsnorm(mem, lp['mem_kv_norm'])
    y = memory_xattn(h, m, lp['w_mem_q'], lp['w_mem_k'], lp['w_mem_v'], lp['w_mem_o'])
    x = x + rmsnorm(y, lp['norm_mem_post'])
    h = rmsnorm(x, lp['norm_ffn_pre'])
    y = conv_ffn(h, lp['w_ffn_up'], lp['ffn_conv_w'], lp['ffn_conv_b'], lp['w_ffn_down'])
    return x + rmsnorm(y, lp['norm_ffn_post'])


def setup_inputs(seed: int = 0) -> dict:
    key = jax.random.key(seed)
    ks = jax.random.split(key, 40)
    f32 = jnp.float32

    def w(k, shape, fan_in):
        return jax.random.normal(k, shape, f32) * (fan_in ** -0.5)

    def gain(k, n):
        return 1.0 + 0.05 * jax.random.normal(k, (DEPTH, n), f32)

    def bias(k, shape, s):
        return s * jax.random.normal(k, shape, f32)

    L = DEPTH
    return {
        'x_prompt': jax.random.normal(ks[0], (BATCH, SEQ, D_MODEL), f32),
        'x_sample': jax.random.normal(ks[1], (DEC_BATCH, DEC_SEQ, D_MODEL), f32),
        'mem_prompt': jax.random.normal(ks[2], (BATCH, MEM_TOKENS, D_MODEL), f32),
        'mem_sample': jax.random.normal(ks[3], (DEC_BATCH, MEM_TOKENS, D_MODEL), f32),
        'norm_mix_pre': gain(ks[4], D_MODEL),
        'norm_mix_post': gain(ks[5], D_MODEL),
        'w_in': w(ks[6], (L, D_MODEL, IN_WIDTH), D_MODEL),
        'gla_gate_w2_fwd': w(ks[7], (L, GLA_GATE_RANK, GLA_HEADS * GLA_DK), GLA_GATE_RANK),
        'gla_gate_b_fwd': bias(ks[8], (L, GLA_HEADS * GLA_DK), 0.1),
        'gla_gate_w2_bwd': w(ks[9], (L, GLA_GATE_RANK, GLA_HEADS * GLA_DK), GLA_GATE_RANK),
        'gla_gate_b_bwd': bias(ks[10], (L, GLA_HEADS * GLA_DK), 0.1),
        'gla_out_norm': gain(ks[11], GLA_DV),
        'mla_q_norm': gain(ks[12], MLA_Q_RANK),
        'mla_w_q_up': w(ks[13], (L, MLA_Q_RANK, MLA_HEADS * (MLA_NOPE + MLA_ROPE)), MLA_Q_RANK),
        'mla_kv_norm': gain(ks[14], MLA_KV_RANK),
        'mla_w_kv_up': w(ks[15], (L, MLA_KV_RANK, MLA_HEADS * (MLA_NOPE + MLA_V)), MLA_KV_RANK),
        'w_out': w(ks[16], (L, MIX_WIDTH, D_MODEL), MIX_WIDTH),
        'norm_mem_pre': gain(ks[17], D_MODEL),
        'norm_mem_post': gain(ks[18], D_MODEL),
        'mem_kv_norm': gain(ks[19], D_MODEL),
        'w_mem_q': w(ks[20], (L, D_MODEL, D_MODEL), D_MODEL),
        'w_mem_k': w(ks[21], (L, D_MODEL, D_MODEL), D_MODEL),
        'w_mem_v': w(ks[22], (L, D_MODEL, D_MODEL), D_MODEL),
        'w_mem_o': w(ks[23], (L, D_MODEL, D_MODEL), D_MODEL),
        'norm_ffn_pre': gain(ks[24], D_MODEL),
        'norm_ffn_post': gain(ks[25], D_MODEL),
        'w_ffn_up': w(ks[26], (L, D_MODEL, 2 * D_FF), D_MODEL),
        'ffn_conv_w': w(ks[27], (L, CONV_WIDTH, 2 * D_FF), CONV_WIDTH),
        'ffn_conv_b': bias(ks[28], (L, 2 * D_FF), 0.01),
        'w_ffn_down': w(ks[29], (L, D_FF, D_MODEL), D_FF),
    }


def reference(x_prompt, x_sample, mem_prompt, mem_sample, norm_mix_pre, norm_mix_post, w_in,
              gla_gate_w2_fwd, gla_gate_b_fwd, gla_gate_w2_bwd, gla_gate_b_bwd, gla_out_norm,
              mla_q_norm, mla_w_q_up, mla_kv_norm, mla_w_kv_up, w_out,
              norm_mem_pre, norm_mem_post, mem_kv_norm, w_mem_q, w_mem_k, w_mem_v, w_mem_o,
              norm_ffn_pre, norm_ffn_post, w_ffn_up, ffn_conv_w, ffn_conv_b, w_ffn_down):
    yp, ys = x_prompt, x_sample
    for l in range(DEPTH):
        lp = {
            'norm_mix_pre': norm_mix_pre[l], 'norm_mix_post': norm_mix_post[l], 'w_in': w_in[l],
            'gla_gate_w2_fwd': gla_gate_w2_fwd[l], 'gla_gate_b_fwd': gla_gate_b_fwd[l],
            'gla_gate_w2_bwd': gla_gate_w2_bwd[l], 'gla_gate_b_bwd': gla_gate_b_bwd[l],
            'gla_out_norm': gla_out_norm[l],
            'mla_q_norm': mla_q_norm[l], 'mla_w_q_up': mla_w_q_up[l],
            'mla_kv_norm': mla_kv_norm[l], 'mla_w_kv_up': mla_w_kv_up[l], 'w_out': w_out[l],
            'norm_mem_pre': norm_mem_pre[l], 'norm_mem_post': norm_mem_post[l], 'mem_kv_norm': mem_kv_norm[l],
            'w_mem_q': w_mem_q[l], 'w_mem_k': w_mem_k[l], 'w_mem_v': w_mem_v[l], 'w_mem_o': w_mem_o[l],
            'norm_ffn_pre': norm_ffn_pre[l], 'norm_ffn_post': norm_ffn_post[l],
            'w_ffn_up': w_ffn_up[l], 'ffn_conv_w': ffn_conv_w[l], 'ffn_conv_b': ffn_conv_b[l],
            'w_ffn_down': w_ffn_down[l],
        }
        yp = apply_layer(yp, mem_prompt, lp)
        ys = apply_layer(ys, mem_sample, lp)
    return (yp, ys)
```

```python
from contextlib import ExitStack
import numpy as np
import concourse.bass as bass
import concourse.mybir as mybir
from concourse.bass_utils import run_bass_kernel_spmd

F32 = mybir.dt.float32
BF16 = mybir.dt.bfloat16
AF = mybir.ActivationFunctionType
ALU = mybir.AluOpType
AX = mybir.AxisListType

S = 4096
D = 2048
KC = 16
NT = 8
TT = 512
IN_W = 4192
DFF = 5632
EPS = 1e-6


class Tl:
    __slots__ = ("name", "lw", "rd", "dsem", "dcnt", "const", "persist")

    def __init__(self, name, const=False):
        self.persist = False
        self.name = name
        self.lw = None
        self.rd = {}
        self.dsem = None
        self.dcnt = 0
        self.const = const


class Op:
    __slots__ = ("eng", "fn", "deps", "eidx", "is_dma", "done", "signal", "sigidx", "waits")

    def __init__(self, eng, fn):
        self.eng = eng
        self.fn = fn
        self.deps = []
        self.is_dma = False
        self.done = None
        self.signal = False
        self.sigidx = 0
        self.waits = []


ENGS = ("pe", "act", "dve", "pool", "sp")


class Sched:
    def __init__(self, nc):
        self.nc = nc
        self.ops = {e: [] for e in ENGS}
        self.esem = {e: nc.alloc_semaphore(name=f"es_{e}") for e in ("pe", "act", "dve", "pool")}
        self.ecount = {e: 0 for e in ENGS}
        self.sigcount = {e: 0 for e in ENGS}
        self.waited = {e: {} for e in ENGS}
        self.waited_dma = {e: {} for e in ENGS}
        self.dma_sems = []
        self.last_op = {e: None for e in ENGS}
        self.nsem = 0
        self.free_sems = []

    def _track(self, o, r, w):
        deps = {}
        for t in r:
            if t.lw is not None:
                deps[id(t.lw)] = t.lw
        for t in w:
            if t.lw is not None:
                deps[id(t.lw)] = t.lw
            for x in t.rd.values():
                deps[id(x)] = x
        deps.pop(id(o), None)
        o.deps = list(deps.values())
        for t in r:
            if not t.const:
                key = ("dma", id(o)) if o.is_dma else o.eng
                t.rd[key] = o
        for t in w:
            t.lw = o
            t.rd = {}

    def op(self, eng, fn, r=(), w=()):
        o = Op(eng, fn)
        self.ecount[eng] += 1
        o.eidx = self.ecount[eng]
        self._track(o, r, w)
        self.ops[eng].append(o)
        self.last_op[eng] = o
        return o

    def dma(self, q, out, in_, key, r=(), w=(), **kw):
        def fn(e):
            return e.dma_start(out=out, in_=in_, **kw)
        o = Op(q, fn)
        o.is_dma = True
        if key.dsem is None:
            if self.free_sems:
                key.dsem, key.dcnt = self.free_sems.pop()
            else:
                key.dsem = self.nc.alloc_semaphore(name=f"ds{self.nsem}")
                self.nsem += 1
            self.dma_sems.append(key)
        key.dcnt += 16
        o.done = (key.dsem, key.dcnt)
        self.ecount[q] += 1
        o.eidx = self.ecount[q]
        self._track(o, r, w)
        self.ops[q].append(o)
        return o

    def barrier(self):
        lasts = [o for o in self.last_op.values() if o is not None]
        keys = [k_ for k_ in self.dma_sems if not k_.persist]
        for e in ENGS:
            def fn(eng):
                return None
            o = Op(e, fn)
            self.ecount[e] += 1
            o.eidx = self.ecount[e]
            o.deps = [x for x in lasts if x.eng != e]
            for k in keys:
                d = Op("sp", None)
                d.is_dma = True
                d.done = (k.dsem, k.dcnt)
                o.deps.append(d)
            self.ops[e].append(o)
        for k_ in keys:
            self.free_sems.append((k_.dsem, k_.dcnt))
        self.dma_sems = [k_ for k_ in self.dma_sems if k_.persist]

    def emit(self):
        nc = self.nc
        for e in ENGS:
            for o in self.ops[e]:
                for d in o.deps:
                    if d.is_dma:
                        sem, val = d.done
                        if self.waited_dma[e].get(sem.num, 0) >= val:
                            continue
                        self.waited_dma[e][sem.num] = val
                        o.waits.append(("d", sem, val))
                    else:
                        p = d.eng
                        if p == e and e == "pe":
                            continue
                        if self.waited[e].get(p, 0) >= d.eidx:
                            continue
                        self.waited[e][p] = d.eidx
                        d.signal = True
                        o.waits.append(("e", d))
        for e in ENGS:
            c = self.sigcount[e]
            for o in self.ops[e]:
                if o.signal:
                    c += 1
                    o.sigidx = c
            self.sigcount[e] = c
        engmap = {"pe": "tensor", "act": "scalar", "dve": "vector", "pool": "gpsimd", "sp": "sync"}
        with nc.Block() as block:
            for e in ENGS:
                ops = self.ops[e]

                def body(eng, ops=ops, e=e):
                    for o in ops:
                        for wt in o.waits:
                            if wt[0] == "d":
                                eng.wait_ge(wt[1], wt[2])
                            else:
                                d = wt[1]
                                eng.wait_ge(self.esem[d.eng], d.sigidx)
                        ins = o.fn(eng)
                        if o.is_dma:
                            ins.then_inc(o.done[0], 16)
                        elif o.signal:
                            assert ins is not None
                            ins.then_inc(self.esem[e], 1)
                getattr(block, engmap[e])(body)
        self.ops = {e: [] for e in ENGS}


class Rot:
    def __init__(self, tiles):
        self.tiles = tiles
        self.i = 0

    def next(self):
        t = self.tiles[self.i % len(self.tiles)]
        self.i += 1
        return t


class Buf:
    def __init__(self, es, nc, name, shape, dt, psum=False, const=False):
        if psum:
            self.t = es.enter_context(nc.psum_tensor(name, shape, dt))
        else:
            self.t = es.enter_context(nc.sbuf_tensor(name, shape, dt))
        self.tl = Tl(name, const=const)

    def __getitem__(self, idx):
        return self.t[idx]


class View:
    def __init__(self, base, idx, name):
        self.base = base
        self.idx = idx
        self.tl = Tl(name)

    def __getitem__(self, idx):
        return self.base.t[self.idx][idx]


def mk(es, nc, name, shape, dt, n=1, psum=False):
    return Rot([Buf(es, nc, f"{name}{i}", shape, dt, psum=psum) for i in range(n)])


class K:
    def __init__(self, nc, dbg=None, stop_after=99):
        self.nc = nc
        self.sc = Sched(nc)
        self.dbg = dbg or {}
        self.stop_after = stop_after
        self.dram = {}
        self.dtl = {}

    def din(self, name, shape, dt=F32):
        self.dram[name] = self.nc.dram_tensor(name, list(shape), dt, kind="ExternalInput").ap()
        self.dtl[name] = Tl(name, const=True)
        return self.dram[name]

    def dout(self, name, shape, dt=F32):
        self.dram[name] = self.nc.dram_tensor(name, list(shape), dt, kind="ExternalOutput").ap()
        return self.dram[name]

    def dscr(self, name, shape, dt=BF16):
        kind = "ExternalOutput" if name in self.dbg else "Internal"
        self.dram[name] = self.nc.dram_tensor(name, list(shape), dt, kind=kind).ap()
        return self.dram[name]

    def act(self, out, in_, func, r, w, bias=None, scale=None, accum=None):
        kw = {}
        if bias is not None:
            kw["bias"] = bias
        if scale is not None:
            kw["scale"] = scale
        if accum is not None:
            kw["accum_out"] = accum
        return self.sc.op("act", lambda e: e.activation(out=out, in_=in_, func=func, **kw), r, w)

    def tt(self, out, a, b, op, r, w, eng="dve"):
        return self.sc.op(eng, lambda e: e.tensor_tensor(out=out, in0=a, in1=b, op=op), r, w)

    def ts(self, out, a, s1, s2, op0, op1, r, w, eng="dve"):
        if op1 is None:
            return self.sc.op(eng, lambda e: e.tensor_scalar(out=out, in0=a, scalar1=s1, scalar2=None, op0=op0), r, w)
        return self.sc.op(eng, lambda e: e.tensor_scalar(out=out, in0=a, scalar1=s1, scalar2=s2, op0=op0, op1=op1), r, w)

    def stt(self, out, a, s, b, op0, op1, r, w):
        return self.sc.op("dve", lambda e: e.scalar_tensor_tensor(out=out, in0=a, scalar=s, in1=b, op0=op0, op1=op1), r, w)

    def cp(self, out, in_, r, w, eng="dve"):
        if eng == "act":
            return self.sc.op("act", lambda e: e.activation(out=out, in_=in_, func=AF.Copy), r, w)
        return self.sc.op(eng, lambda e: e.tensor_copy(out=out, in_=in_), r, w)

    def recip(self, out, in_, r, w):
        return self.sc.op("dve", lambda e: e.reciprocal(out=out, in_=in_), r, w)

    def mm(self, out, lhsT, rhs, start, stop, r, w):
        return self.sc.op("pe", lambda e: e.matmul(out, lhsT, rhs, start=start, stop=stop), r, w)

    def tr(self, out, in_, ident, r, w):
        return self.sc.op("pe", lambda e: e.transpose(out, in_, ident), r, w)

    def ld(self, out, in_, key, r=(), q="sp", **kw):
        return self.sc.dma(q, out, in_, key.tl, r=r, w=[key.tl], **kw)

    def st(self, out, in_, key, w=(), q="pool", **kw):
        return self.sc.dma(q, out, in_, key.tl, r=[key.tl], w=w, **kw)

    def memset(self, ap, val, w, eng="dve"):
        return self.sc.op(eng, lambda e: e.memset(ap, val), (), w)

    def rstd_from_ss(self, rstd, ss, tmp, n, r, w):
        self.ts(tmp, ss, 1.0 / n, EPS, ALU.mult, ALU.add, r, w)
        self.act(tmp, tmp, AF.Sqrt, w, w)
        self.recip(rstd, tmp, w, w)


def build(nc, dbg=None, stop_after=99):
    k = K(nc, dbg, stop_after)
    sc = k.sc
    x = k.din("x", [S, D])
    mem = k.din("mem", [256, D])
    w_in = k.din("w_in", [D, IN_W])
    w_q_up = k.din("w_q_up", [512, 1536])
    w_kv_up = k.din("w_kv_up", [512, 2048])
    w_out = k.din("w_out", [D, D])
    w_mq = k.din("w_mq", [D, D])
    w_mk = k.din("w_mk", [D, D])
    w_mv = k.din("w_mv", [D, D])
    w_mo = k.din("w_mo", [D, D])
    w_up = k.din("w_up", [D, 2 * DFF])
    w_dn = k.din("w_dn", [DFF, D])
    consts = k.din("consts", [128, 1408])
    gcols = k.din("gcols", [128, 72])
    gpost = k.din("gpost", [128, 3 * D + 256])
    convp = k.din("convp", [128, 4 * 88])
    w2aug = k.din("w2aug", [33, 1024])
    ropet = k.din("ropet", [64, 2 * S])
    y_out = k.dout("y", [S, D])

    Wi = k.dscr("Wi", [D, IN_W])
    Wq = k.dscr("Wq", [512, 1536])
    Wkv = k.dscr("Wkv", [512, 2048])
    Wo = k.dscr("Wo", [D, D])
    Wmq = k.dscr("Wmq", [D, D])
    Wmk = k.dscr("Wmk", [D, D])
    Wmv = k.dscr("Wmv", [D, D])
    Wmo = k.dscr("Wmo", [D, D])
    Wup = k.dscr("Wup", [D, 2 * DFF])
    Wdn = k.dscr("Wdn", [DFF, D])
    Wkrr = k.dscr("Wkrr", [D, 64])
    Wqr = k.dscr("Wqr", [512, 512])
    H1T = k.dscr("H1T", [128, KC, S])
    MKT = k.dscr("MKT", [128, KC, 256])
    MV = k.dscr("MV", [2, 128, D])
    SILU = k.dscr("SILU", [S, 1024], F32)
    QBB = k.dscr("QBB", [128, 4, S])
    DECB = k.dscr("DECB", [128, 4, 64], F32)
    KDB = k.dscr("KDB", [S, 512])
    VG = k.dscr("VG", [S, 1024])
    OP = k.dscr("OP", [S, 1024], F32)
    silu_tl = [Tl(f"SILU{t}") for t in range(NT)]
    qbb_tl = [Tl(f"QBB{t}") for t in range(NT)]
    decb_tl = [Tl(f"DECB{t}") for t in range(NT)]
    kdb_tl = [Tl(f"KDB{t}") for t in range(NT)]
    vg_tl = [Tl(f"VG{t}") for t in range(NT)]
    op_tl = [Tl(f"OP{t}") for t in range(NT)]
    KPE = k.dscr("KPE", [64, S])
    QTN = k.dscr("QTN", [8, 128, S])
    QTR = k.dscr("QTR", [8, 64, S])
    KTN = k.dscr("KTN", [8, 128, S])
    VS = k.dscr("VS", [8, 128, 32, 128])
    OGT = k.dscr("OGT", [128, 8, S])
    OMT = k.dscr("OMT", [128, 8, S])
    kpe_tl = [Tl(f"KPE{t}") for t in range(NT)]
    qtn_tl = [Tl(f"QTN{t}") for t in range(NT)]
    qtr_tl = [Tl(f"QTR{t}") for t in range(NT)]
    ktn_tl = [Tl(f"KTN{t}") for t in range(NT)]
    vs_tl = [Tl(f"VS{t}") for t in range(NT)]
    ogt_tl = [Tl(f"OGT{t}") for t in range(NT)]
    omt_tl = [Tl(f"OMT{t}") for t in range(NT)]
    X2 = k.dscr("X2", [S, D], F32)
    H3T = k.dscr("H3T", [128, KC, S])
    AT = k.dscr("AT", [128, 44, S])
    x2_tl = [Tl(f"X2{t}") for t in range(NT)]
    h3t_tl = [Tl(f"H3T{t}") for t in range(NT)]
    at_tl = [Tl(f"AT{t}") for t in range(NT)]
    yout_tl = Tl("yout")
    mkt_tl = Tl("MKT")
    mv_tl = Tl("MV")
    h1t_tl = [Tl(f"H1T{t}") for t in range(NT)]
    wt = {n: Tl(n, const=True) for n in ("Wi", "Wq", "Wkv", "Wo", "Wmq", "Wmk", "Wmv", "Wmo", "Wup", "Wdn", "Wkrr", "Wqr")}

    def cast_chunks(dst, src, name, rows, rchunk):
        tl = wt[name]
        key = Tl("k_" + name)
        key.persist = True
        opsl = []
        n = (rows + rchunk - 1) // rchunk

        def issue(r0):
            r1 = min(rows, r0 + rchunk)
            opsl.append(sc.dma("pool", dst[r0:r1, :], src[r0:r1, :], key, r=(), w=(), max_dma_last_dim=8192))
            if len(opsl) == n:
                for o in opsl:
                    o.done = (key.dsem, key.dcnt)
                tl.lw = opsl[-1]
        return [(lambda r0=r0: issue(r0)) for r0 in range(0, rows, rchunk)]

    def cast_w(dst, src, name, rows, cols, rchunk):
        for th in cast_chunks(dst, src, name, rows, rchunk):
            th()

    with ExitStack() as es:
        cast_w(Wmk, w_mk, "Wmk", D, D, 1024)
        cast_w(Wmv, w_mv, "Wmv", D, D, 1024)
        cast_w(Wi, w_in, "Wi", D, IN_W, 512)
        cast_w(Wq, w_q_up, "Wq", 512, 1536, 512)
        cast_w(Wkv, w_kv_up, "Wkv", 512, 2048, 512)


        cst = Buf(es, nc, "cst0", [128, 1408], F32)
        k.ld(cst[:], consts, cst)
        identb = Buf(es, nc, "identb0", [128, 128], BF16)
        k.cp(identb[:], cst[:, 0:128], [cst.tl], [identb.tl])
        gc = Buf(es, nc, "gc0", [128, 72], F32)
        k.ld(gc[:], gcols, gc)
        xs = mk(es, nc, "p1xs", [128, D], BF16, n=4)
        stt_ = mk(es, nc, "p1st", [128, 4], F32, n=4)
        pT = mk(es, nc, "p1pT", [128, 512], BF16, n=2, psum=True)

        xb = mk(es, nc, "p1xb", [128, D], F32, n=6)
        hT = mk(es, nc, "p1hT", [128, KC, TT], BF16, n=2)
        for t in range(NT):
            blocks = []
            for b in range(4):
                xbuf = xb.next()
                k.ld(xbuf[:], x[t * 512 + b * 128: t * 512 + (b + 1) * 128, :], xbuf)
                blocks.append(xbuf)
            h = hT.next()
            norm_T(k, blocks, xs.tiles, stt_.tiles, gc, 0, identb, pT, h)
            k.st(H1T[:, :, t * 512:(t + 1) * 512], h[:], h, w=[h1t_tl[t]], q="act")

        tkr = Buf(es, nc, "tkr", [128, KC, 64], F32)
        k.ld(tkr[:], w_in[:, 4128:4192].rearrange("(kc p) c -> p kc c", p=128), tkr)
        rkr = Buf(es, nc, "rkr", [128, KC, 64], BF16)
        k.ts(rkr[:, :, 0:32], tkr[:, :, 32:64], -1.0, None, ALU.mult, None, [tkr.tl], [rkr.tl])
        k.cp(rkr[:, :, 32:64], tkr[:, :, 0:32], [tkr.tl], [rkr.tl], eng="act")
        k.st(Wkrr.rearrange("(kc p) c -> p kc c", p=128), rkr[:], rkr, w=[wt["Wkrr"]], q="act")
        tq = Buf(es, nc, "tq", [128, 4, 8, 64], F32)
        wqv = w_q_up.rearrange("(kc p) (h c) -> p kc h c", p=128, c=192)
        for kc in range(4):
            k.ld(tq[:, kc, :, :], wqv[:, kc, :, 128:192], tq)
        rq = Buf(es, nc, "rq", [128, 4, 8, 64], BF16)
        for kc in range(4):
            k.ts(rq[:, kc, :, 0:32], tq[:, kc, :, 32:64], -1.0, None, ALU.mult, None, [tq.tl], [rq.tl])
            k.cp(rq[:, kc, :, 32:64], tq[:, kc, :, 0:32], [tq.tl], [rq.tl], eng="act")
        for kc in range(4):
            k.st(Wqr.rearrange("(kc p) (h c) -> p kc h c", p=128, c=64)[:, kc, :, :], rq[:, kc, :, :], rq, w=[wt["Wqr"]], q="act")

        mT = Buf(es, nc, "p0mT", [128, KC, 256], BF16)
        blocks = []
        for b in range(2):
            xbuf = xb.next()
            k.ld(xbuf[:], mem[b * 128:(b + 1) * 128, :], xbuf)
            blocks.append(xbuf)
        norm_T(k, blocks, xs.tiles, stt_.tiles, gc, 48, identb, pT, mT)
        wb = mk(es, nc, "p0wb", [128, KC, 512], BF16, n=2)
        ps = mk(es, nc, "p0ps", [128, 512], F32, n=3, psum=True)
        mkt = Buf(es, nc, "p0mkt", [128, KC, 256], BF16)
        mvs = Buf(es, nc, "p0mvs", [128, 2, D], BF16)
        for g in range(4):
            w_ = wb.next()
            ldw(k, wt, w_, Wmk, "Wmk", 0, KC, g * 512, 512)
            for j in range(4):
                p_ = ps.next()
                for kc in range(KC):
                    k.mm(p_[:, 0:256], w_[:, kc, j * 128:(j + 1) * 128], mT[:, kc, :], kc == 0, kc == KC - 1,
                         [w_.tl, mT.tl], [p_.tl])
                k.cp(mkt[:, g * 4 + j, :], p_[:, 0:256], [p_.tl], [mkt.tl], eng=("act" if j % 2 else "dve"))
        for g in range(4):
            w_ = wb.next()
            ldw(k, wt, w_, Wmv, "Wmv", 0, KC, g * 512, 512)
            for kb in range(2):
                p_ = ps.next()
                for kc in range(KC):
                    k.mm(p_[:], mT[:, kc, kb * 128:(kb + 1) * 128], w_[:, kc, :], kc == 0, kc == KC - 1,
                         [w_.tl, mT.tl], [p_.tl])
                k.cp(mvs[:, kb, g * 512:(g + 1) * 512], p_[:], [p_.tl], [mvs.tl], eng=("act" if kb % 2 else "dve"))
        k.st(MKT, mkt[:], mkt, w=[mkt_tl], q="act")
        k.st(MV.rearrange("kb p d -> p kb d"), mvs[:], mvs, w=[mv_tl], q="act")
        sc.barrier()
        sc.emit()
    if stop_after <= 1:
        return k

    with ExitStack() as es:
        cst = Buf(es, nc, "cst2", [128, 1408], F32)
        k.ld(cst[:], consts, cst)
        cst.tl.const = True
        C1 = (cst[:, 128:256], cst[:, 512:640])
        C2 = (cst[:, 256:384], cst[:, 640:768])
        C3 = (cst[:, 384:512], cst[:, 768:896])
        MSK = (cst[:, 896:1024], cst[:, 1024:1152])
        w2f = Buf(es, nc, "w2f", [33, 1024], F32)
        k.ld(w2f[:], w2aug, w2f)
        w2b = Buf(es, nc, "w2b", [33, 1024], BF16)
        k.cp(w2b[:], w2f[:], [w2f.tl], [w2b.tl])
        w2b.tl.const = True
        wgg = Buf(es, nc, "wgg", [128, KC, 32], BF16)
        ldw(k, wt, wgg, Wi, "Wi", 0, KC, 3072, 32)
        wgg.tl.const = True
        ggaug = Buf(es, nc, "ggaug", [33, TT], BF16)
        k.memset(ggaug[32:33, :], 1.0, [ggaug.tl])
        hT = mk(es, nc, "p2hT", [128, KC, TT], BF16, n=1)
        wb = mk(es, nc, "p2wb", [128, KC, 512], BF16, n=2)
        ps = mk(es, nc, "p2ps", [128, 512], F32, n=3, psum=True)
        ops_ = mk(es, nc, "p2o", [128, 1024], F32, n=1, psum=True)
        kvps = mk(es, nc, "p2kv", [128, 1024], F32, n=1, psum=True)
        qT = Buf(es, nc, "qT", [128, 4, TT], F32)
        kT = Buf(es, nc, "kT", [128, 4, TT], F32)
        kTM = Buf(es, nc, "kTM", [128, 4, 512], F32)
        sp = [[Buf(es, nc, f"sp{d_}{b}", [128, 512], F32) for b in range(4)] for d_ in range(2)]
        etmp = mk(es, nc, "etmp", [128, 512], F32, n=2)
        Eb = mk(es, nc, "Eb", [128, 512], F32, n=6)
        qi = [Buf(es, nc, f"qi{d_}", [128, 4, TT], BF16) for d_ in range(2)]
        ki = [Buf(es, nc, f"ki{d_}", [128, 4, TT], BF16) for d_ in range(2)]
        qb = [Buf(es, nc, f"qb{d_}", [128, 4, TT], BF16) for d_ in range(2)]
        kdec = [[Buf(es, nc, f"kdec{d_}{b}", [128, 512], BF16) for b in range(4)] for d_ in range(2)]
        vv = [Buf(es, nc, f"vv{b}", [128, 1024], BF16) for b in range(4)]
        decf = Buf(es, nc, "decf", [128, 4, 8], F32)
        decb = Buf(es, nc, "decb", [128, 4, 8], F32)
        sstage = mk(es, nc, "sstage", [128, 512], F32, n=2)
        ostage = mk(es, nc, "ostage", [128, 1024], F32, n=2)
        attsb = mk(es, nc, "attsb", [128, 128], BF16, n=4)
        Sf = Buf(es, nc, "Sf", [128, 1024], F32)
        Sfb = Buf(es, nc, "Sfb", [128, 1024], BF16)
        k.memset(Sf[:], 0.0, [Sf.tl])
        k.memset(Sfb[:], 0.0, [Sfb.tl])
        for t in range(NT):
            tok0 = t * TT
            h = hT.next()
            k.ld(h[:], H1T[:, :, tok0:tok0 + TT], h, r=[h1t_tl[t]])
            p_ = ps.next()
            for kc in range(KC):
                k.mm(p_[0:32, :], wgg[:, kc, :], h[:, kc, :], kc == 0, kc == KC - 1, [wgg.tl, h.tl], [p_.tl])
            k.cp(ggaug[0:32, :], p_[0:32, :], [p_.tl], [ggaug.tl], eng="act")
            for d_ in range(2):
                for b in range(4):
                    p_ = ps.next()
                    k.mm(p_[:], ggaug[0:33, b * 128:(b + 1) * 128], w2b[0:33, d_ * 512:(d_ + 1) * 512], True, True,
                         [ggaug.tl, w2b.tl], [p_.tl])
                    e_ = etmp.next()
                    k.act(e_[:], p_[:], AF.Exp, [p_.tl], [e_.tl], scale=-1.0)
                    k.act(sp[d_][b][:], e_[:], AF.Ln, [e_.tl], [sp[d_][b].tl], bias=1.0)
            w_ = wb.next()
            ldw(k, wt, w_, Wi, "Wi", 0, KC, 0, 512)
            for hh in range(4):
                p_ = ps.next()
                for kc in range(KC):
                    k.mm(p_[:], w_[:, kc, hh * 128:(hh + 1) * 128], h[:, kc, :], kc == 0, kc == KC - 1, [w_.tl, h.tl], [p_.tl])
                k.act(qT[:, hh, :], p_[:], AF.Copy, [p_.tl], [qT.tl], scale=float(128 ** -0.5))
            w_ = wb.next()
            ldw(k, wt, w_, Wi, "Wi", 0, KC, 512, 512)
            for hh in range(4):
                p_ = ps.next()
                for kc in range(KC):
                    k.mm(p_[:], w_[:, kc, hh * 128:(hh + 1) * 128], h[:, kc, :], kc == 0, kc == KC - 1, [w_.tl, h.tl], [p_.tl])
                k.cp(kT[:, hh, :], p_[:], [p_.tl], [kT.tl])
            for b in range(4):
                p_ = ps.next()
                for kc in range(KC):
                    k.mm(p_[:], h[:, kc, b * 128:(b + 1) * 128], w_[:, kc, :], kc == 0, kc == KC - 1, [w_.tl, h.tl], [p_.tl])
                k.cp(kTM[:, b, :], p_[:], [p_.tl], [kTM.tl], eng="act")
            for g in range(2):
                w_ = wb.next()
                ldw(k, wt, w_, Wi, "Wi", 0, KC, 1024 + g * 512, 512)
                for b in range(4):
                    p_ = ps.next()
                    for kc in range(KC):
                        k.mm(p_[:], h[:, kc, b * 128:(b + 1) * 128], w_[:, kc, :], kc == 0, kc == KC - 1, [w_.tl, h.tl], [p_.tl])
                    k.cp(vv[b][:, g * 512:(g + 1) * 512], p_[:], [p_.tl], [vv[b].tl], eng=("act" if b % 2 else "dve"))
            gr_w = []
            for g in range(2):
                w_ = wb.next()
                ldw(k, wt, w_, Wi, "Wi", 0, KC, 2048 + g * 512, 512)
                gr_w.append(w_)

            def gr_batch(g, b, h=h, tok0=tok0, t=t, gr_w=gr_w):
                w_ = gr_w[g]
                p_ = ps.next()
                for kc in range(KC):
                    k.mm(p_[:], h[:, kc, b * 128:(b + 1) * 128], w_[:, kc, :], kc == 0, kc == KC - 1, [w_.tl, h.tl], [p_.tl])
                s_ = sstage.next()
                k.act(s_[:], p_[:], AF.Silu, [p_.tl], [s_.tl])
                k.st(SILU[tok0 + b * 128: tok0 + (b + 1) * 128, g * 512:(g + 1) * 512], s_[:], s_, w=[silu_tl[t]])
            gr_todo = [(g, b) for g in range(2) for b in range(4)]
            for d_ in range(2):
                for hh in range(4):
                    p1 = ps.next()
                    for b in range(4):
                        k.mm(p1[:, b * 128:(b + 1) * 128], sp[d_][b][:, hh * 128:(hh + 1) * 128], C1[d_], True, True,
                             [sp[d_][b].tl], [p1.tl])
                    e1 = Eb.next()
                    k.act(e1[:], p1[:], AF.Exp, [p1.tl], [e1.tl])
                    e1i = Eb.next()
                    k.act(e1i[:], p1[:], AF.Exp, [p1.tl], [e1i.tl], scale=-1.0)
                    p2_ = ps.next()
                    for b in range(4):
                        k.mm(p2_[:, b * 128:(b + 1) * 128], sp[d_][b][:, hh * 128:(hh + 1) * 128], C2[d_], True, True,
                             [sp[d_][b].tl], [p2_.tl])
                    e2 = Eb.next()
                    k.act(e2[:], p2_[:], AF.Exp, [p2_.tl], [e2.tl])
                    k.tt(qi[d_][:, hh, :], qT[:, hh, :], e1[:], ALU.mult, [qT.tl, e1.tl], [qi[d_].tl])
                    k.tt(ki[d_][:, hh, :], kT[:, hh, :], e1i[:], ALU.mult, [kT.tl, e1i.tl], [ki[d_].tl])
                    k.tt(qb[d_][:, hh, :], qT[:, hh, :], e2[:], ALU.mult, [qT.tl, e2.tl], [qb[d_].tl])
                    if d_ == 0:
                        k.cp(decf[:, hh, :], e2[:, 63:512:64], [e2.tl], [decf.tl])
                    else:
                        k.cp(decb[:, hh, :], e2[:, 0:512:64], [e2.tl], [decb.tl])
                for b in range(4):
                    p3 = ps.next()
                    k.mm(p3[:], C3[d_], sp[d_][b][:], True, True, [sp[d_][b].tl], [p3.tl])
                    e3 = Eb.next()
                    k.act(e3[:], p3[:], AF.Exp, [p3.tl], [e3.tl])
                    k.tt(kdec[d_][b][:], kTM[:, b, :], e3[:], ALU.mult, [kTM.tl, e3.tl], [kdec[d_][b].tl])
            k.st(QBB[:, :, tok0:tok0 + TT], qb[1][:], qb[1], w=[qbb_tl[t]])
            k.st(DECB[:, :, t * 8:(t + 1) * 8], decb[:], decb, w=[decb_tl[t]])
            for b in range(4):
                k.st(KDB[tok0 + b * 128: tok0 + (b + 1) * 128, :], kdec[1][b][:], kdec[1][b], w=[kdb_tl[t]])
                k.st(VG[tok0 + b * 128: tok0 + (b + 1) * 128, :], vv[b][:], vv[b], w=[vg_tl[t]])
            for b in range(4):
                o_ = ops_.next()
                blk = slice(b * 128, (b + 1) * 128)
                for hh in range(4):
                    hs = slice(hh * 256, (hh + 1) * 256)
                    for d_ in range(2):
                        a_ = ps.next()
                        k.mm(a_[:, 0:128], ki[d_][:, hh, blk], qi[d_][:, hh, blk], True, True, [ki[d_].tl, qi[d_].tl], [a_.tl])
                        as_ = attsb.next()
                        k.tt(as_[:], a_[:, 0:128], MSK[d_], ALU.mult, [a_.tl], [as_.tl])
                        k.mm(o_[:, hs], as_[:], vv[b][:, hs], d_ == 0 and hh % 2 == 0, False, [as_.tl, vv[b].tl], [o_.tl])
                for c in range(2):
                    rows = slice(c * 64, (c + 1) * 64)
                    cs = slice(b * 128 + c * 64, b * 128 + (c + 1) * 64)
                    for hh in range(4):
                        hs = slice(hh * 256, (hh + 1) * 256)
                        k.mm(o_[rows, hs], qb[0][:, hh, cs], Sfb[:, hs], False, True, [qb[0].tl, Sfb.tl], [o_.tl])
                    kv_ = kvps.next()
                    for hh in range(4):
                        hs = slice(hh * 256, (hh + 1) * 256)
                        k.mm(kv_[:, hs], kdec[0][b][rows, hh * 128:(hh + 1) * 128], vv[b][rows, hs], True, True,
                             [kdec[0][b].tl, vv[b].tl], [kv_.tl])
                    for hh in range(4):
                        hs = slice(hh * 256, (hh + 1) * 256)
                        k.stt(Sf[:, hs], Sf[:, hs], decf[:, hh, b * 2 + c:b * 2 + c + 1], kv_[:, hs], ALU.mult, ALU.add,
                              [Sf.tl, decf.tl, kv_.tl], [Sf.tl])
                    k.cp(Sfb[:], Sf[:], [Sf.tl], [Sfb.tl], eng="act")
                    if gr_todo:
                        gr_batch(*gr_todo.pop(0))
                og = ostage.next()
                k.cp(og[:], o_[:], [o_.tl], [og.tl], eng="act")
                k.st(OP[tok0 + b * 128: tok0 + (b + 1) * 128, :], og[:], og, w=[op_tl[t]])
        sc.barrier()
        sc.emit()
    if stop_after <= 2:
        return k

    with ExitStack() as es:
        late = cast_chunks(Wo, w_out, "Wo", D, 512) + cast_chunks(Wmq, w_mq, "Wmq", D, 512) + cast_chunks(Wmo, w_mo, "Wmo", D, 512)
        cst = Buf(es, nc, "cst3", [128, 1408], F32)
        k.ld(cst[:], consts, cst)
        cst.tl.const = True
        ones32 = cst[:, 1152:1280]
        gc = Buf(es, nc, "gc3", [128, 72], F32)
        k.ld(gc[:], gcols, gc)
        gc.tl.const = True
        wmla = Buf(es, nc, "wmla", [128, KC, 1152], BF16)
        ldw(k, wt, wmla, Wi, "Wi", 0, KC, 3104, 1088)
        ldw(k, wt, wmla, Wkrr, "Wkrr", 0, KC, 0, 64, col_off=1088)
        wmla.tl.const = True
        WqS = Buf(es, nc, "WqS", [128, 4, 8, 192], BF16)
        WqrS = Buf(es, nc, "WqrS", [128, 4, 8, 64], BF16)
        WkvS = Buf(es, nc, "WkvS", [128, 4, 8, 256], BF16)
        for c in range(4):
            k.ld(WqS[:, c, :, :], Wq[c * 128:(c + 1) * 128, :].rearrange("p (h c) -> p h c", c=192), WqS, r=[wt["Wq"]])
            k.ld(WqrS[:, c, :, :], Wqr[c * 128:(c + 1) * 128, :].rearrange("p (h c) -> p h c", c=64), WqrS, r=[wt["Wqr"]])
            k.ld(WkvS[:, c, :, :], Wkv[c * 128:(c + 1) * 128, :].rearrange("p (h c) -> p h c", c=256), WkvS, r=[wt["Wkv"]])
        for b_ in (WqS, WqrS, WkvS):
            b_.tl.const = True
        hT = mk(es, nc, "p3hT", [128, KC, TT], BF16, n=2)
        ps = mk(es, nc, "p3ps", [128, 512], F32, n=6, psum=True)
        cqb = Buf(es, nc, "cqb", [128, 4, TT], BF16)
        ckvb = Buf(es, nc, "ckvb", [128, 4, TT], BF16)
        sq = Buf(es, nc, "sq", [128, 4, TT], F32)
        rqbc = Buf(es, nc, "rqbc", [128, TT], F32)
        rkbc = Buf(es, nc, "rkbc", [128, TT], F32)
        rtm = Buf(es, nc, "rtm", [128, 16], F32)
        cs_ = mk(es, nc, "cossin", [64, 2, TT], F32, n=2)
        t1 = mk(es, nc, "ropet1", [64, TT], F32, n=2)
        t2 = mk(es, nc, "ropet2", [64, TT], F32, n=2)
        kpe_st = mk(es, nc, "kpe_st", [64, TT], BF16, n=2)
        qn_st = mk(es, nc, "qn_st", [128, 8, TT], BF16, n=2)
        qr_st = mk(es, nc, "qr_st", [64, 8, TT], BF16, n=2)
        kn_st = mk(es, nc, "kn_st", [128, 8, TT], BF16, n=2)
        v_st = mk(es, nc, "v_st", [128, 8, 128], BF16, n=4)
        QSC = float(192 ** -0.5)

        def rms_fm(src_off, gcol_off, dst_bf, rbc, h):
            for c in range(4):
                p_ = ps.next()
                for kc in range(KC):
                    k.mm(p_[:], wmla[:, kc, src_off + c * 128: src_off + (c + 1) * 128], h[:, kc, :], kc == 0, kc == KC - 1,
                         [wmla.tl, h.tl], [p_.tl])
                k.act(dst_bf[:, c, :], p_[:], AF.Copy, [p_.tl], [dst_bf.tl], scale=gc[:, gcol_off + c:gcol_off + c + 1])
                k.act(sq[:, c, :], p_[:], AF.Square, [p_.tl], [sq.tl])
            pss = ps.next()
            for c in range(4):
                k.mm(pss[:], ones32, sq[:, c, :], c == 0, c == 3, [sq.tl], [pss.tl])
            k.ts(rbc[:], pss[:], 1.0 / 512, EPS, ALU.mult, ALU.add, [pss.tl], [rbc.tl])
            k.act(rbc[:], rbc[:], AF.Sqrt, [rbc.tl], [rbc.tl])
            k.recip(rbc[:], rbc[:], [rbc.tl], [rbc.tl])

        for t in range(NT):
            tok0 = t * TT
            h = hT.next()
            k.ld(h[:], H1T[:, :, tok0:tok0 + TT], h, r=[h1t_tl[t]])
            cs = cs_.next()
            k.ld(cs[:, 0, :], ropet[:, tok0:tok0 + TT], cs)
            k.ld(cs[:, 1, :], ropet[:, S + tok0:S + tok0 + TT], cs)
            for _ in range(2):
                if late:
                    late.pop(0)()
            rms_fm(0, 64, cqb, rqbc, h)
            rms_fm(512, 68, ckvb, rkbc, h)
            for b in range(4):
                p_ = ps.next()
                for c in range(4):
                    k.mm(p_[:, 0:2], sq[:, c, b * 128:(b + 1) * 128], ones32[:, 0:2], c == 0, c == 3, [sq.tl], [p_.tl])
                k.ts(rtm[:, 4 * b + 1:4 * b + 2], p_[:, 0:1], 1.0 / 512, EPS, ALU.mult, ALU.add, [p_.tl], [rtm.tl])
            k.act(rtm[:], rtm[:], AF.Sqrt, [rtm.tl], [rtm.tl])
            k.recip(rtm[:], rtm[:], [rtm.tl], [rtm.tl])
            pr = ps.next()
            for kc in range(KC):
                k.mm(pr[0:64, :], wmla[:, kc, 1024:1088], h[:, kc, :], kc == 0, kc == KC - 1, [wmla.tl, h.tl], [pr.tl])
            pq = ps.next()
            for kc in range(KC):
                k.mm(pq[0:64, :], wmla[:, kc, 1088:1152], h[:, kc, :], kc == 0, kc == KC - 1, [wmla.tl, h.tl], [pq.tl])
            a1 = t1.next()
            a2 = t2.next()
            k.tt(a1[:], pr[0:64, :], cs[:, 0, :], ALU.mult, [pr.tl, cs.tl], [a1.tl])
            k.tt(a2[:], pq[0:64, :], cs[:, 1, :], ALU.mult, [pq.tl, cs.tl], [a2.tl])
            kp = kpe_st.next()
            k.tt(kp[:], a1[:], a2[:], ALU.add, [a1.tl, a2.tl], [kp.tl])
            k.st(KPE[0:64, tok0:tok0 + TT], kp[:], kp, w=[kpe_tl[t]])
            qn = qn_st.next()
            qr = qr_st.next()
            for hh in range(8):
                pn = ps.next()
                for c in range(4):
                    k.mm(pn[:], WqS[:, c, hh, 0:128], cqb[:, c, :], c == 0, c == 3, [WqS.tl, cqb.tl], [pn.tl])
                k.stt(qn[:, hh, :], pn[:], QSC, rqbc[:], ALU.mult, ALU.mult, [pn.tl, rqbc.tl], [qn.tl])
                pr = ps.next()
                for c in range(4):
                    k.mm(pr[0:64, :], WqS[:, c, hh, 128:192], cqb[:, c, :], c == 0, c == 3, [WqS.tl, cqb.tl], [pr.tl])
                pq = ps.next()
                for c in range(4):
                    k.mm(pq[0:64, :], WqrS[:, c, hh, :], cqb[:, c, :], c == 0, c == 3, [WqrS.tl, cqb.tl], [pq.tl])
                a1 = t1.next()
                a2 = t2.next()
                k.tt(a1[:], pr[0:64, :], cs[:, 0, :], ALU.mult, [pr.tl, cs.tl], [a1.tl])
                k.tt(a2[:], pq[0:64, :], cs[:, 1, :], ALU.mult, [pq.tl, cs.tl], [a2.tl])
                k.tt(a1[:], a1[:], a2[:], ALU.add, [a1.tl, a2.tl], [a1.tl], eng="pool")
                k.stt(qr[:, hh, :], a1[:], QSC, rqbc[0:64, :], ALU.mult, ALU.mult, [a1.tl, rqbc.tl], [qr.tl])
            k.st(QTN[:, :, tok0:tok0 + TT].rearrange("h p t -> p h t"), qn[:], qn, w=[qtn_tl[t]])
            k.st(QTR[:, :, tok0:tok0 + TT].rearrange("h p t -> p h t"), qr[:], qr, w=[qtr_tl[t]])
            kn = kn_st.next()
            for hh in range(8):
                pk = ps.next()
                for c in range(4):
                    k.mm(pk[:], WkvS[:, c, hh, 0:128], ckvb[:, c, :], c == 0, c == 3, [WkvS.tl, ckvb.tl], [pk.tl])
                k.tt(kn[:, hh, :], pk[:], rkbc[:], ALU.mult, [pk.tl, rkbc.tl], [kn.tl])
            k.st(KTN[:, :, tok0:tok0 + TT].rearrange("h p t -> p h t"), kn[:], kn, w=[ktn_tl[t]])
            for b in range(4):
                vs = v_st.next()
                for half in range(2):
                    pv = ps.next()
                    for c in range(4):
                        k.mm(pv[:], ckvb[:, c, b * 128:(b + 1) * 128], WkvS[:, c, half * 4:(half + 1) * 4, 128:256],
                             c == 0, c == 3, [WkvS.tl, ckvb.tl], [pv.tl])
                    k.act(vs[:, half * 4:(half + 1) * 4, :], pv[:].rearrange("p (h d) -> p h d", d=128), AF.Copy, [pv.tl, rtm.tl], [vs.tl],
                          scale=rtm[:, 4 * b + 1:4 * b + 2])
                k.st(VS[:, :, 4 * t + b, :].rearrange("h p d -> p h d"), vs[:], vs, w=[vs_tl[t]])
        while late:
            late.pop(0)()
        sc.barrier()
        sc.emit()
    if stop_after <= 3:
        return k

    with ExitStack() as es:
        late = cast_chunks(Wup, w_up, "Wup", D, 128) + cast_chunks(Wdn, w_dn, "Wdn", DFF, 352)
        cst = Buf(es, nc, "cst5", [128, 1408], F32)
        k.ld(cst[:], consts, cst)
        onesb = Buf(es, nc, "onesb5", [128, 128], BF16)
        k.cp(onesb[:], cst[:, 1152:1280], [cst.tl], [onesb.tl])
        onesb.tl.const = True
        identb = Buf(es, nc, "identb4", [128, 128], BF16)
        k.cp(identb[:], cst[:, 0:128], [cst.tl], [identb.tl])
        identb.tl.const = True
        gout = Buf(es, nc, "gout", [128, 256], F32)
        k.ld(gout[:], gpost[:, 3 * D:3 * D + 256], gout)
        gout.tl.const = True
        kpe = Buf(es, nc, "kpe5", [64, S], BF16)
        k.ld(kpe[:], KPE[0:64, :], kpe, r=kpe_tl)
        kpe.tl.const = True
        ktn = mk(es, nc, "ktn5", [128, S], BF16, n=2)
        vsb = mk(es, nc, "vs5", [128, 32, 128], BF16, n=2)
        qnb = mk(es, nc, "qn5", [128, TT], BF16, n=4)
        qrb = mk(es, nc, "qr5", [64, TT], BF16, n=4)
        ps = mk(es, nc, "p5ps", [128, 512], F32, n=3, psum=True)
        oTp = mk(es, nc, "p5oT", [128, 512], F32, n=1, psum=True)
        dnp = mk(es, nc, "p5dn", [128, 512], F32, n=1, psum=True)
        ptb = mk(es, nc, "p5pt", [128, 512], BF16, n=6)
        rden = mk(es, nc, "p5rd", [128, 512], F32, n=2)
        omb = mk(es, nc, "p5om", [128, 512], BF16, n=3)
        qbb = mk(es, nc, "p4qbb", [128, 4, TT], BF16, n=2)
        kdb = mk(es, nc, "p4kdb", [128, 512], BF16, n=8)
        vvb = mk(es, nc, "p4vv", [128, 1024], BF16, n=8)
        opb = mk(es, nc, "p4op", [128, 1024], F32, n=6)
        silb = mk(es, nc, "p4sil", [128, 1024], F32, n=6)
        dcb = mk(es, nc, "p4dec", [128, 4, 8], F32, n=2)
        oi_ps = mk(es, nc, "p4oi", [128, 512], F32, n=1, psum=True)
        kv_ps = mk(es, nc, "p4kv", [128, 512], F32, n=1, psum=True)
        pT = mk(es, nc, "p4pT", [128, 1024], BF16, n=1, psum=True)
        Sb = Buf(es, nc, "Sb", [128, 1024], F32)
        Sbb = Buf(es, nc, "Sbb", [128, 1024], BF16)
        k.memset(Sb[:], 0.0, [Sb.tl])
        k.memset(Sbb[:], 0.0, [Sbb.tl])
        junk = Buf(es, nc, "p4junk", [128, 256], BF16)
        ssb = mk(es, nc, "p4ss", [128, 4], F32, n=2)
        tmpn = mk(es, nc, "p4tmpn", [128, 1024], F32, n=2)
        ogb = mk(es, nc, "p4og", [128, 1024], BF16, n=2)
        ogT = mk(es, nc, "p4ogT", [128, 8, TT], BF16, n=2)

        def gla_bwd():
            for t in reversed(range(NT)):
                tok0 = t * TT
                q_ = qbb.next()
                k.ld(q_[:], QBB[:, :, tok0:tok0 + TT], q_, r=[qbb_tl[t]])
                dc = dcb.next()
                k.ld(dc[:], DECB[:, :, t * 8:(t + 1) * 8], dc, r=[decb_tl[t]])
                kd, vb, ob, sb = {}, {}, {}, {}
                for b in reversed(range(4)):
                    rs = slice(tok0 + b * 128, tok0 + (b + 1) * 128)
                    kd[b] = kdb.next()
                    k.ld(kd[b][:], KDB[rs, :], kd[b], r=[kdb_tl[t]])
                    vb[b] = vvb.next()
                    k.ld(vb[b][:], VG[rs, :], vb[b], r=[vg_tl[t]])
                    ob[b] = opb.next()
                    k.ld(ob[b][:], OP[rs, :], ob[b], r=[op_tl[t]])
                    sb[b] = silb.next()
                    k.ld(sb[b][:], SILU[rs, :], sb[b], r=[silu_tl[t]])
                yield
                oT_ = ogT.next()
                for b in reversed(range(4)):
                    o_ = ob[b]
                    for hp in range(2):
                        hps = slice(hp * 512, (hp + 1) * 512)
                        oi = oi_ps.next()
                        for c in (1, 0):
                            rows = slice(c * 64, (c + 1) * 64)
                            cs = slice(b * 128 + c * 64, b * 128 + (c + 1) * 64)
                            for hl in range(2):
                                hh = hp * 2 + hl
                                k.mm(oi[rows, hl * 256:(hl + 1) * 256], q_[:, hh, cs], Sbb[:, hh * 256:(hh + 1) * 256], True, True,
                                     [q_.tl, Sbb.tl], [oi.tl])
                            kv_ = kv_ps.next()
                            for hl in range(2):
                                hh = hp * 2 + hl
                                k.mm(kv_[:, hl * 256:(hl + 1) * 256], kd[b][rows, hh * 128:(hh + 1) * 128],
                                     vb[b][rows, hh * 256:(hh + 1) * 256], True, True, [kd[b].tl, vb[b].tl], [kv_.tl])
                            yield
                            for hl in range(2):
                                hh = hp * 2 + hl
                                hs = slice(hh * 256, (hh + 1) * 256)
                                k.stt(Sb[:, hs], Sb[:, hs], dc[:, hh, b * 2 + c:b * 2 + c + 1], kv_[:, hl * 256:(hl + 1) * 256],
                                      ALU.mult, ALU.add, [Sb.tl, dc.tl, kv_.tl], [Sb.tl])
                            yield
                            k.cp(Sbb[:, hps], Sb[:, hps], [Sb.tl], [Sbb.tl], eng="act")
                            yield
                        k.tt(o_[:, hps], oi[:], o_[:, hps], ALU.add, [oi.tl, o_.tl], [o_.tl])
                        yield
                    ss = ssb.next()
                    for hh in range(4):
                        hs = slice(hh * 256, (hh + 1) * 256)
                        k.act(junk[:], o_[:, hs], AF.Square, [o_.tl], [junk.tl, ss.tl], accum=ss[:, hh:hh + 1])
                    yield
                    k.ts(ss[:], ss[:], 1.0 / 256, EPS, ALU.mult, ALU.add, [ss.tl], [ss.tl])
                    yield
                    k.act(ss[:], ss[:], AF.Sqrt, [ss.tl], [ss.tl])
                    yield
                    k.recip(ss[:], ss[:], [ss.tl], [ss.tl])
                    tn = tmpn.next()
                    for hh in range(4):
                        hs = slice(hh * 256, (hh + 1) * 256)
                        k.stt(tn[:, hs], o_[:, hs], ss[:, hh:hh + 1], gout[:], ALU.mult, ALU.mult, [o_.tl, ss.tl], [tn.tl])
                    yield
                    og = ogb.next()
                    k.tt(og[:], tn[:], sb[b][:], ALU.mult, [tn.tl, sb[b].tl], [og.tl], eng="pool")
                    yield
                    yield
                    p_ = pT.next()
                    for c8 in range(8):
                        k.tr(p_[:, c8 * 128:(c8 + 1) * 128], og[:, c8 * 128:(c8 + 1) * 128], identb[:], [og.tl], [p_.tl])
                    yield
                    k.cp(oT_[:, :, b * 128:(b + 1) * 128], p_[:].rearrange("p (c t) -> p c t", t=128), [p_.tl], [oT_.tl], eng="act")
                    yield
                k.st(OGT[:, :, tok0:tok0 + TT], oT_[:], oT_, w=[ogt_tl[t]])

        gen = gla_bwd()
        gen_done = [False]

        def step_gen():
            if not gen_done[0]:
                try:
                    next(gen)
                except StopIteration:
                    gen_done[0] = True

        it_no = 0
        for hh in range(8):
            kt_ = ktn.next()
            k.ld(kt_[:], KTN[hh, :, :], kt_, r=ktn_tl)
            vs_ = vsb.next()
            k.ld(vs_[:], VS[hh, :, :, :], vs_, r=vs_tl)
            for t in range(NT):
                tok0 = t * TT
                if late and (hh * NT + t) % 2 == 0:
                    late.pop(0)()
                qn = qnb.next()
                k.ld(qn[:], QTN[hh, :, tok0:tok0 + TT], qn, r=[qtn_tl[t]])
                qr = qrb.next()
                k.ld(qr[:], QTR[hh, :, tok0:tok0 + TT], qr, r=[qtr_tl[t]])
                oT = oTp.next()
                dn = dnp.next()
                pend = []

                def score(kb):
                    ks = slice(kb * 128, (kb + 1) * 128)
                    sT = ps.next()
                    k.mm(sT[:], kt_[:, ks], qn[:], True, False, [kt_.tl, qn.tl], [sT.tl])
                    k.mm(sT[:], kpe[0:64, ks], qr[0:64, :], False, True, [kpe.tl, qr.tl], [sT.tl])
                    pt = ptb.next()
                    k.act(pt[:], sT[:], AF.Exp, [sT.tl], [pt.tl])
                    pend.append((kb, pt))

                def pv():
                    kb, pt = pend.pop(0)
                    k.mm(oT[:], vs_[:, kb, :], pt[:], kb == 0, kb == 31, [vs_.tl, pt.tl], [oT.tl])
                    k.mm(dn[:], onesb[:], pt[:], kb == 0, kb == 31, [pt.tl], [dn.tl])

                for kb in range(32):
                    score(kb)
                    if len(pend) > 2:
                        pv()
                    it_no += 1
                    if it_no % 2 == 0:
                        step_gen()
                while pend:
                    pv()
                rd = rden.next()
                k.recip(rd[:], dn[:], [dn.tl], [rd.tl])
                om = omb.next()
                k.tt(om[:], oT[:], rd[:], ALU.mult, [oT.tl, rd.tl], [om.tl])
                k.st(OMT[:, hh, tok0:tok0 + TT], om[:], om, w=[omt_tl[t]])
        while not gen_done[0]:
            step_gen()
        while late:
            late.pop(0)()
        sc.barrier()
        sc.emit()
    if stop_after <= 5:
        return k

    TC = 512
    NB = TC // 128
    with ExitStack() as es:
        cst = Buf(es, nc, "cst6", [128, 256], F32)
        k.ld(cst[:, 0:128], consts[:, 0:128], cst)
        k.ld(cst[:, 128:256], consts[:, 1152:1280], cst)
        identb = Buf(es, nc, "identb6", [128, 128], BF16)
        k.cp(identb[:], cst[:, 0:128], [cst.tl], [identb.tl])
        identb.tl.const = True
        onesb = Buf(es, nc, "onesb6", [128, 128], BF16)
        k.cp(onesb[:], cst[:, 128:256], [cst.tl], [onesb.tl])
        onesb.tl.const = True
        gc = Buf(es, nc, "gc6", [128, 72], F32)
        k.ld(gc[:], gcols, gc)
        gc.tl.const = True
        gp = Buf(es, nc, "gp6", [128, 2 * D], F32)
        k.ld(gp[:], gpost[:, 0:2 * D], gp)
        gp.tl.const = True
        mkt = Buf(es, nc, "mkt6", [128, KC, 256], BF16)
        k.ld(mkt[:], MKT, mkt, r=[mkt_tl])
        mkt.tl.const = True
        mvs = Buf(es, nc, "mvs6", [128, 2, D], BF16)
        k.ld(mvs[:], MV.rearrange("kb p d -> p kb d"), mvs, r=[mv_tl])
        mvs.tl.const = True
        actAr = mk(es, nc, "actA", [128, KC, TC], BF16, n=2)
        ssp = [Buf(es, nc, f"ssp{b}", [128, 4], F32) for b in range(NB)]
        junk2 = Buf(es, nc, "p6junk2", [128, 512], BF16)
        actB = Buf(es, nc, "actB", [128, KC, TC], BF16)
        wb = mk(es, nc, "p6wb", [128, KC, 512], BF16, n=2)
        ysb = [Buf(es, nc, f"ysb{b}", [128, D], F32) for b in range(NB)]
        xres = mk(es, nc, "xres", [128, D], F32, n=4)
        xs = mk(es, nc, "p6xs", [128, D], BF16, n=NB)
        stt_ = mk(es, nc, "p6st", [128, 4], F32, n=NB)
        pT = mk(es, nc, "p6pT", [128, 512], BF16, n=2, psum=True)
        ps = mk(es, nc, "p6ps", [128, 512], F32, n=6, psum=True)
        ptb = mk(es, nc, "p6pt", [128, TC], BF16, n=4)
        rdb = mk(es, nc, "p6rd", [128, TC], F32, n=2)
        MSC = float(512 ** -0.5)

        def proj_tm(W, name, src):
            for g in range(4):
                w_ = wb.next()
                ldw(k, wt, w_, W, name, 0, KC, g * 512, 512)
                for b in range(NB):
                    p_ = ps.next()
                    for kc in range(KC):
                        k.mm(p_[:], src[:, kc, b * 128:(b + 1) * 128], w_[:, kc, :], kc == 0, kc == KC - 1, [w_.tl, src.tl], [p_.tl])
                    k.act(ysb[b][:, g * 512:(g + 1) * 512], p_[:], AF.Copy, [p_.tl], [ysb[b].tl])
                    k.act(junk2[:], p_[:], AF.Square, [p_.tl], [junk2.tl, ssp[b].tl], accum=ssp[b][:, g:g + 1])

        def post_res(xr, goff):
            for b in range(NB):
                st = stt_.next()
                k.sc.op("dve", lambda e, o_=st[:, 0:1], i_=ssp[b][:, 0:4]: e.reduce_sum(out=o_, in_=i_, axis=AX.X), [ssp[b].tl], [st.tl])
                k.rstd_from_ss(st[:, 2:3], st[:, 0:1], st[:, 1:2], D, [st.tl], [st.tl])
                k.stt(ysb[b][:], ysb[b][:], st[:, 2:3], gp[:, goff:goff + D], ALU.mult, ALU.mult, [ysb[b].tl, st.tl], [ysb[b].tl])
                k.tt(xr[b][:], xr[b][:], ysb[b][:], ALU.add, [xr[b].tl, ysb[b].tl], [xr[b].tl], eng=("pool" if b % 2 else "dve"))

        for t in range(S // TC):
            tok0 = t * TC
            t5 = tok0 // TT
            actA = actAr.next()
            k.ld(actA[:, 0:8, :], OGT[:, :, tok0:tok0 + TC], actA, r=[ogt_tl[t5]])
            k.ld(actA[:, 8:16, :], OMT[:, :, tok0:tok0 + TC], actA, r=[omt_tl[t5]])
            xr = []
            for b in range(NB):
                xb_ = xres.next()
                k.ld(xb_[:], x[tok0 + b * 128: tok0 + (b + 1) * 128, :], xb_)
                xr.append(xb_)
            proj_tm(Wo, "Wo", actA)
            post_res(xr, 0)
            norm_T(k, xr, xs.tiles, stt_.tiles, gc, 16, identb, pT, actA)
            for g in range(4):
                w_ = wb.next()
                ldw(k, wt, w_, Wmq, "Wmq", 0, KC, g * 512, 512)
                for j in range(4):
                    p_ = ps.next()
                    for kc in range(KC):
                        k.mm(p_[:, 0:TC], w_[:, kc, j * 128:(j + 1) * 128], actA[:, kc, :], kc == 0, kc == KC - 1, [w_.tl, actA.tl], [p_.tl])
                    k.act(actB[:, g * 4 + j, :], p_[:, 0:TC], AF.Copy, [p_.tl], [actB.tl], scale=MSC)
            for hh in range(4):
                pts = []
                for kb in range(2):
                    sT = ps.next()
                    for dc in range(4):
                        k.mm(sT[:, 0:TC], mkt[:, hh * 4 + dc, kb * 128:(kb + 1) * 128], actB[:, hh * 4 + dc, :], dc == 0, dc == 3,
                             [actB.tl], [sT.tl])
                    pt = ptb.next()
                    k.act(pt[:], sT[:, 0:TC], AF.Exp, [sT.tl], [pt.tl])
                    pts.append(pt)
                dn = ps.next()
                for kb in range(2):
                    k.mm(dn[:, 0:TC], onesb[:], pts[kb][:], kb == 0, kb == 1, [pts[kb].tl], [dn.tl])
                rd = rdb.next()
                k.recip(rd[:], dn[:, 0:TC], [dn.tl], [rd.tl])
                for dvc in range(4):
                    po = ps.next()
                    for kb in range(2):
                        k.mm(po[:, 0:TC], mvs[:, kb, hh * 512 + dvc * 128: hh * 512 + (dvc + 1) * 128], pts[kb][:], kb == 0, kb == 1,
                             [pts[kb].tl], [po.tl])
                    k.tt(actA[:, hh * 4 + dvc, :], po[:, 0:TC], rd[:], ALU.mult, [po.tl, rd.tl], [actA.tl])
            proj_tm(Wmo, "Wmo", actA)
            post_res(xr, D)
            for b in range(NB):
                k.st(X2[tok0 + b * 128: tok0 + (b + 1) * 128, :], xr[b][:], xr[b], w=[x2_tl[t5]])
            norm_T(k, xr, xs.tiles, stt_.tiles, gc, 32, identb, pT, actB)
            k.st(H3T[:, :, tok0:tok0 + TC], actB[:], actB, w=[h3t_tl[t5]])
        sc.barrier()
        sc.emit()
    if stop_after <= 6:
        return k

    with ExitStack() as es:
        cvp = Buf(es, nc, "cvp", [128, 4, 88], F32)
        k.ld(cvp[:], convp.rearrange("p (a j) -> p a j", j=88), cvp)
        cvp.tl.const = True
        carry = [Buf(es, nc, f"carry{i}", [128, 2], F32) for i in range(88)]
        for i in range(88):
            k.memset(carry[i][:], 0.0, [carry[i].tl], eng="pool")
        hT = mk(es, nc, "p7hT", [128, KC, TT], BF16, n=2)
        wb = mk(es, nc, "p7wb", [128, KC, 1024], BF16, n=2)
        ps = mk(es, nc, "p7ps", [128, 512], F32, n=6, psum=True)
        ub = mk(es, nc, "p7ub", [128, 514], F32, n=6)
        cb = mk(es, nc, "p7cb", [128, 512], F32, n=6)
        gb = mk(es, nc, "p7gb", [128, 512], F32, n=3)
        atb = mk(es, nc, "p7at", [128, 4, 512], BF16, n=2)

        def conv(u, idx, out, eng0):
            w0 = cvp[:, 0, idx:idx + 1]
            w1 = cvp[:, 1, idx:idx + 1]
            w2 = cvp[:, 2, idx:idx + 1]
            bb = cvp[:, 3, idx:idx + 1]
            k.ts(out[:], u[:, 1:513], w1, bb, ALU.mult, ALU.add, [u.tl], [out.tl], eng=eng0)
            k.stt(out[:], u[:, 0:512], w0, out[:], ALU.mult, ALU.add, [u.tl, out.tl], [out.tl])
            k.stt(out[:], u[:, 2:514], w2, out[:], ALU.mult, ALU.add, [u.tl, out.tl], [out.tl])

        for t in range(NT):
            tok0 = t * TT
            h = hT.next()
            k.ld(h[:], H3T[:, :, tok0:tok0 + TT], h, r=[h3t_tl[t]])
            for j4 in range(11):
                w_ = wb.next()
                ldw(k, wt, w_, Wup, "Wup", 0, KC, j4 * 512, 512)
                ldw(k, wt, w_, Wup, "Wup", 0, KC, DFF + j4 * 512, 512, col_off=512)
                at = atb.next()
                for jj in range(4):
                    j = j4 * 4 + jj
                    us = []
                    for part in range(2):
                        p_ = ps.next()
                        for kc in range(KC):
                            k.mm(p_[:], w_[:, kc, part * 512 + jj * 128: part * 512 + (jj + 1) * 128], h[:, kc, :], kc == 0, kc == KC - 1,
                                 [w_.tl, h.tl], [p_.tl])
                        u = ub.next()
                        cy = carry[part * 44 + j]
                        k.cp(u[:, 0:2], cy[:], [cy.tl], [u.tl], eng="pool")
                        k.cp(u[:, 2:514], p_[:], [p_.tl], [u.tl], eng="act")
                        k.cp(cy[:], u[:, 512:514], [u.tl], [cy.tl], eng="pool")
                        us.append(u)
                    cg = cb.next()
                    conv(us[0], j, cg, "dve")
                    cv = cb.next()
                    conv(us[1], 44 + j, cv, "pool")
                    g_ = gb.next()
                    k.act(g_[:], cg[:], AF.Gelu_apprx_tanh, [cg.tl], [g_.tl])
                    k.tt(at[:, jj, :], g_[:], cv[:], ALU.mult, [g_.tl, cv.tl], [at.tl], eng="pool")
                if t == 0:
                    k.st(AT[:, j4 * 4:(j4 + 1) * 4, 0:511], at[:, :, 1:512], at, w=[at_tl[0]])
                else:
                    k.st(AT[:, j4 * 4:(j4 + 1) * 4, tok0 - 1:tok0 + 511], at[:, :, :], at, w=[at_tl[t], at_tl[t - 1]])
        cl = Buf(es, nc, "p7cl", [128, 88], F32)
        cl2 = Buf(es, nc, "p7cl2", [128, 88], F32)
        for i in range(88):
            cy = carry[i]
            k.stt(cl[:, i:i + 1], cy[:, 1:2], cvp[:, 1, i:i + 1], cvp[:, 3, i:i + 1], ALU.mult, ALU.add, [cy.tl], [cl.tl])
            k.stt(cl2[:, i:i + 1], cy[:, 0:1], cvp[:, 0, i:i + 1], cl[:, i:i + 1], ALU.mult, ALU.add, [cy.tl, cl.tl], [cl2.tl])
        gl = Buf(es, nc, "p7gl", [128, 44], F32)
        k.act(gl[:], cl2[:, 0:44], AF.Gelu_apprx_tanh, [cl2.tl], [gl.tl])
        al = Buf(es, nc, "p7al", [128, 44], BF16)
        k.tt(al[:], gl[:], cl2[:, 44:88], ALU.mult, [gl.tl, cl2.tl], [al.tl])
        for q4 in range(4):
            k.st(AT[:, q4 * 11:(q4 + 1) * 11, 4095:4096], al[:, q4 * 11:(q4 + 1) * 11].rearrange("p (j o) -> p j o", o=1), al, w=[at_tl[7]],
                 allow_slow_non_contiguous=True)
        sc.barrier()
        sc.emit()
    if stop_after <= 7:
        return k

    with ExitStack() as es:
        gp = Buf(es, nc, "gp8", [128, D], F32)
        k.ld(gp[:], gpost[:, 2 * D:3 * D], gp)
        gp.tl.const = True
        atl = mk(es, nc, "p8at", [128, 44, TT], BF16, n=2)
        xres = mk(es, nc, "p8x", [128, D], F32, n=4)
        wdb = mk(es, nc, "p8wd", [128, 1024], BF16, n=8)
        acc = mk(es, nc, "p8acc", [128, 1024], F32, n=4, psum=True)
        junk = mk(es, nc, "p8junk", [128, D], BF16, n=1)
        ysb = mk(es, nc, "p8y", [128, D], F32, n=4)
        stt_ = mk(es, nc, "p8st", [128, 4], F32, n=4)
        def load_at(i):
            a2 = atl.next()
            deps = [at_tl[i]] + ([at_tl[i + 1]] if i + 1 < NT else [])
            k.ld(a2[:], AT[:, :, i * TT:(i + 1) * TT], a2, r=deps)
            return a2

        for it in range(NT):
            tok0 = it * TT
            ys = []
            if it == 0:
                a_next = load_at(0)
            a_ = a_next
            xr = []
            for b in range(4):
                ys.append(ysb.next())
            for half in range(2):
                ac = [acc.next() for _ in range(4)]
                for j in range(44):
                    if j == 8 and half == 0 and it + 1 < NT:
                        a_next = load_at(it + 1)
                    if j == 8 and half == 1:
                        for b in range(4):
                            x_ = xres.next()
                            k.ld(x_[:], X2[tok0 + b * 128: tok0 + (b + 1) * 128, :], x_, r=[x2_tl[it]])
                            xr.append(x_)
                    wd = wdb.next()
                    k.ld(wd[:], Wdn[j * 128:(j + 1) * 128, half * 1024:(half + 1) * 1024], wd, r=[wt["Wdn"]])
                    for b in range(4):
                        for g in range(2):
                            k.mm(ac[b][:, g * 512:(g + 1) * 512], a_[:, j, b * 128:(b + 1) * 128], wd[:, g * 512:(g + 1) * 512], j == 0, j == 43,
                                 [a_.tl, wd.tl], [ac[b].tl])
                for b in range(4):
                    k.cp(ys[b][:, half * 1024:(half + 1) * 1024], ac[b][:], [ac[b].tl], [ys[b].tl], eng=("act" if b % 2 == 0 else "dve"))
            for b in range(4):
                rs = slice(tok0 + b * 128, tok0 + (b + 1) * 128)
                x_ = xr[b]
                st = stt_.next()
                jk = junk.next()
                k.act(jk[:], ys[b][:], AF.Square, [ys[b].tl], [jk.tl, st.tl], accum=st[:, 0:1])
                k.rstd_from_ss(st[:, 2:3], st[:, 0:1], st[:, 1:2], D, [st.tl], [st.tl])
                k.stt(ys[b][:], ys[b][:], st[:, 2:3], gp[:], ALU.mult, ALU.mult, [ys[b].tl, st.tl], [ys[b].tl])
                k.tt(ys[b][:], ys[b][:], x_[:], ALU.add, [ys[b].tl, x_.tl], [ys[b].tl], eng="pool")
                k.st(y_out[rs, :], ys[b][:], ys[b], w=[yout_tl])
        sc.barrier()
        sc.emit()
    return k


def ldw(k, wt, buf, W, name, r0, nkc, c0, ncols, col_off=0):
    src = W[r0:r0 + nkc * 128, c0:c0 + ncols].rearrange("(kc p) c -> p kc c", p=128)
    return k.ld(buf[:, 0:nkc, col_off:col_off + ncols], src, buf, r=[wt[name]])


def norm_T(k, blocks, xs_list, st_list, gc, gc_off, identb, pT, dstT):
    nb = len(blocks)
    for b, xbuf in enumerate(blocks):
        xs = xs_list[b]
        st = st_list[b]
        k.act(xs[:], xbuf[:], AF.Square, [xbuf.tl], [xs.tl, st.tl], accum=st[:, 0:1])
        k.rstd_from_ss(st[:, 2:3], st[:, 0:1], st[:, 1:2], D, [st.tl], [st.tl])
        k.ts(xs[:], xbuf[:], st[:, 2:3], None, ALU.mult, None, [xbuf.tl, st.tl], [xs.tl])
    for kc in range(KC):
        p_ = pT.next()
        for b in range(nb):
            k.tr(p_[:, b * 128:(b + 1) * 128], xs_list[b][:, kc * 128:(kc + 1) * 128], identb[:],
                 [xs_list[b].tl, identb.tl], [p_.tl])
        if kc % 2 == 0:
            k.act(dstT[:, kc, 0:nb * 128], p_[:, 0:nb * 128], AF.Copy, [p_.tl], [dstT.tl],
                  scale=gc[:, gc_off + kc:gc_off + kc + 1])
        else:
            k.ts(dstT[:, kc, 0:nb * 128], p_[:, 0:nb * 128], gc[:, gc_off + kc:gc_off + kc + 1], None,
                 ALU.mult, None, [p_.tl, gc.tl], [dstT.tl])


def _consts():
    a = np.arange(128)
    j = a[:, None]
    i = a[None, :]
    same = (j // 64) == (i // 64)
    c = np.float32(-1.0 / 16.0)
    Lf = lambda ii, jj: (((ii // 64) == (jj // 64)) & (jj <= ii)).astype(np.float32)
    Lb = lambda ii, jj: (((ii // 64) == (jj // 64)) & (jj >= ii)).astype(np.float32)
    reff = (i // 64) * 64 + 32
    refb = (i // 64) * 64 + 31
    out = np.zeros((128, 1408), np.float32)
    out[:, 0:128] = np.eye(128, dtype=np.float32)
    out[:, 128:256] = c * (Lf(i, j) - Lf(reff, j))
    out[:, 256:384] = c * Lf(i, j)
    out[:, 384:512] = c * (same & (j > i)).astype(np.float32)
    out[:, 512:640] = c * (Lb(i, j) - Lb(refb, j))
    out[:, 640:768] = c * Lb(i, j)
    out[:, 768:896] = c * (same & (j < i)).astype(np.float32)
    out[:, 896:1024] = (same & (j <= i)).astype(np.float32)
    out[:, 1024:1152] = (same & (j > i)).astype(np.float32)
    out[:, 1152:1280] = 1.0
    return out


def _col(v, n):
    return np.ascontiguousarray(np.asarray(v, np.float32).reshape(n, 128).T)


def make_in_maps(inp):
    f = lambda n: np.ascontiguousarray(np.asarray(inp[n], np.float32)[0])
    shared = {
        "w_in": f("w_in"), "w_q_up": f("mla_w_q_up"), "w_kv_up": f("mla_w_kv_up"), "w_out": f("w_out"),
        "w_mq": f("w_mem_q"), "w_mk": f("w_mem_k"), "w_mv": f("w_mem_v"), "w_mo": f("w_mem_o"),
        "w_up": f("w_ffn_up"), "w_dn": f("w_ffn_down"),
    }
    shared["consts"] = _consts()
    shared["gcols"] = np.ascontiguousarray(np.concatenate([
        _col(f("norm_mix_pre"), 16), _col(f("norm_mem_pre"), 16), _col(f("norm_ffn_pre"), 16),
        _col(f("mem_kv_norm"), 16), _col(f("mla_q_norm"), 4), _col(f("mla_kv_norm"), 4)], axis=1))
    gp = np.concatenate([f("norm_mix_post"), f("norm_mem_post"), f("norm_ffn_post"), f("gla_out_norm")])
    shared["gpost"] = np.ascontiguousarray(np.broadcast_to(gp[None, :], (128, gp.shape[0])))
    cw = f("ffn_conv_w")
    shared["convp"] = np.ascontiguousarray(np.concatenate(
        [_col(cw[0], 88), _col(cw[1], 88), _col(cw[2], 88), _col(f("ffn_conv_b"), 88)], axis=1))
    w2 = np.zeros((33, 1024), np.float32)
    w2[0:16, 0:512] = f("gla_gate_w2_fwd")
    w2[32, 0:512] = f("gla_gate_b_fwd")
    w2[16:32, 512:1024] = f("gla_gate_w2_bwd")
    w2[32, 512:1024] = f("gla_gate_b_bwd")
    shared["w2aug"] = w2
    half = 32
    freqs = (np.float32(10000.0) ** (-np.arange(half, dtype=np.float32) / np.float32(half))).astype(np.float32)
    ang = (np.arange(S, dtype=np.float32)[None, :] * freqs[:, None]).astype(np.float32)
    ang = np.concatenate([ang, ang], axis=0)
    shared["ropet"] = np.ascontiguousarray(np.concatenate([np.cos(ang), np.sin(ang)], axis=1).astype(np.float32))
    xs = [np.asarray(inp["x_prompt"], np.float32)[b] for b in range(2)] + \
         [np.asarray(inp["x_sample"], np.float32)[b] for b in range(4)]
    ms = [np.asarray(inp["mem_prompt"], np.float32)[b] for b in range(2)] + \
         [np.asarray(inp["mem_sample"], np.float32)[b] for b in range(4)]
    maps = []
    for c in range(8):
        s = c if c < 6 else c - 6
        m = dict(shared)
        m["x"] = np.ascontiguousarray(xs[s])
        m["mem"] = np.ascontiguousarray(ms[s])
        maps.append(m)
    return maps


def kernel(**inputs):
    nc = bass.Bass("TRN2", target_bir_lowering=False)
    build(nc)
    maps = make_in_maps(inputs)
    res = run_bass_kernel_spmd(nc, maps, core_ids=list(range(8)))
    ys = [np.asarray(res.results[c]["y"], np.float32) for c in range(6)]
    return (np.stack(ys[0:2], axis=0), np.stack(ys[2:6], axis=0))
```

```python
from contextlib import ExitStack
import numpy as np
import concourse.bass as bass
import concourse.mybir as mybir
from concourse.bass_utils import run_bass_kernel_spmd

F32 = mybir.dt.float32
BF16 = mybir.dt.bfloat16
AF = mybir.ActivationFunctionType
ALU = mybir.AluOpType
AX = mybir.AxisListType

S = 4096
D = 2048
KC = 16
NT = 8
TT = 512
IN_W = 4192
DFF = 5632
EPS = 1e-6
SEQ_CORES = [0, 1, 2, 4, 5, 6]
SPARE_CORES = (3, 7)


class Tl:
    __slots__ = ("name", "lw", "rd", "dsem", "dcnt", "const", "persist")

    def __init__(self, name, const=False):
        self.persist = False
        self.name = name
        self.lw = None
        self.rd = {}
        self.dsem = None
        self.dcnt = 0
        self.const = const


class Op:
    __slots__ = ("eng", "fn", "deps", "eidx", "is_dma", "done", "signal", "sigidx", "waits")

    def __init__(self, eng, fn):
        self.eng = eng
        self.fn = fn
        self.deps = []
        self.is_dma = False
        self.done = None
        self.signal = False
        self.sigidx = 0
        self.waits = []


ENGS = ("pe", "act", "dve", "pool", "sp")


class Sched:
    def __init__(self, nc):
        self.nc = nc
        self.ops = {e: [] for e in ENGS}
        self.esem = {e: nc.alloc_semaphore(name=f"es_{e}") for e in ("pe", "act", "dve", "pool")}
        self.ecount = {e: 0 for e in ENGS}
        self.sigcount = {e: 0 for e in ENGS}
        self.waited = {e: {} for e in ENGS}
        self.waited_dma = {e: {} for e in ENGS}
        self.dma_sems = []
        self.last_op = {e: None for e in ENGS}
        self.nsem = 0
        self.free_sems = []

    def _track(self, o, r, w):
        deps = {}
        for t in r:
            if t.lw is not None:
                deps[id(t.lw)] = t.lw
        for t in w:
            if t.lw is not None:
                deps[id(t.lw)] = t.lw
            for x in t.rd.values():
                deps[id(x)] = x
        deps.pop(id(o), None)
        o.deps = list(deps.values())
        for t in r:
            if not t.const:
                key = ("dma", id(o)) if o.is_dma else o.eng
                t.rd[key] = o
        for t in w:
            t.lw = o
            t.rd = {}

    def op(self, eng, fn, r=(), w=()):
        o = Op(eng, fn)
        self.ecount[eng] += 1
        o.eidx = self.ecount[eng]
        self._track(o, r, w)
        self.ops[eng].append(o)
        self.last_op[eng] = o
        return o

    def dma(self, q, out, in_, key, r=(), w=(), **kw):
        def fn(e):
            return e.dma_start(out=out, in_=in_, **kw)
        o = Op(q, fn)
        o.is_dma = True
        if key.dsem is None:
            if self.free_sems:
                key.dsem, key.dcnt = self.free_sems.pop()
            else:
                key.dsem = self.nc.alloc_semaphore(name=f"ds{self.nsem}")
                self.nsem += 1
            self.dma_sems.append(key)
        key.dcnt += 16
        o.done = (key.dsem, key.dcnt)
        self.ecount[q] += 1
        o.eidx = self.ecount[q]
        self._track(o, r, w)
        self.ops[q].append(o)
        return o

    def barrier(self):
        lasts = [o for o in self.last_op.values() if o is not None]
        keys = [k_ for k_ in self.dma_sems if not k_.persist]
        for e in ENGS:
            def fn(eng):
                return None
            o = Op(e, fn)
            self.ecount[e] += 1
            o.eidx = self.ecount[e]
            o.deps = [x for x in lasts if x.eng != e]
            for k in keys:
                d = Op("sp", None)
                d.is_dma = True
                d.done = (k.dsem, k.dcnt)
                o.deps.append(d)
            self.ops[e].append(o)
        for k_ in keys:
            self.free_sems.append((k_.dsem, k_.dcnt))
        self.dma_sems = [k_ for k_ in self.dma_sems if k_.persist]

    def emit(self):
        nc = self.nc
        for e in ENGS:
            for o in self.ops[e]:
                for d in o.deps:
                    if d.is_dma:
                        sem, val = d.done
                        if self.waited_dma[e].get(sem.num, 0) >= val:
                            continue
                        self.waited_dma[e][sem.num] = val
                        o.waits.append(("d", sem, val))
                    else:
                        p = d.eng
                        if p == e and e == "pe":
                            continue
                        if self.waited[e].get(p, 0) >= d.eidx:
                            continue
                        self.waited[e][p] = d.eidx
                        d.signal = True
                        o.waits.append(("e", d))
        for e in ENGS:
            c = self.sigcount[e]
            for o in self.ops[e]:
                if o.signal:
                    c += 1
                    o.sigidx = c
            self.sigcount[e] = c
        engmap = {"pe": "tensor", "act": "scalar", "dve": "vector", "pool": "gpsimd", "sp": "sync"}
        with nc.Block() as block:
            for e in ENGS:
                ops = self.ops[e]

                def body(eng, ops=ops, e=e):
                    for o in ops:
                        for wt in o.waits:
                            if wt[0] == "d":
                                eng.wait_ge(wt[1], wt[2])
                            else:
                                d = wt[1]
                                eng.wait_ge(self.esem[d.eng], d.sigidx)
                        ins = o.fn(eng)
                        if o.is_dma:
                            ins.then_inc(o.done[0], 16)
                        elif o.signal:
                            assert ins is not None
                            ins.then_inc(self.esem[e], 1)
                getattr(block, engmap[e])(body)
        self.ops = {e: [] for e in ENGS}


class Rot:
    def __init__(self, tiles):
        self.tiles = tiles
        self.i = 0

    def next(self):
        t = self.tiles[self.i % len(self.tiles)]
        self.i += 1
        return t


class Buf:
    def __init__(self, es, nc, name, shape, dt, psum=False, const=False):
        if psum:
            self.t = es.enter_context(nc.psum_tensor(name, shape, dt))
        else:
            self.t = es.enter_context(nc.sbuf_tensor(name, shape, dt))
        self.tl = Tl(name, const=const)

    def __getitem__(self, idx):
        return self.t[idx]


class View:
    def __init__(self, base, idx, name):
        self.base = base
        self.idx = idx
        self.tl = Tl(name)

    def __getitem__(self, idx):
        return self.base.t[self.idx][idx]


def mk(es, nc, name, shape, dt, n=1, psum=False):
    return Rot([Buf(es, nc, f"{name}{i}", shape, dt, psum=psum) for i in range(n)])


class K:
    def __init__(self, nc, dbg=None, stop_after=99):
        self.nc = nc
        self.sc = Sched(nc)
        self.dbg = dbg or {}
        self.stop_after = stop_after
        self.dram = {}
        self.dtl = {}

    def din(self, name, shape, dt=F32):
        self.dram[name] = self.nc.dram_tensor(name, list(shape), dt, kind="ExternalInput").ap()
        self.dtl[name] = Tl(name, const=True)
        return self.dram[name]

    def dout(self, name, shape, dt=F32):
        self.dram[name] = self.nc.dram_tensor(name, list(shape), dt, kind="ExternalOutput").ap()
        return self.dram[name]

    def dscr(self, name, shape, dt=BF16):
        kind = "ExternalOutput" if name in self.dbg else "Internal"
        self.dram[name] = self.nc.dram_tensor(name, list(shape), dt, kind=kind).ap()
        return self.dram[name]

    def act(self, out, in_, func, r, w, bias=None, scale=None, accum=None):
        kw = {}
        if bias is not None:
            kw["bias"] = bias
        if scale is not None:
            kw["scale"] = scale
        if accum is not None:
            kw["accum_out"] = accum
        return self.sc.op("act", lambda e: e.activation(out=out, in_=in_, func=func, **kw), r, w)

    def tt(self, out, a, b, op, r, w, eng="dve"):
        return self.sc.op(eng, lambda e: e.tensor_tensor(out=out, in0=a, in1=b, op=op), r, w)

    def ts(self, out, a, s1, s2, op0, op1, r, w, eng="dve"):
        if op1 is None:
            return self.sc.op(eng, lambda e: e.tensor_scalar(out=out, in0=a, scalar1=s1, scalar2=None, op0=op0), r, w)
        return self.sc.op(eng, lambda e: e.tensor_scalar(out=out, in0=a, scalar1=s1, scalar2=s2, op0=op0, op1=op1), r, w)

    def stt(self, out, a, s, b, op0, op1, r, w):
        return self.sc.op("dve", lambda e: e.scalar_tensor_tensor(out=out, in0=a, scalar=s, in1=b, op0=op0, op1=op1), r, w)

    def cp(self, out, in_, r, w, eng="dve"):
        if eng == "act":
            return self.sc.op("act", lambda e: e.activation(out=out, in_=in_, func=AF.Copy), r, w)
        return self.sc.op(eng, lambda e: e.tensor_copy(out=out, in_=in_), r, w)

    def recip(self, out, in_, r, w):
        return self.sc.op("dve", lambda e: e.reciprocal(out=out, in_=in_), r, w)

    def mm(self, out, lhsT, rhs, start, stop, r, w):
        return self.sc.op("pe", lambda e: e.matmul(out, lhsT, rhs, start=start, stop=stop), r, w)

    def tr(self, out, in_, ident, r, w):
        return self.sc.op("pe", lambda e: e.transpose(out, in_, ident), r, w)

    def ld(self, out, in_, key, r=(), q="sp", **kw):
        return self.sc.dma(q, out, in_, key.tl, r=r, w=[key.tl], **kw)

    def st(self, out, in_, key, w=(), q="pool", **kw):
        return self.sc.dma(q, out, in_, key.tl, r=[key.tl], w=w, **kw)

    def memset(self, ap, val, w, eng="dve"):
        return self.sc.op(eng, lambda e: e.memset(ap, val), (), w)

    def rstd_from_ss(self, rstd, ss, tmp, n, r, w):
        self.ts(tmp, ss, 1.0 / n, EPS, ALU.mult, ALU.add, r, w)
        self.act(tmp, tmp, AF.Sqrt, w, w)
        self.recip(rstd, tmp, w, w)


def build(nc, dbg=None, stop_after=99):
    k = K(nc, dbg, stop_after)
    sc = k.sc
    x = k.din("x", [S, D])
    mem = k.din("mem", [256, D])
    w_in = k.din("w_in", [D, IN_W])
    w_q_up = k.din("w_q_up", [512, 1536])
    w_kv_up = k.din("w_kv_up", [512, 2048])
    w_out = k.din("w_out", [D, D])
    w_mq = k.din("w_mq", [D, D])
    w_mk = k.din("w_mk", [D, D])
    w_mv = k.din("w_mv", [D, D])
    w_mo = k.din("w_mo", [D, D])
    w_up = k.din("w_up", [D, 2 * DFF])
    w_dn = k.din("w_dn", [DFF, D])
    consts = k.din("consts", [128, 1408])
    gcols = k.din("gcols", [128, 72])
    gpost = k.din("gpost", [128, 3 * D + 256])
    convp = k.din("convp", [128, 4 * 88])
    w2aug = k.din("w2aug", [33, 1024])
    ropet = k.din("ropet", [64, 2 * S])
    y_out = k.dout("y", [S, D])

    Wi = k.dscr("Wi", [D, IN_W])
    Wq = k.dscr("Wq", [512, 1536])
    Wkv = k.dscr("Wkv", [512, 2048])
    Wo = k.dscr("Wo", [D, D])
    Wmq = k.dscr("Wmq", [D, D])
    Wmk = k.dscr("Wmk", [D, D])
    Wmv = k.dscr("Wmv", [D, D])
    Wmo = k.dscr("Wmo", [D, D])
    Wup = k.dscr("Wup", [D, 2 * DFF])
    Wdn = k.dscr("Wdn", [DFF, D])
    Wkrr = k.dscr("Wkrr", [D, 64])
    Wqr = k.dscr("Wqr", [512, 512])
    H1T = k.dscr("H1T", [128, KC, S])
    MKT = k.dscr("MKT", [128, KC, 256])
    MV = k.dscr("MV", [2, 128, D])
    SILU = k.dscr("SILU", [S, 1024], F32)
    QBB = k.dscr("QBB", [128, 4, S])
    DECB = k.dscr("DECB", [128, 4, 64], F32)
    KDB = k.dscr("KDB", [S, 512])
    VG = k.dscr("VG", [S, 1024])
    OP = k.dscr("OP", [S, 1024], F32)
    silu_tl = [Tl(f"SILU{t}") for t in range(NT)]
    qbb_tl = [Tl(f"QBB{t}") for t in range(NT)]
    decb_tl = [Tl(f"DECB{t}") for t in range(NT)]
    kdb_tl = [Tl(f"KDB{t}") for t in range(NT)]
    vg_tl = [Tl(f"VG{t}") for t in range(NT)]
    op_tl = [Tl(f"OP{t}") for t in range(NT)]
    KPE = k.dscr("KPE", [64, S])
    QTN = k.dscr("QTN", [8, 128, S])
    QTR = k.dscr("QTR", [8, 64, S])
    KTN = k.dscr("KTN", [8, 128, S])
    VS = k.dscr("VS", [8, 128, 32, 128])
    OGT = k.dscr("OGT", [128, 8, S])
    OMT = k.dscr("OMT", [128, 8, S])
    kpe_tl = [Tl(f"KPE{t}") for t in range(NT)]
    qtn_tl = [Tl(f"QTN{t}") for t in range(NT)]
    qtr_tl = [Tl(f"QTR{t}") for t in range(NT)]
    ktn_tl = [Tl(f"KTN{t}") for t in range(NT)]
    vs_tl = [Tl(f"VS{t}") for t in range(NT)]
    ogt_tl = [Tl(f"OGT{t}") for t in range(NT)]
    omt_tl = [Tl(f"OMT{t}") for t in range(NT)]
    X2 = k.dscr("X2", [S, D], F32)
    H3T = k.dscr("H3T", [128, KC, S])
    AT = k.dscr("AT", [128, 44, S])
    x2_tl = [Tl(f"X2{t}") for t in range(NT)]
    h3t_tl = [Tl(f"H3T{t}") for t in range(NT)]
    at_tl = [Tl(f"AT{t}") for t in range(NT)]
    yout_tl = Tl("yout")
    mkt_tl = Tl("MKT")
    mv_tl = Tl("MV")
    h1t_tl = [Tl(f"H1T{t}") for t in range(NT)]
    wt = {n: Tl(n, const=True) for n in ("Wi", "Wq", "Wkv", "Wo", "Wmq", "Wmk", "Wmv", "Wmo", "Wup", "Wdn", "Wkrr", "Wqr")}

    def cast_chunks(dst, src, name, rows, rchunk):
        tl = wt[name]
        key = Tl("k_" + name)
        key.persist = True
        opsl = []
        n = (rows + rchunk - 1) // rchunk

        def issue(r0):
            r1 = min(rows, r0 + rchunk)
            opsl.append(sc.dma("pool", dst[r0:r1, :], src[r0:r1, :], key, r=(), w=(), max_dma_last_dim=8192))
            if len(opsl) == n:
                for o in opsl:
                    o.done = (key.dsem, key.dcnt)
                tl.lw = opsl[-1]
        return [(lambda r0=r0: issue(r0)) for r0 in range(0, rows, rchunk)]

    def cast_w(dst, src, name, rows, cols, rchunk):
        for th in cast_chunks(dst, src, name, rows, rchunk):
            th()

    with ExitStack() as es:
        cast_w(Wmk, w_mk, "Wmk", D, D, 1024)
        cast_w(Wmv, w_mv, "Wmv", D, D, 1024)
        cast_w(Wi, w_in, "Wi", D, IN_W, 512)
        cast_w(Wq, w_q_up, "Wq", 512, 1536, 512)
        cast_w(Wkv, w_kv_up, "Wkv", 512, 2048, 512)


        cst = Buf(es, nc, "cst0", [128, 1408], F32)
        k.ld(cst[:], consts, cst)
        identb = Buf(es, nc, "identb0", [128, 128], BF16)
        k.cp(identb[:], cst[:, 0:128], [cst.tl], [identb.tl])
        gc = Buf(es, nc, "gc0", [128, 72], F32)
        k.ld(gc[:], gcols, gc)
        xs = mk(es, nc, "p1xs", [128, D], BF16, n=4)
        stt_ = mk(es, nc, "p1st", [128, 4], F32, n=4)
        pT = mk(es, nc, "p1pT", [128, 512], BF16, n=2, psum=True)

        xb = mk(es, nc, "p1xb", [128, D], F32, n=6)
        hT = mk(es, nc, "p1hT", [128, KC, TT], BF16, n=2)
        for t in range(NT):
            blocks = []
            for b in range(4):
                xbuf = xb.next()
                k.ld(xbuf[:], x[t * 512 + b * 128: t * 512 + (b + 1) * 128, :], xbuf)
                blocks.append(xbuf)
            h = hT.next()
            norm_T(k, blocks, xs.tiles, stt_.tiles, gc, 0, identb, pT, h)
            k.st(H1T[:, :, t * 512:(t + 1) * 512], h[:], h, w=[h1t_tl[t]], q="act")

        tkr = Buf(es, nc, "tkr", [128, KC, 64], F32)
        k.ld(tkr[:], w_in[:, 4128:4192].rearrange("(kc p) c -> p kc c", p=128), tkr)
        rkr = Buf(es, nc, "rkr", [128, KC, 64], BF16)
        k.ts(rkr[:, :, 0:32], tkr[:, :, 32:64], -1.0, None, ALU.mult, None, [tkr.tl], [rkr.tl])
        k.cp(rkr[:, :, 32:64], tkr[:, :, 0:32], [tkr.tl], [rkr.tl], eng="act")
        k.st(Wkrr.rearrange("(kc p) c -> p kc c", p=128), rkr[:], rkr, w=[wt["Wkrr"]], q="act")
        tq = Buf(es, nc, "tq", [128, 4, 8, 64], F32)
        wqv = w_q_up.rearrange("(kc p) (h c) -> p kc h c", p=128, c=192)
        for kc in range(4):
            k.ld(tq[:, kc, :, :], wqv[:, kc, :, 128:192], tq)
        rq = Buf(es, nc, "rq", [128, 4, 8, 64], BF16)
        for kc in range(4):
            k.ts(rq[:, kc, :, 0:32], tq[:, kc, :, 32:64], -1.0, None, ALU.mult, None, [tq.tl], [rq.tl])
            k.cp(rq[:, kc, :, 32:64], tq[:, kc, :, 0:32], [tq.tl], [rq.tl], eng="act")
        for kc in range(4):
            k.st(Wqr.rearrange("(kc p) (h c) -> p kc h c", p=128, c=64)[:, kc, :, :], rq[:, kc, :, :], rq, w=[wt["Wqr"]], q="act")

        mT = Buf(es, nc, "p0mT", [128, KC, 256], BF16)
        blocks = []
        for b in range(2):
            xbuf = xb.next()
            k.ld(xbuf[:], mem[b * 128:(b + 1) * 128, :], xbuf)
            blocks.append(xbuf)
        norm_T(k, blocks, xs.tiles, stt_.tiles, gc, 48, identb, pT, mT)
        wb = mk(es, nc, "p0wb", [128, KC, 512], BF16, n=2)
        ps = mk(es, nc, "p0ps", [128, 512], F32, n=3, psum=True)
        mkt = Buf(es, nc, "p0mkt", [128, KC, 256], BF16)
        mvs = Buf(es, nc, "p0mvs", [128, 2, D], BF16)
        for g in range(4):
            w_ = wb.next()
            ldw(k, wt, w_, Wmk, "Wmk", 0, KC, g * 512, 512)
            for j in range(4):
                p_ = ps.next()
                for kc in range(KC):
                    k.mm(p_[:, 0:256], w_[:, kc, j * 128:(j + 1) * 128], mT[:, kc, :], kc == 0, kc == KC - 1,
                         [w_.tl, mT.tl], [p_.tl])
                k.cp(mkt[:, g * 4 + j, :], p_[:, 0:256], [p_.tl], [mkt.tl], eng=("act" if j % 2 else "dve"))
        for g in range(4):
            w_ = wb.next()
            ldw(k, wt, w_, Wmv, "Wmv", 0, KC, g * 512, 512)
            for kb in range(2):
                p_ = ps.next()
                for kc in range(KC):
                    k.mm(p_[:], mT[:, kc, kb * 128:(kb + 1) * 128], w_[:, kc, :], kc == 0, kc == KC - 1,
                         [w_.tl, mT.tl], [p_.tl])
                k.cp(mvs[:, kb, g * 512:(g + 1) * 512], p_[:], [p_.tl], [mvs.tl], eng=("act" if kb % 2 else "dve"))
        k.st(MKT, mkt[:], mkt, w=[mkt_tl], q="act")
        k.st(MV.rearrange("kb p d -> p kb d"), mvs[:], mvs, w=[mv_tl], q="act")
        sc.barrier()
        sc.emit()
    if stop_after <= 1:
        return k

    with ExitStack() as es:
        cst = Buf(es, nc, "cst2", [128, 1408], F32)
        k.ld(cst[:], consts, cst)
        cst.tl.const = True
        C1 = (cst[:, 128:256], cst[:, 512:640])
        C2 = (cst[:, 256:384], cst[:, 640:768])
        C3 = (cst[:, 384:512], cst[:, 768:896])
        MSK = (cst[:, 896:1024], cst[:, 1024:1152])
        w2f = Buf(es, nc, "w2f", [33, 1024], F32)
        k.ld(w2f[:], w2aug, w2f)
        w2b = Buf(es, nc, "w2b", [33, 1024], BF16)
        k.cp(w2b[:], w2f[:], [w2f.tl], [w2b.tl])
        w2b.tl.const = True
        wgg = Buf(es, nc, "wgg", [128, KC, 32], BF16)
        ldw(k, wt, wgg, Wi, "Wi", 0, KC, 3072, 32)
        wgg.tl.const = True
        ggaug = Buf(es, nc, "ggaug", [33, TT], BF16)
        k.memset(ggaug[32:33, :], 1.0, [ggaug.tl])
        hT = mk(es, nc, "p2hT", [128, KC, TT], BF16, n=1)
        wb = mk(es, nc, "p2wb", [128, KC, 512], BF16, n=2)
        ps = mk(es, nc, "p2ps", [128, 512], F32, n=3, psum=True)
        ops_ = mk(es, nc, "p2o", [128, 1024], F32, n=1, psum=True)
        kvps = mk(es, nc, "p2kv", [128, 1024], F32, n=1, psum=True)
        qT = Buf(es, nc, "qT", [128, 4, TT], F32)
        kT = Buf(es, nc, "kT", [128, 4, TT], F32)
        kTM = Buf(es, nc, "kTM", [128, 4, 512], F32)
        sp = [[Buf(es, nc, f"sp{d_}{b}", [128, 512], F32) for b in range(4)] for d_ in range(2)]
        etmp = mk(es, nc, "etmp", [128, 512], F32, n=2)
        Eb = mk(es, nc, "Eb", [128, 512], F32, n=6)
        qi = [Buf(es, nc, f"qi{d_}", [128, 4, TT], BF16) for d_ in range(2)]
        ki = [Buf(es, nc, f"ki{d_}", [128, 4, TT], BF16) for d_ in range(2)]
        qb = [Buf(es, nc, f"qb{d_}", [128, 4, TT], BF16) for d_ in range(2)]
        kdec = [[Buf(es, nc, f"kdec{d_}{b}", [128, 512], BF16) for b in range(4)] for d_ in range(2)]
        vv = [Buf(es, nc, f"vv{b}", [128, 1024], BF16) for b in range(4)]
        decf = Buf(es, nc, "decf", [128, 4, 8], F32)
        decb = Buf(es, nc, "decb", [128, 4, 8], F32)
        sstage = mk(es, nc, "sstage", [128, 512], F32, n=2)
        ostage = mk(es, nc, "ostage", [128, 1024], F32, n=2)
        attsb = mk(es, nc, "attsb", [128, 128], BF16, n=4)
        Sf = Buf(es, nc, "Sf", [128, 1024], F32)
        Sfb = Buf(es, nc, "Sfb", [128, 1024], BF16)
        k.memset(Sf[:], 0.0, [Sf.tl])
        k.memset(Sfb[:], 0.0, [Sfb.tl])
        for t in range(NT):
            tok0 = t * TT
            h = hT.next()
            k.ld(h[:], H1T[:, :, tok0:tok0 + TT], h, r=[h1t_tl[t]])
            p_ = ps.next()
            for kc in range(KC):
                k.mm(p_[0:32, :], wgg[:, kc, :], h[:, kc, :], kc == 0, kc == KC - 1, [wgg.tl, h.tl], [p_.tl])
            k.cp(ggaug[0:32, :], p_[0:32, :], [p_.tl], [ggaug.tl], eng="act")
            for d_ in range(2):
                for b in range(4):
                    p_ = ps.next()
                    k.mm(p_[:], ggaug[0:33, b * 128:(b + 1) * 128], w2b[0:33, d_ * 512:(d_ + 1) * 512], True, True,
                         [ggaug.tl, w2b.tl], [p_.tl])
                    e_ = etmp.next()
                    k.act(e_[:], p_[:], AF.Exp, [p_.tl], [e_.tl], scale=-1.0)
                    k.act(sp[d_][b][:], e_[:], AF.Ln, [e_.tl], [sp[d_][b].tl], bias=1.0)
            w_ = wb.next()
            ldw(k, wt, w_, Wi, "Wi", 0, KC, 0, 512)
            for hh in range(4):
                p_ = ps.next()
                for kc in range(KC):
                    k.mm(p_[:], w_[:, kc, hh * 128:(hh + 1) * 128], h[:, kc, :], kc == 0, kc == KC - 1, [w_.tl, h.tl], [p_.tl])
                k.act(qT[:, hh, :], p_[:], AF.Copy, [p_.tl], [qT.tl], scale=float(128 ** -0.5))
            w_ = wb.next()
            ldw(k, wt, w_, Wi, "Wi", 0, KC, 512, 512)
            for hh in range(4):
                p_ = ps.next()
                for kc in range(KC):
                    k.mm(p_[:], w_[:, kc, hh * 128:(hh + 1) * 128], h[:, kc, :], kc == 0, kc == KC - 1, [w_.tl, h.tl], [p_.tl])
                k.cp(kT[:, hh, :], p_[:], [p_.tl], [kT.tl])
            for b in range(4):
                p_ = ps.next()
                for kc in range(KC):
                    k.mm(p_[:], h[:, kc, b * 128:(b + 1) * 128], w_[:, kc, :], kc == 0, kc == KC - 1, [w_.tl, h.tl], [p_.tl])
                k.cp(kTM[:, b, :], p_[:], [p_.tl], [kTM.tl], eng="act")
            for g in range(2):
                w_ = wb.next()
                ldw(k, wt, w_, Wi, "Wi", 0, KC, 1024 + g * 512, 512)
                for b in range(4):
                    p_ = ps.next()
                    for kc in range(KC):
                        k.mm(p_[:], h[:, kc, b * 128:(b + 1) * 128], w_[:, kc, :], kc == 0, kc == KC - 1, [w_.tl, h.tl], [p_.tl])
                    k.cp(vv[b][:, g * 512:(g + 1) * 512], p_[:], [p_.tl], [vv[b].tl], eng=("act" if b % 2 else "dve"))
            gr_w = []
            for g in range(2):
                w_ = wb.next()
                ldw(k, wt, w_, Wi, "Wi", 0, KC, 2048 + g * 512, 512)
                gr_w.append(w_)

            def gr_batch(g, b, h=h, tok0=tok0, t=t, gr_w=gr_w):
                w_ = gr_w[g]
                p_ = ps.next()
                for kc in range(KC):
                    k.mm(p_[:], h[:, kc, b * 128:(b + 1) * 128], w_[:, kc, :], kc == 0, kc == KC - 1, [w_.tl, h.tl], [p_.tl])
                s_ = sstage.next()
                k.act(s_[:], p_[:], AF.Silu, [p_.tl], [s_.tl])
                k.st(SILU[tok0 + b * 128: tok0 + (b + 1) * 128, g * 512:(g + 1) * 512], s_[:], s_, w=[silu_tl[t]])
            gr_todo = [(g, b) for g in range(2) for b in range(4)]
            for d_ in range(2):
                for hh in range(4):
                    p1 = ps.next()
                    for b in range(4):
                        k.mm(p1[:, b * 128:(b + 1) * 128], sp[d_][b][:, hh * 128:(hh + 1) * 128], C1[d_], True, True,
                             [sp[d_][b].tl], [p1.tl])
                    e1 = Eb.next()
                    k.act(e1[:], p1[:], AF.Exp, [p1.tl], [e1.tl])
                    e1i = Eb.next()
                    k.act(e1i[:], p1[:], AF.Exp, [p1.tl], [e1i.tl], scale=-1.0)
                    p2_ = ps.next()
                    for b in range(4):
                        k.mm(p2_[:, b * 128:(b + 1) * 128], sp[d_][b][:, hh * 128:(hh + 1) * 128], C2[d_], True, True,
                             [sp[d_][b].tl], [p2_.tl])
                    e2 = Eb.next()
                    k.act(e2[:], p2_[:], AF.Exp, [p2_.tl], [e2.tl])
                    k.tt(qi[d_][:, hh, :], qT[:, hh, :], e1[:], ALU.mult, [qT.tl, e1.tl], [qi[d_].tl])
                    k.tt(ki[d_][:, hh, :], kT[:, hh, :], e1i[:], ALU.mult, [kT.tl, e1i.tl], [ki[d_].tl])
                    k.tt(qb[d_][:, hh, :], qT[:, hh, :], e2[:], ALU.mult, [qT.tl, e2.tl], [qb[d_].tl])
                    if d_ == 0:
                        k.cp(decf[:, hh, :], e2[:, 63:512:64], [e2.tl], [decf.tl])
                    else:
                        k.cp(decb[:, hh, :], e2[:, 0:512:64], [e2.tl], [decb.tl])
                for b in range(4):
                    p3 = ps.next()
                    k.mm(p3[:], C3[d_], sp[d_][b][:], True, True, [sp[d_][b].tl], [p3.tl])
                    e3 = Eb.next()
                    k.act(e3[:], p3[:], AF.Exp, [p3.tl], [e3.tl])
                    k.tt(kdec[d_][b][:], kTM[:, b, :], e3[:], ALU.mult, [kTM.tl, e3.tl], [kdec[d_][b].tl])
            k.st(QBB[:, :, tok0:tok0 + TT], qb[1][:], qb[1], w=[qbb_tl[t]])
            k.st(DECB[:, :, t * 8:(t + 1) * 8], decb[:], decb, w=[decb_tl[t]])
            for b in range(4):
                k.st(KDB[tok0 + b * 128: tok0 + (b + 1) * 128, :], kdec[1][b][:], kdec[1][b], w=[kdb_tl[t]])
                k.st(VG[tok0 + b * 128: tok0 + (b + 1) * 128, :], vv[b][:], vv[b], w=[vg_tl[t]])
            for b in range(4):
                o_ = ops_.next()
                blk = slice(b * 128, (b + 1) * 128)
                for hh in range(4):
                    hs = slice(hh * 256, (hh + 1) * 256)
                    for d_ in range(2):
                        a_ = ps.next()
                        k.mm(a_[:, 0:128], ki[d_][:, hh, blk], qi[d_][:, hh, blk], True, True, [ki[d_].tl, qi[d_].tl], [a_.tl])
                        as_ = attsb.next()
                        k.tt(as_[:], a_[:, 0:128], MSK[d_], ALU.mult, [a_.tl], [as_.tl])
                        k.mm(o_[:, hs], as_[:], vv[b][:, hs], d_ == 0 and hh % 2 == 0, False, [as_.tl, vv[b].tl], [o_.tl])
                for c in range(2):
                    rows = slice(c * 64, (c + 1) * 64)
                    cs = slice(b * 128 + c * 64, b * 128 + (c + 1) * 64)
                    for hh in range(4):
                        hs = slice(hh * 256, (hh + 1) * 256)
                        k.mm(o_[rows, hs], qb[0][:, hh, cs], Sfb[:, hs], False, True, [qb[0].tl, Sfb.tl], [o_.tl])
                    kv_ = kvps.next()
                    for hh in range(4):
                        hs = slice(hh * 256, (hh + 1) * 256)
                        k.mm(kv_[:, hs], kdec[0][b][rows, hh * 128:(hh + 1) * 128], vv[b][rows, hs], True, True,
                             [kdec[0][b].tl, vv[b].tl], [kv_.tl])
                    for hh in range(4):
                        hs = slice(hh * 256, (hh + 1) * 256)
                        k.stt(Sf[:, hs], Sf[:, hs], decf[:, hh, b * 2 + c:b * 2 + c + 1], kv_[:, hs], ALU.mult, ALU.add,
                              [Sf.tl, decf.tl, kv_.tl], [Sf.tl])
                    k.cp(Sfb[:], Sf[:], [Sf.tl], [Sfb.tl], eng="act")
                    if gr_todo:
                        gr_batch(*gr_todo.pop(0))
                og = ostage.next()
                k.cp(og[:], o_[:], [o_.tl], [og.tl], eng="act")
                k.st(OP[tok0 + b * 128: tok0 + (b + 1) * 128, :], og[:], og, w=[op_tl[t]])
        sc.barrier()
        sc.emit()
    if stop_after <= 2:
        return k

    with ExitStack() as es:
        late = cast_chunks(Wo, w_out, "Wo", D, 512) + cast_chunks(Wmq, w_mq, "Wmq", D, 512) + cast_chunks(Wmo, w_mo, "Wmo", D, 512)
        cst = Buf(es, nc, "cst3", [128, 1408], F32)
        k.ld(cst[:], consts, cst)
        cst.tl.const = True
        ones32 = cst[:, 1152:1280]
        gc = Buf(es, nc, "gc3", [128, 72], F32)
        k.ld(gc[:], gcols, gc)
        gc.tl.const = True
        wmla = Buf(es, nc, "wmla", [128, KC, 1152], BF16)
        ldw(k, wt, wmla, Wi, "Wi", 0, KC, 3104, 1088)
        ldw(k, wt, wmla, Wkrr, "Wkrr", 0, KC, 0, 64, col_off=1088)
        wmla.tl.const = True
        WqS = Buf(es, nc, "WqS", [128, 4, 8, 192], BF16)
        WqrS = Buf(es, nc, "WqrS", [128, 4, 8, 64], BF16)
        WkvS = Buf(es, nc, "WkvS", [128, 4, 8, 256], BF16)
        for c in range(4):
            k.ld(WqS[:, c, :, :], Wq[c * 128:(c + 1) * 128, :].rearrange("p (h c) -> p h c", c=192), WqS, r=[wt["Wq"]])
            k.ld(WqrS[:, c, :, :], Wqr[c * 128:(c + 1) * 128, :].rearrange("p (h c) -> p h c", c=64), WqrS, r=[wt["Wqr"]])
            k.ld(WkvS[:, c, :, :], Wkv[c * 128:(c + 1) * 128, :].rearrange("p (h c) -> p h c", c=256), WkvS, r=[wt["Wkv"]])
        for b_ in (WqS, WqrS, WkvS):
            b_.tl.const = True
        hT = mk(es, nc, "p3hT", [128, KC, TT], BF16, n=2)
        ps = mk(es, nc, "p3ps", [128, 512], F32, n=6, psum=True)
        cqb = Buf(es, nc, "cqb", [128, 4, TT], BF16)
        ckvb = Buf(es, nc, "ckvb", [128, 4, TT], BF16)
        sq = Buf(es, nc, "sq", [128, 4, TT], F32)
        rqbc = Buf(es, nc, "rqbc", [128, TT], F32)
        rkbc = Buf(es, nc, "rkbc", [128, TT], F32)
        rtm = Buf(es, nc, "rtm", [128, 16], F32)
        cs_ = mk(es, nc, "cossin", [64, 2, TT], F32, n=2)
        t1 = mk(es, nc, "ropet1", [64, TT], F32, n=2)
        t2 = mk(es, nc, "ropet2", [64, TT], F32, n=2)
        kpe_st = mk(es, nc, "kpe_st", [64, TT], BF16, n=2)
        qn_st = mk(es, nc, "qn_st", [128, 8, TT], BF16, n=2)
        qr_st = mk(es, nc, "qr_st", [64, 8, TT], BF16, n=2)
        kn_st = mk(es, nc, "kn_st", [128, 8, TT], BF16, n=2)
        v_st = mk(es, nc, "v_st", [128, 8, 128], BF16, n=4)
        QSC = float(192 ** -0.5)

        def rms_fm(src_off, gcol_off, dst_bf, rbc, h):
            for c in range(4):
                p_ = ps.next()
                for kc in range(KC):
                    k.mm(p_[:], wmla[:, kc, src_off + c * 128: src_off + (c + 1) * 128], h[:, kc, :], kc == 0, kc == KC - 1,
                         [wmla.tl, h.tl], [p_.tl])
                k.act(dst_bf[:, c, :], p_[:], AF.Copy, [p_.tl], [dst_bf.tl], scale=gc[:, gcol_off + c:gcol_off + c + 1])
                k.act(sq[:, c, :], p_[:], AF.Square, [p_.tl], [sq.tl])
            pss = ps.next()
            for c in range(4):
                k.mm(pss[:], ones32, sq[:, c, :], c == 0, c == 3, [sq.tl], [pss.tl])
            k.ts(rbc[:], pss[:], 1.0 / 512, EPS, ALU.mult, ALU.add, [pss.tl], [rbc.tl])
            k.act(rbc[:], rbc[:], AF.Sqrt, [rbc.tl], [rbc.tl])
            k.recip(rbc[:], rbc[:], [rbc.tl], [rbc.tl])

        for t in range(NT):
            tok0 = t * TT
            h = hT.next()
            k.ld(h[:], H1T[:, :, tok0:tok0 + TT], h, r=[h1t_tl[t]])
            cs = cs_.next()
            k.ld(cs[:, 0, :], ropet[:, tok0:tok0 + TT], cs)
            k.ld(cs[:, 1, :], ropet[:, S + tok0:S + tok0 + TT], cs)
            for _ in range(2):
                if late:
                    late.pop(0)()
            rms_fm(0, 64, cqb, rqbc, h)
            rms_fm(512, 68, ckvb, rkbc, h)
            for b in range(4):
                p_ = ps.next()
                for c in range(4):
                    k.mm(p_[:, 0:2], sq[:, c, b * 128:(b + 1) * 128], ones32[:, 0:2], c == 0, c == 3, [sq.tl], [p_.tl])
                k.ts(rtm[:, 4 * b + 1:4 * b + 2], p_[:, 0:1], 1.0 / 512, EPS, ALU.mult, ALU.add, [p_.tl], [rtm.tl])
            k.act(rtm[:], rtm[:], AF.Sqrt, [rtm.tl], [rtm.tl])
            k.recip(rtm[:], rtm[:], [rtm.tl], [rtm.tl])
            pr = ps.next()
            for kc in range(KC):
                k.mm(pr[0:64, :], wmla[:, kc, 1024:1088], h[:, kc, :], kc == 0, kc == KC - 1, [wmla.tl, h.tl], [pr.tl])
            pq = ps.next()
            for kc in range(KC):
                k.mm(pq[0:64, :], wmla[:, kc, 1088:1152], h[:, kc, :], kc == 0, kc == KC - 1, [wmla.tl, h.tl], [pq.tl])
            a1 = t1.next()
            a2 = t2.next()
            k.tt(a1[:], pr[0:64, :], cs[:, 0, :], ALU.mult, [pr.tl, cs.tl], [a1.tl])
            k.tt(a2[:], pq[0:64, :], cs[:, 1, :], ALU.mult, [pq.tl, cs.tl], [a2.tl])
            kp = kpe_st.next()
            k.tt(kp[:], a1[:], a2[:], ALU.add, [a1.tl, a2.tl], [kp.tl])
            k.st(KPE[0:64, tok0:tok0 + TT], kp[:], kp, w=[kpe_tl[t]])
            qn = qn_st.next()
            qr = qr_st.next()
            for hh in range(8):
                pn = ps.next()
                for c in range(4):
                    k.mm(pn[:], WqS[:, c, hh, 0:128], cqb[:, c, :], c == 0, c == 3, [WqS.tl, cqb.tl], [pn.tl])
                k.stt(qn[:, hh, :], pn[:], QSC, rqbc[:], ALU.mult, ALU.mult, [pn.tl, rqbc.tl], [qn.tl])
                pr = ps.next()
                for c in range(4):
                    k.mm(pr[0:64, :], WqS[:, c, hh, 128:192], cqb[:, c, :], c == 0, c == 3, [WqS.tl, cqb.tl], [pr.tl])
                pq = ps.next()
                for c in range(4):
                    k.mm(pq[0:64, :], WqrS[:, c, hh, :], cqb[:, c, :], c == 0, c == 3, [WqrS.tl, cqb.tl], [pq.tl])
                a1 = t1.next()
                a2 = t2.next()
                k.tt(a1[:], pr[0:64, :], cs[:, 0, :], ALU.mult, [pr.tl, cs.tl], [a1.tl])
                k.tt(a2[:], pq[0:64, :], cs[:, 1, :], ALU.mult, [pq.tl, cs.tl], [a2.tl])
                k.tt(a1[:], a1[:], a2[:], ALU.add, [a1.tl, a2.tl], [a1.tl], eng="pool")
                k.stt(qr[:, hh, :], a1[:], QSC, rqbc[0:64, :], ALU.mult, ALU.mult, [a1.tl, rqbc.tl], [qr.tl])
            k.st(QTN[:, :, tok0:tok0 + TT].rearrange("h p t -> p h t"), qn[:], qn, w=[qtn_tl[t]])
            k.st(QTR[:, :, tok0:tok0 + TT].rearrange("h p t -> p h t"), qr[:], qr, w=[qtr_tl[t]])
            kn = kn_st.next()
            for hh in range(8):
                pk = ps.next()
                for c in range(4):
                    k.mm(pk[:], WkvS[:, c, hh, 0:128], ckvb[:, c, :], c == 0, c == 3, [WkvS.tl, ckvb.tl], [pk.tl])
                k.tt(kn[:, hh, :], pk[:], rkbc[:], ALU.mult, [pk.tl, rkbc.tl], [kn.tl])
            k.st(KTN[:, :, tok0:tok0 + TT].rearrange("h p t -> p h t"), kn[:], kn, w=[ktn_tl[t]])
            for b in range(4):
                vs = v_st.next()
                for half in range(2):
                    pv = ps.next()
                    for c in range(4):
                        k.mm(pv[:], ckvb[:, c, b * 128:(b + 1) * 128], WkvS[:, c, half * 4:(half + 1) * 4, 128:256],
                             c == 0, c == 3, [WkvS.tl, ckvb.tl], [pv.tl])
                    k.act(vs[:, half * 4:(half + 1) * 4, :], pv[:].rearrange("p (h d) -> p h d", d=128), AF.Copy, [pv.tl, rtm.tl], [vs.tl],
                          scale=rtm[:, 4 * b + 1:4 * b + 2])
                k.st(VS[:, :, 4 * t + b, :].rearrange("h p d -> p h d"), vs[:], vs, w=[vs_tl[t]])
        while late:
            late.pop(0)()
        sc.barrier()
        sc.emit()
    if stop_after <= 3:
        return k

    with ExitStack() as es:
        late = cast_chunks(Wup, w_up, "Wup", D, 128) + cast_chunks(Wdn, w_dn, "Wdn", DFF, 352)
        cst = Buf(es, nc, "cst5", [128, 1408], F32)
        k.ld(cst[:], consts, cst)
        onesb = Buf(es, nc, "onesb5", [128, 128], BF16)
        k.cp(onesb[:], cst[:, 1152:1280], [cst.tl], [onesb.tl])
        onesb.tl.const = True
        identb = Buf(es, nc, "identb4", [128, 128], BF16)
        k.cp(identb[:], cst[:, 0:128], [cst.tl], [identb.tl])
        identb.tl.const = True
        gout = Buf(es, nc, "gout", [128, 256], F32)
        k.ld(gout[:], gpost[:, 3 * D:3 * D + 256], gout)
        gout.tl.const = True
        kpe = Buf(es, nc, "kpe5", [64, S], BF16)
        k.ld(kpe[:], KPE[0:64, :], kpe, r=kpe_tl)
        kpe.tl.const = True
        ktn = mk(es, nc, "ktn5", [128, S], BF16, n=2)
        vsb = mk(es, nc, "vs5", [128, 32, 128], BF16, n=2)
        qnb = mk(es, nc, "qn5", [128, TT], BF16, n=4)
        qrb = mk(es, nc, "qr5", [64, TT], BF16, n=4)
        ps = mk(es, nc, "p5ps", [128, 512], F32, n=3, psum=True)
        oTp = mk(es, nc, "p5oT", [128, 512], F32, n=1, psum=True)
        dnp = mk(es, nc, "p5dn", [128, 512], F32, n=1, psum=True)
        ptb = mk(es, nc, "p5pt", [128, 512], BF16, n=6)
        rden = mk(es, nc, "p5rd", [128, 512], F32, n=2)
        omb = mk(es, nc, "p5om", [128, 512], BF16, n=3)
        qbb = mk(es, nc, "p4qbb", [128, 4, TT], BF16, n=2)
        kdb = mk(es, nc, "p4kdb", [128, 512], BF16, n=8)
        vvb = mk(es, nc, "p4vv", [128, 1024], BF16, n=8)
        opb = mk(es, nc, "p4op", [128, 1024], F32, n=6)
        silb = mk(es, nc, "p4sil", [128, 1024], F32, n=6)
        dcb = mk(es, nc, "p4dec", [128, 4, 8], F32, n=2)
        oi_ps = mk(es, nc, "p4oi", [128, 512], F32, n=1, psum=True)
        kv_ps = mk(es, nc, "p4kv", [128, 512], F32, n=1, psum=True)
        pT = mk(es, nc, "p4pT", [128, 1024], BF16, n=1, psum=True)
        Sb = Buf(es, nc, "Sb", [128, 1024], F32)
        Sbb = Buf(es, nc, "Sbb", [128, 1024], BF16)
        k.memset(Sb[:], 0.0, [Sb.tl])
        k.memset(Sbb[:], 0.0, [Sbb.tl])
        junk = Buf(es, nc, "p4junk", [128, 256], BF16)
        ssb = mk(es, nc, "p4ss", [128, 4], F32, n=2)
        tmpn = mk(es, nc, "p4tmpn", [128, 1024], F32, n=2)
        ogb = mk(es, nc, "p4og", [128, 1024], BF16, n=2)
        ogT = mk(es, nc, "p4ogT", [128, 8, TT], BF16, n=2)

        def gla_bwd():
            for t in reversed(range(NT)):
                tok0 = t * TT
                q_ = qbb.next()
                k.ld(q_[:], QBB[:, :, tok0:tok0 + TT], q_, r=[qbb_tl[t]])
                dc = dcb.next()
                k.ld(dc[:], DECB[:, :, t * 8:(t + 1) * 8], dc, r=[decb_tl[t]])
                kd, vb, ob, sb = {}, {}, {}, {}
                for b in reversed(range(4)):
                    rs = slice(tok0 + b * 128, tok0 + (b + 1) * 128)
                    kd[b] = kdb.next()
                    k.ld(kd[b][:], KDB[rs, :], kd[b], r=[kdb_tl[t]])
                    vb[b] = vvb.next()
                    k.ld(vb[b][:], VG[rs, :], vb[b], r=[vg_tl[t]])
                    ob[b] = opb.next()
                    k.ld(ob[b][:], OP[rs, :], ob[b], r=[op_tl[t]])
                    sb[b] = silb.next()
                    k.ld(sb[b][:], SILU[rs, :], sb[b], r=[silu_tl[t]])
                yield
                oT_ = ogT.next()
                for b in reversed(range(4)):
                    o_ = ob[b]
                    for hp in range(2):
                        hps = slice(hp * 512, (hp + 1) * 512)
                        oi = oi_ps.next()
                        for c in (1, 0):
                            rows = slice(c * 64, (c + 1) * 64)
                            cs = slice(b * 128 + c * 64, b * 128 + (c + 1) * 64)
                            for hl in range(2):
                                hh = hp * 2 + hl
                                k.mm(oi[rows, hl * 256:(hl + 1) * 256], q_[:, hh, cs], Sbb[:, hh * 256:(hh + 1) * 256], True, True,
                                     [q_.tl, Sbb.tl], [oi.tl])
                            kv_ = kv_ps.next()
                            for hl in range(2):
                                hh = hp * 2 + hl
                                k.mm(kv_[:, hl * 256:(hl + 1) * 256], kd[b][rows, hh * 128:(hh + 1) * 128],
                                     vb[b][rows, hh * 256:(hh + 1) * 256], True, True, [kd[b].tl, vb[b].tl], [kv_.tl])
                            yield
                            for hl in range(2):
                                hh = hp * 2 + hl
                                hs = slice(hh * 256, (hh + 1) * 256)
                                k.stt(Sb[:, hs], Sb[:, hs], dc[:, hh, b * 2 + c:b * 2 + c + 1], kv_[:, hl * 256:(hl + 1) * 256],
                                      ALU.mult, ALU.add, [Sb.tl, dc.tl, kv_.tl], [Sb.tl])
                            yield
                            k.cp(Sbb[:, hps], Sb[:, hps], [Sb.tl], [Sbb.tl], eng="act")
                            yield
                        k.tt(o_[:, hps], oi[:], o_[:, hps], ALU.add, [oi.tl, o_.tl], [o_.tl])
                        yield
                    ss = ssb.next()
                    for hh in range(4):
                        hs = slice(hh * 256, (hh + 1) * 256)
                        k.act(junk[:], o_[:, hs], AF.Square, [o_.tl], [junk.tl, ss.tl], accum=ss[:, hh:hh + 1])
                    yield
                    k.ts(ss[:], ss[:], 1.0 / 256, EPS, ALU.mult, ALU.add, [ss.tl], [ss.tl])
                    yield
                    k.act(ss[:], ss[:], AF.Sqrt, [ss.tl], [ss.tl])
                    yield
                    k.recip(ss[:], ss[:], [ss.tl], [ss.tl])
                    tn = tmpn.next()
                    for hh in range(4):
                        hs = slice(hh * 256, (hh + 1) * 256)
                        k.stt(tn[:, hs], o_[:, hs], ss[:, hh:hh + 1], gout[:], ALU.mult, ALU.mult, [o_.tl, ss.tl], [tn.tl])
                    yield
                    og = ogb.next()
                    k.tt(og[:], tn[:], sb[b][:], ALU.mult, [tn.tl, sb[b].tl], [og.tl], eng="pool")
                    yield
                    yield
                    p_ = pT.next()
                    for c8 in range(8):
                        k.tr(p_[:, c8 * 128:(c8 + 1) * 128], og[:, c8 * 128:(c8 + 1) * 128], identb[:], [og.tl], [p_.tl])
                    yield
                    k.cp(oT_[:, :, b * 128:(b + 1) * 128], p_[:].rearrange("p (c t) -> p c t", t=128), [p_.tl], [oT_.tl], eng="act")
                    yield
                k.st(OGT[:, :, tok0:tok0 + TT], oT_[:], oT_, w=[ogt_tl[t]])

        gen = gla_bwd()
        gen_done = [False]

        def step_gen():
            if not gen_done[0]:
                try:
                    next(gen)
                except StopIteration:
                    gen_done[0] = True

        it_no = 0
        for hh in range(8):
            kt_ = ktn.next()
            k.ld(kt_[:], KTN[hh, :, :], kt_, r=ktn_tl)
            vs_ = vsb.next()
            k.ld(vs_[:], VS[hh, :, :, :], vs_, r=vs_tl)
            for t in range(NT):
                tok0 = t * TT
                if late and (hh * NT + t) % 2 == 0:
                    late.pop(0)()
                qn = qnb.next()
                k.ld(qn[:], QTN[hh, :, tok0:tok0 + TT], qn, r=[qtn_tl[t]])
                qr = qrb.next()
                k.ld(qr[:], QTR[hh, :, tok0:tok0 + TT], qr, r=[qtr_tl[t]])
                oT = oTp.next()
                dn = dnp.next()
                pend = []

                def score(kb):
                    ks = slice(kb * 128, (kb + 1) * 128)
                    sT = ps.next()
                    k.mm(sT[:], kt_[:, ks], qn[:], True, False, [kt_.tl, qn.tl], [sT.tl])
                    k.mm(sT[:], kpe[0:64, ks], qr[0:64, :], False, True, [kpe.tl, qr.tl], [sT.tl])
                    pt = ptb.next()
                    k.act(pt[:], sT[:], AF.Exp, [sT.tl], [pt.tl])
                    pend.append((kb, pt))

                def pv():
                    kb, pt = pend.pop(0)
                    k.mm(oT[:], vs_[:, kb, :], pt[:], kb == 0, kb == 31, [vs_.tl, pt.tl], [oT.tl])
                    k.mm(dn[:], onesb[:], pt[:], kb == 0, kb == 31, [pt.tl], [dn.tl])

                for kb in range(32):
                    score(kb)
                    if len(pend) > 2:
                        pv()
                    it_no += 1
                    if it_no % 2 == 0:
                        step_gen()
                while pend:
                    pv()
                rd = rden.next()
                k.recip(rd[:], dn[:], [dn.tl], [rd.tl])
                om = omb.next()
                k.tt(om[:], oT[:], rd[:], ALU.mult, [oT.tl, rd.tl], [om.tl])
                k.st(OMT[:, hh, tok0:tok0 + TT], om[:], om, w=[omt_tl[t]])
        while not gen_done[0]:
            step_gen()
        while late:
            late.pop(0)()
        sc.barrier()
        sc.emit()
    if stop_after <= 5:
        return k

    TC = 512
    NB = TC // 128
    with ExitStack() as es:
        cst = Buf(es, nc, "cst6", [128, 256], F32)
        k.ld(cst[:, 0:128], consts[:, 0:128], cst)
        k.ld(cst[:, 128:256], consts[:, 1152:1280], cst)
        identb = Buf(es, nc, "identb6", [128, 128], BF16)
        k.cp(identb[:], cst[:, 0:128], [cst.tl], [identb.tl])
        identb.tl.const = True
        onesb = Buf(es, nc, "onesb6", [128, 128], BF16)
        k.cp(onesb[:], cst[:, 128:256], [cst.tl], [onesb.tl])
        onesb.tl.const = True
        gc = Buf(es, nc, "gc6", [128, 72], F32)
        k.ld(gc[:], gcols, gc)
        gc.tl.const = True
        gp = Buf(es, nc, "gp6", [128, 2 * D], F32)
        k.ld(gp[:], gpost[:, 0:2 * D], gp)
        gp.tl.const = True
        mkt = Buf(es, nc, "mkt6", [128, KC, 256], BF16)
        k.ld(mkt[:], MKT, mkt, r=[mkt_tl])
        mkt.tl.const = True
        mvs = Buf(es, nc, "mvs6", [128, 2, D], BF16)
        k.ld(mvs[:], MV.rearrange("kb p d -> p kb d"), mvs, r=[mv_tl])
        mvs.tl.const = True
        actAr = mk(es, nc, "actA", [128, KC, TC], BF16, n=2)
        ssp = [Buf(es, nc, f"ssp{b}", [128, 4], F32) for b in range(NB)]
        junk2 = Buf(es, nc, "p6junk2", [128, 512], BF16)
        actB = Buf(es, nc, "actB", [128, KC, TC], BF16)
        wb = mk(es, nc, "p6wb", [128, KC, 512], BF16, n=2)
        ysb = [Buf(es, nc, f"ysb{b}", [128, D], F32) for b in range(NB)]
        xres = mk(es, nc, "xres", [128, D], F32, n=4)
        xs = mk(es, nc, "p6xs", [128, D], BF16, n=NB)
        stt_ = mk(es, nc, "p6st", [128, 4], F32, n=NB)
        pT = mk(es, nc, "p6pT", [128, 512], BF16, n=2, psum=True)
        ps = mk(es, nc, "p6ps", [128, 512], F32, n=6, psum=True)
        ptb = mk(es, nc, "p6pt", [128, TC], BF16, n=4)
        rdb = mk(es, nc, "p6rd", [128, TC], F32, n=2)
        MSC = float(512 ** -0.5)

        def proj_tm(W, name, src):
            for g in range(4):
                w_ = wb.next()
                ldw(k, wt, w_, W, name, 0, KC, g * 512, 512)
                for b in range(NB):
                    p_ = ps.next()
                    for kc in range(KC):
                        k.mm(p_[:], src[:, kc, b * 128:(b + 1) * 128], w_[:, kc, :], kc == 0, kc == KC - 1, [w_.tl, src.tl], [p_.tl])
                    k.act(ysb[b][:, g * 512:(g + 1) * 512], p_[:], AF.Copy, [p_.tl], [ysb[b].tl])
                    k.act(junk2[:], p_[:], AF.Square, [p_.tl], [junk2.tl, ssp[b].tl], accum=ssp[b][:, g:g + 1])

        def post_res(xr, goff):
            for b in range(NB):
                st = stt_.next()
                k.sc.op("dve", lambda e, o_=st[:, 0:1], i_=ssp[b][:, 0:4]: e.reduce_sum(out=o_, in_=i_, axis=AX.X), [ssp[b].tl], [st.tl])
                k.rstd_from_ss(st[:, 2:3], st[:, 0:1], st[:, 1:2], D, [st.tl], [st.tl])
                k.stt(ysb[b][:], ysb[b][:], st[:, 2:3], gp[:, goff:goff + D], ALU.mult, ALU.mult, [ysb[b].tl, st.tl], [ysb[b].tl])
                k.tt(xr[b][:], xr[b][:], ysb[b][:], ALU.add, [xr[b].tl, ysb[b].tl], [xr[b].tl], eng=("pool" if b % 2 else "dve"))

        for t in range(S // TC):
            tok0 = t * TC
            t5 = tok0 // TT
            actA = actAr.next()
            k.ld(actA[:, 0:8, :], OGT[:, :, tok0:tok0 + TC], actA, r=[ogt_tl[t5]])
            k.ld(actA[:, 8:16, :], OMT[:, :, tok0:tok0 + TC], actA, r=[omt_tl[t5]])
            xr = []
            for b in range(NB):
                xb_ = xres.next()
                k.ld(xb_[:], x[tok0 + b * 128: tok0 + (b + 1) * 128, :], xb_)
                xr.append(xb_)
            proj_tm(Wo, "Wo", actA)
            post_res(xr, 0)
            norm_T(k, xr, xs.tiles, stt_.tiles, gc, 16, identb, pT, actA)
            for g in range(4):
                w_ = wb.next()
                ldw(k, wt, w_, Wmq, "Wmq", 0, KC, g * 512, 512)
                for j in range(4):
                    p_ = ps.next()
                    for kc in range(KC):
                        k.mm(p_[:, 0:TC], w_[:, kc, j * 128:(j + 1) * 128], actA[:, kc, :], kc == 0, kc == KC - 1, [w_.tl, actA.tl], [p_.tl])
                    k.act(actB[:, g * 4 + j, :], p_[:, 0:TC], AF.Copy, [p_.tl], [actB.tl], scale=MSC)
            for hh in range(4):
                pts = []
                for kb in range(2):
                    sT = ps.next()
                    for dc in range(4):
                        k.mm(sT[:, 0:TC], mkt[:, hh * 4 + dc, kb * 128:(kb + 1) * 128], actB[:, hh * 4 + dc, :], dc == 0, dc == 3,
                             [actB.tl], [sT.tl])
                    pt = ptb.next()
                    k.act(pt[:], sT[:, 0:TC], AF.Exp, [sT.tl], [pt.tl])
                    pts.append(pt)
                dn = ps.next()
                for kb in range(2):
                    k.mm(dn[:, 0:TC], onesb[:], pts[kb][:], kb == 0, kb == 1, [pts[kb].tl], [dn.tl])
                rd = rdb.next()
                k.recip(rd[:], dn[:, 0:TC], [dn.tl], [rd.tl])
                for dvc in range(4):
                    po = ps.next()
                    for kb in range(2):
                        k.mm(po[:, 0:TC], mvs[:, kb, hh * 512 + dvc * 128: hh * 512 + (dvc + 1) * 128], pts[kb][:], kb == 0, kb == 1,
                             [pts[kb].tl], [po.tl])
                    k.tt(actA[:, hh * 4 + dvc, :], po[:, 0:TC], rd[:], ALU.mult, [po.tl, rd.tl], [actA.tl])
            proj_tm(Wmo, "Wmo", actA)
            post_res(xr, D)
            for b in range(NB):
                k.st(X2[tok0 + b * 128: tok0 + (b + 1) * 128, :], xr[b][:], xr[b], w=[x2_tl[t5]])
            norm_T(k, xr, xs.tiles, stt_.tiles, gc, 32, identb, pT, actB)
            k.st(H3T[:, :, tok0:tok0 + TC], actB[:], actB, w=[h3t_tl[t5]])
        sc.barrier()
        sc.emit()
    if stop_after <= 6:
        return k

    with ExitStack() as es:
        cvp = Buf(es, nc, "cvp", [128, 4, 88], F32)
        k.ld(cvp[:], convp.rearrange("p (a j) -> p a j", j=88), cvp)
        cvp.tl.const = True
        carry = [Buf(es, nc, f"carry{i}", [128, 2], F32) for i in range(88)]
        for i in range(88):
            k.memset(carry[i][:], 0.0, [carry[i].tl], eng="pool")
        hT = mk(es, nc, "p7hT", [128, KC, TT], BF16, n=2)
        wb = mk(es, nc, "p7wb", [128, KC, 1024], BF16, n=2)
        ps = mk(es, nc, "p7ps", [128, 512], F32, n=6, psum=True)
        ub = mk(es, nc, "p7ub", [128, 514], F32, n=6)
        cb = mk(es, nc, "p7cb", [128, 512], F32, n=6)
        gb = mk(es, nc, "p7gb", [128, 512], F32, n=3)
        atb = mk(es, nc, "p7at", [128, 4, 512], BF16, n=2)

        def conv(u, idx, out, eng0):
            w0 = cvp[:, 0, idx:idx + 1]
            w1 = cvp[:, 1, idx:idx + 1]
            w2 = cvp[:, 2, idx:idx + 1]
            bb = cvp[:, 3, idx:idx + 1]
            k.ts(out[:], u[:, 1:513], w1, bb, ALU.mult, ALU.add, [u.tl], [out.tl], eng=eng0)
            k.stt(out[:], u[:, 0:512], w0, out[:], ALU.mult, ALU.add, [u.tl, out.tl], [out.tl])
            k.stt(out[:], u[:, 2:514], w2, out[:], ALU.mult, ALU.add, [u.tl, out.tl], [out.tl])

        for t in range(NT):
            tok0 = t * TT
            h = hT.next()
            k.ld(h[:], H3T[:, :, tok0:tok0 + TT], h, r=[h3t_tl[t]])
            for j4 in range(11):
                w_ = wb.next()
                ldw(k, wt, w_, Wup, "Wup", 0, KC, j4 * 512, 512)
                ldw(k, wt, w_, Wup, "Wup", 0, KC, DFF + j4 * 512, 512, col_off=512)
                at = atb.next()
                for jj in range(4):
                    j = j4 * 4 + jj
                    us = []
                    for part in range(2):
                        p_ = ps.next()
                        for kc in range(KC):
                            k.mm(p_[:], w_[:, kc, part * 512 + jj * 128: part * 512 + (jj + 1) * 128], h[:, kc, :], kc == 0, kc == KC - 1,
                                 [w_.tl, h.tl], [p_.tl])
                        u = ub.next()
                        cy = carry[part * 44 + j]
                        k.cp(u[:, 0:2], cy[:], [cy.tl], [u.tl], eng="pool")
                        k.cp(u[:, 2:514], p_[:], [p_.tl], [u.tl], eng="act")
                        k.cp(cy[:], u[:, 512:514], [u.tl], [cy.tl], eng="pool")
                        us.append(u)
                    cg = cb.next()
                    conv(us[0], j, cg, "dve")
                    cv = cb.next()
                    conv(us[1], 44 + j, cv, "pool")
                    g_ = gb.next()
                    k.act(g_[:], cg[:], AF.Gelu_apprx_tanh, [cg.tl], [g_.tl])
                    k.tt(at[:, jj, :], g_[:], cv[:], ALU.mult, [g_.tl, cv.tl], [at.tl], eng="pool")
                if t == 0:
                    k.st(AT[:, j4 * 4:(j4 + 1) * 4, 0:511], at[:, :, 1:512], at, w=[at_tl[0]])
                else:
                    k.st(AT[:, j4 * 4:(j4 + 1) * 4, tok0 - 1:tok0 + 511], at[:, :, :], at, w=[at_tl[t], at_tl[t - 1]])
        cl = Buf(es, nc, "p7cl", [128, 88], F32)
        cl2 = Buf(es, nc, "p7cl2", [128, 88], F32)
        for i in range(88):
            cy = carry[i]
            k.stt(cl[:, i:i + 1], cy[:, 1:2], cvp[:, 1, i:i + 1], cvp[:, 3, i:i + 1], ALU.mult, ALU.add, [cy.tl], [cl.tl])
            k.stt(cl2[:, i:i + 1], cy[:, 0:1], cvp[:, 0, i:i + 1], cl[:, i:i + 1], ALU.mult, ALU.add, [cy.tl, cl.tl], [cl2.tl])
        gl = Buf(es, nc, "p7gl", [128, 44], F32)
        k.act(gl[:], cl2[:, 0:44], AF.Gelu_apprx_tanh, [cl2.tl], [gl.tl])
        al = Buf(es, nc, "p7al", [128, 44], BF16)
        k.tt(al[:], gl[:], cl2[:, 44:88], ALU.mult, [gl.tl, cl2.tl], [al.tl])
        for q4 in range(4):
            k.st(AT[:, q4 * 11:(q4 + 1) * 11, 4095:4096], al[:, q4 * 11:(q4 + 1) * 11].rearrange("p (j o) -> p j o", o=1), al, w=[at_tl[7]],
                 allow_slow_non_contiguous=True)
        sc.barrier()
        sc.emit()
    if stop_after <= 7:
        return k

    with ExitStack() as es:
        gp = Buf(es, nc, "gp8", [128, D], F32)
        k.ld(gp[:], gpost[:, 2 * D:3 * D], gp)
        gp.tl.const = True
        atl = mk(es, nc, "p8at", [128, 44, TT], BF16, n=2)
        xres = mk(es, nc, "p8x", [128, D], F32, n=4)
        wdb = mk(es, nc, "p8wd", [128, 1024], BF16, n=8)
        acc = mk(es, nc, "p8acc", [128, 1024], F32, n=4, psum=True)
        junk = mk(es, nc, "p8junk", [128, D], BF16, n=1)
        ysb = mk(es, nc, "p8y", [128, D], F32, n=4)
        stt_ = mk(es, nc, "p8st", [128, 4], F32, n=4)
        def load_at(i):
            a2 = atl.next()
            deps = [at_tl[i]] + ([at_tl[i + 1]] if i + 1 < NT else [])
            k.ld(a2[:], AT[:, :, i * TT:(i + 1) * TT], a2, r=deps)
            return a2

        for it in range(NT):
            tok0 = it * TT
            ys = []
            if it == 0:
                a_next = load_at(0)
            a_ = a_next
            xr = []
            for b in range(4):
                ys.append(ysb.next())
            for half in range(2):
                ac = [acc.next() for _ in range(4)]
                for j in range(44):
                    if j == 8 and half == 0 and it + 1 < NT:
                        a_next = load_at(it + 1)
                    if j == 8 and half == 1:
                        for b in range(4):
                            x_ = xres.next()
                            k.ld(x_[:], X2[tok0 + b * 128: tok0 + (b + 1) * 128, :], x_, r=[x2_tl[it]])
                            xr.append(x_)
                    wd = wdb.next()
                    k.ld(wd[:], Wdn[j * 128:(j + 1) * 128, half * 1024:(half + 1) * 1024], wd, r=[wt["Wdn"]])
                    for b in range(4):
                        for g in range(2):
                            k.mm(ac[b][:, g * 512:(g + 1) * 512], a_[:, j, b * 128:(b + 1) * 128], wd[:, g * 512:(g + 1) * 512], j == 0, j == 43,
                                 [a_.tl, wd.tl], [ac[b].tl])
                for b in range(4):
                    k.cp(ys[b][:, half * 1024:(half + 1) * 1024], ac[b][:], [ac[b].tl], [ys[b].tl], eng=("act" if b % 2 == 0 else "dve"))
            for b in range(4):
                rs = slice(tok0 + b * 128, tok0 + (b + 1) * 128)
                x_ = xr[b]
                st = stt_.next()
                jk = junk.next()
                k.act(jk[:], ys[b][:], AF.Square, [ys[b].tl], [jk.tl, st.tl], accum=st[:, 0:1])
                k.rstd_from_ss(st[:, 2:3], st[:, 0:1], st[:, 1:2], D, [st.tl], [st.tl])
                k.stt(ys[b][:], ys[b][:], st[:, 2:3], gp[:], ALU.mult, ALU.mult, [ys[b].tl, st.tl], [ys[b].tl])
                k.tt(ys[b][:], ys[b][:], x_[:], ALU.add, [ys[b].tl, x_.tl], [ys[b].tl], eng="pool")
                k.st(y_out[rs, :], ys[b][:], ys[b], w=[yout_tl])
        sc.barrier()
        sc.emit()
    return k


def ldw(k, wt, buf, W, name, r0, nkc, c0, ncols, col_off=0):
    src = W[r0:r0 + nkc * 128, c0:c0 + ncols].rearrange("(kc p) c -> p kc c", p=128)
    return k.ld(buf[:, 0:nkc, col_off:col_off + ncols], src, buf, r=[wt[name]])


def norm_T(k, blocks, xs_list, st_list, gc, gc_off, identb, pT, dstT):
    nb = len(blocks)
    for b, xbuf in enumerate(blocks):
        xs = xs_list[b]
        st = st_list[b]
        k.act(xs[:], xbuf[:], AF.Square, [xbuf.tl], [xs.tl, st.tl], accum=st[:, 0:1])
        k.rstd_from_ss(st[:, 2:3], st[:, 0:1], st[:, 1:2], D, [st.tl], [st.tl])
        k.ts(xs[:], xbuf[:], st[:, 2:3], None, ALU.mult, None, [xbuf.tl, st.tl], [xs.tl])
    for kc in range(KC):
        p_ = pT.next()
        for b in range(nb):
            k.tr(p_[:, b * 128:(b + 1) * 128], xs_list[b][:, kc * 128:(kc + 1) * 128], identb[:],
                 [xs_list[b].tl, identb.tl], [p_.tl])
        if kc % 2 == 0:
            k.act(dstT[:, kc, 0:nb * 128], p_[:, 0:nb * 128], AF.Copy, [p_.tl], [dstT.tl],
                  scale=gc[:, gc_off + kc:gc_off + kc + 1])
        else:
            k.ts(dstT[:, kc, 0:nb * 128], p_[:, 0:nb * 128], gc[:, gc_off + kc:gc_off + kc + 1], None,
                 ALU.mult, None, [p_.tl, gc.tl], [dstT.tl])


def _consts():
    a = np.arange(128)
    j = a[:, None]
    i = a[None, :]
    same = (j // 64) == (i // 64)
    c = np.float32(-1.0 / 16.0)
    Lf = lambda ii, jj: (((ii // 64) == (jj // 64)) & (jj <= ii)).astype(np.float32)
    Lb = lambda ii, jj: (((ii // 64) == (jj // 64)) & (jj >= ii)).astype(np.float32)
    reff = (i // 64) * 64 + 32
    refb = (i // 64) * 64 + 31
    out = np.zeros((128, 1408), np.float32)
    out[:, 0:128] = np.eye(128, dtype=np.float32)
    out[:, 128:256] = c * (Lf(i, j) - Lf(reff, j))
    out[:, 256:384] = c * Lf(i, j)
    out[:, 384:512] = c * (same & (j > i)).astype(np.float32)
    out[:, 512:640] = c * (Lb(i, j) - Lb(refb, j))
    out[:, 640:768] = c * Lb(i, j)
    out[:, 768:896] = c * (same & (j < i)).astype(np.float32)
    out[:, 896:1024] = (same & (j <= i)).astype(np.float32)
    out[:, 1024:1152] = (same & (j > i)).astype(np.float32)
    out[:, 1152:1280] = 1.0
    return out


def _col(v, n):
    return np.ascontiguousarray(np.asarray(v, np.float32).reshape(n, 128).T)


def make_in_maps(inp):
    f = lambda n: np.ascontiguousarray(np.asarray(inp[n], np.float32)[0])
    shared = {
        "w_in": f("w_in"), "w_q_up": f("mla_w_q_up"), "w_kv_up": f("mla_w_kv_up"), "w_out": f("w_out"),
        "w_mq": f("w_mem_q"), "w_mk": f("w_mem_k"), "w_mv": f("w_mem_v"), "w_mo": f("w_mem_o"),
        "w_up": f("w_ffn_up"), "w_dn": f("w_ffn_down"),
    }
    shared["consts"] = _consts()
    shared["gcols"] = np.ascontiguousarray(np.concatenate([
        _col(f("norm_mix_pre"), 16), _col(f("norm_mem_pre"), 16), _col(f("norm_ffn_pre"), 16),
        _col(f("mem_kv_norm"), 16), _col(f("mla_q_norm"), 4), _col(f("mla_kv_norm"), 4)], axis=1))
    gp = np.concatenate([f("norm_mix_post"), f("norm_mem_post"), f("norm_ffn_post"), f("gla_out_norm")])
    shared["gpost"] = np.ascontiguousarray(np.broadcast_to(gp[None, :], (128, gp.shape[0])))
    cw = f("ffn_conv_w")
    shared["convp"] = np.ascontiguousarray(np.concatenate(
        [_col(cw[0], 88), _col(cw[1], 88), _col(cw[2], 88), _col(f("ffn_conv_b"), 88)], axis=1))
    w2 = np.zeros((33, 1024), np.float32)
    w2[0:16, 0:512] = f("gla_gate_w2_fwd")
    w2[32, 0:512] = f("gla_gate_b_fwd")
    w2[16:32, 512:1024] = f("gla_gate_w2_bwd")
    w2[32, 512:1024] = f("gla_gate_b_bwd")
    shared["w2aug"] = w2
    half = 32
    freqs = (np.float32(10000.0) ** (-np.arange(half, dtype=np.float32) / np.float32(half))).astype(np.float32)
    ang = (np.arange(S, dtype=np.float32)[None, :] * freqs[:, None]).astype(np.float32)
    ang = np.concatenate([ang, ang], axis=0)
    shared["ropet"] = np.ascontiguousarray(np.concatenate([np.cos(ang), np.sin(ang)], axis=1).astype(np.float32))
    xs = [np.asarray(inp["x_prompt"], np.float32)[b] for b in range(2)] + \
         [np.asarray(inp["x_sample"], np.float32)[b] for b in range(4)]
    ms = [np.asarray(inp["mem_prompt"], np.float32)[b] for b in range(2)] + \
         [np.asarray(inp["mem_sample"], np.float32)[b] for b in range(4)]
    zeros = {n_: np.zeros_like(v) for n_, v in shared.items()}
    zx = np.zeros((S, D), np.float32)
    zm = np.zeros((256, D), np.float32)
    maps = []
    for c in range(8):
        if c in SPARE_CORES:
            m = dict(zeros)
            m["x"] = zx
            m["mem"] = zm
        else:
            s = SEQ_CORES.index(c)
            m = dict(shared)
            m["x"] = np.ascontiguousarray(xs[s])
            m["mem"] = np.ascontiguousarray(ms[s])
        maps.append(m)
    return maps


def kernel(**inputs):
    nc = bass.Bass("TRN2", target_bir_lowering=False)
    build(nc)
    maps = make_in_maps(inputs)
    res = run_bass_kernel_spmd(nc, maps, core_ids=list(range(8)))
    ys = [np.asarray(res.results[c]["y"], np.float32) for c in SEQ_CORES]
    return (np.stack(ys[0:2], axis=0), np.stack(ys[2:6], axis=0))
```

```python
from contextlib import ExitStack
import numpy as np
import concourse.bass as bass
import concourse.mybir as mybir
from concourse.bass_utils import run_bass_kernel_spmd

F32 = mybir.dt.float32
BF16 = mybir.dt.bfloat16
AF = mybir.ActivationFunctionType
ALU = mybir.AluOpType
AX = mybir.AxisListType

S = 4096
D = 2048
KC = 16
NT = 8
TT = 512
IN_W = 4192
DFF = 5632
EPS = 1e-6
SEQ_CORES = [0, 1, 2, 4, 5, 6]
SPARE_CORES = (3, 7)


class Tl:
    __slots__ = ("name", "lw", "rd", "dsem", "dcnt", "const", "persist")

    def __init__(self, name, const=False):
        self.persist = False
        self.name = name
        self.lw = None
        self.rd = {}
        self.dsem = None
        self.dcnt = 0
        self.const = const


class Op:
    __slots__ = ("eng", "fn", "deps", "eidx", "is_dma", "done", "signal", "sigidx", "waits")

    def __init__(self, eng, fn):
        self.eng = eng
        self.fn = fn
        self.deps = []
        self.is_dma = False
        self.done = None
        self.signal = False
        self.sigidx = 0
        self.waits = []


ENGS = ("pe", "act", "dve", "pool", "sp")


class Sched:
    def __init__(self, nc):
        self.nc = nc
        self.ops = {e: [] for e in ENGS}
        self.esem = {e: nc.alloc_semaphore(name=f"es_{e}") for e in ("pe", "act", "dve", "pool")}
        self.ecount = {e: 0 for e in ENGS}
        self.sigcount = {e: 0 for e in ENGS}
        self.waited = {e: {} for e in ENGS}
        self.waited_dma = {e: {} for e in ENGS}
        self.dma_sems = []
        self.last_op = {e: None for e in ENGS}
        self.nsem = 0
        self.free_sems = []

    def _track(self, o, r, w):
        deps = {}
        for t in r:
            if t.lw is not None:
                deps[id(t.lw)] = t.lw
        for t in w:
            if t.lw is not None:
                deps[id(t.lw)] = t.lw
            for x in t.rd.values():
                deps[id(x)] = x
        deps.pop(id(o), None)
        o.deps = list(deps.values())
        for t in r:
            if not t.const:
                key = ("dma", id(o)) if o.is_dma else o.eng
                t.rd[key] = o
        for t in w:
            t.lw = o
            t.rd = {}

    def op(self, eng, fn, r=(), w=()):
        o = Op(eng, fn)
        self.ecount[eng] += 1
        o.eidx = self.ecount[eng]
        self._track(o, r, w)
        self.ops[eng].append(o)
        self.last_op[eng] = o
        return o

    def dma(self, q, out, in_, key, r=(), w=(), **kw):
        def fn(e):
            return e.dma_start(out=out, in_=in_, **kw)
        o = Op(q, fn)
        o.is_dma = True
        if key.dsem is None:
            if self.free_sems:
                key.dsem, key.dcnt = self.free_sems.pop()
            else:
                key.dsem = self.nc.alloc_semaphore(name=f"ds{self.nsem}")
                self.nsem += 1
            self.dma_sems.append(key)
        key.dcnt += 16
        o.done = (key.dsem, key.dcnt)
        self.ecount[q] += 1
        o.eidx = self.ecount[q]
        self._track(o, r, w)
        self.ops[q].append(o)
        return o

    def barrier(self):
        lasts = [o for o in self.last_op.values() if o is not None]
        keys = [k_ for k_ in self.dma_sems if not k_.persist]
        for e in ENGS:
            def fn(eng):
                return None
            o = Op(e, fn)
            self.ecount[e] += 1
            o.eidx = self.ecount[e]
            o.deps = [x for x in lasts if x.eng != e]
            for k in keys:
                d = Op("sp", None)
                d.is_dma = True
                d.done = (k.dsem, k.dcnt)
                o.deps.append(d)
            self.ops[e].append(o)
        for k_ in keys:
            self.free_sems.append((k_.dsem, k_.dcnt))
        self.dma_sems = [k_ for k_ in self.dma_sems if k_.persist]

    def emit(self):
        nc = self.nc
        for e in ENGS:
            for o in self.ops[e]:
                for d in o.deps:
                    if d.is_dma:
                        sem, val = d.done
                        if self.waited_dma[e].get(sem.num, 0) >= val:
                            continue
                        self.waited_dma[e][sem.num] = val
                        o.waits.append(("d", sem, val))
                    else:
                        p = d.eng
                        if p == e and e == "pe":
                            continue
                        if self.waited[e].get(p, 0) >= d.eidx:
                            continue
                        self.waited[e][p] = d.eidx
                        d.signal = True
                        o.waits.append(("e", d))
        for e in ENGS:
            c = self.sigcount[e]
            for o in self.ops[e]:
                if o.signal:
                    c += 1
                    o.sigidx = c
            self.sigcount[e] = c
        engmap = {"pe": "tensor", "act": "scalar", "dve": "vector", "pool": "gpsimd", "sp": "sync"}
        with nc.Block() as block:
            for e in ENGS:
                ops = self.ops[e]

                def body(eng, ops=ops, e=e):
                    for o in ops:
                        for wt in o.waits:
                            if wt[0] == "d":
                                eng.wait_ge(wt[1], wt[2])
                            else:
                                d = wt[1]
                                eng.wait_ge(self.esem[d.eng], d.sigidx)
                        ins = o.fn(eng)
                        if o.is_dma:
                            ins.then_inc(o.done[0], 16)
                        elif o.signal:
                            assert ins is not None
                            ins.then_inc(self.esem[e], 1)
                getattr(block, engmap[e])(body)
        self.ops = {e: [] for e in ENGS}


class Rot:
    def __init__(self, tiles):
        self.tiles = tiles
        self.i = 0

    def next(self):
        t = self.tiles[self.i % len(self.tiles)]
        self.i += 1
        return t


class Buf:
    def __init__(self, es, nc, name, shape, dt, psum=False, const=False):
        if psum:
            self.t = es.enter_context(nc.psum_tensor(name, shape, dt))
        else:
            self.t = es.enter_context(nc.sbuf_tensor(name, shape, dt))
        self.tl = Tl(name, const=const)

    def __getitem__(self, idx):
        return self.t[idx]


class View:
    def __init__(self, base, idx, name):
        self.base = base
        self.idx = idx
        self.tl = Tl(name)

    def __getitem__(self, idx):
        return self.base.t[self.idx][idx]


def mk(es, nc, name, shape, dt, n=1, psum=False):
    return Rot([Buf(es, nc, f"{name}{i}", shape, dt, psum=psum) for i in range(n)])


class K:
    def __init__(self, nc, dbg=None, stop_after=99):
        self.nc = nc
        self.sc = Sched(nc)
        self.dbg = dbg or {}
        self.stop_after = stop_after
        self.dram = {}
        self.dtl = {}

    def din(self, name, shape, dt=F32):
        self.dram[name] = self.nc.dram_tensor(name, list(shape), dt, kind="ExternalInput").ap()
        self.dtl[name] = Tl(name, const=True)
        return self.dram[name]

    def dout(self, name, shape, dt=F32):
        self.dram[name] = self.nc.dram_tensor(name, list(shape), dt, kind="ExternalOutput").ap()
        return self.dram[name]

    def dscr(self, name, shape, dt=BF16):
        kind = "ExternalOutput" if name in self.dbg else "Internal"
        self.dram[name] = self.nc.dram_tensor(name, list(shape), dt, kind=kind).ap()
        return self.dram[name]

    def act(self, out, in_, func, r, w, bias=None, scale=None, accum=None):
        kw = {}
        if bias is not None:
            kw["bias"] = bias
        if scale is not None:
            kw["scale"] = scale
        if accum is not None:
            kw["accum_out"] = accum
        return self.sc.op("act", lambda e: e.activation(out=out, in_=in_, func=func, **kw), r, w)

    def tt(self, out, a, b, op, r, w, eng="dve"):
        return self.sc.op(eng, lambda e: e.tensor_tensor(out=out, in0=a, in1=b, op=op), r, w)

    def ts(self, out, a, s1, s2, op0, op1, r, w, eng="dve"):
        if op1 is None:
            return self.sc.op(eng, lambda e: e.tensor_scalar(out=out, in0=a, scalar1=s1, scalar2=None, op0=op0), r, w)
        return self.sc.op(eng, lambda e: e.tensor_scalar(out=out, in0=a, scalar1=s1, scalar2=s2, op0=op0, op1=op1), r, w)

    def stt(self, out, a, s, b, op0, op1, r, w):
        return self.sc.op("dve", lambda e: e.scalar_tensor_tensor(out=out, in0=a, scalar=s, in1=b, op0=op0, op1=op1), r, w)

    def cp(self, out, in_, r, w, eng="dve"):
        if eng == "act":
            return self.sc.op("act", lambda e: e.activation(out=out, in_=in_, func=AF.Copy), r, w)
        return self.sc.op(eng, lambda e: e.tensor_copy(out=out, in_=in_), r, w)

    def recip(self, out, in_, r, w):
        return self.sc.op("dve", lambda e: e.reciprocal(out=out, in_=in_), r, w)

    def mm(self, out, lhsT, rhs, start, stop, r, w):
        return self.sc.op("pe", lambda e: e.matmul(out, lhsT, rhs, start=start, stop=stop), r, w)

    def tr(self, out, in_, ident, r, w):
        return self.sc.op("pe", lambda e: e.transpose(out, in_, ident), r, w)

    def ld(self, out, in_, key, r=(), q="sp", **kw):
        return self.sc.dma(q, out, in_, key.tl, r=r, w=[key.tl], **kw)

    def st(self, out, in_, key, w=(), q="pool", **kw):
        return self.sc.dma(q, out, in_, key.tl, r=[key.tl], w=w, **kw)

    def memset(self, ap, val, w, eng="dve"):
        return self.sc.op(eng, lambda e: e.memset(ap, val), (), w)

    def rstd_from_ss(self, rstd, ss, tmp, n, r, w):
        self.ts(tmp, ss, 1.0 / n, EPS, ALU.mult, ALU.add, r, w)
        self.act(tmp, tmp, AF.Sqrt, w, w)
        self.recip(rstd, tmp, w, w)


def build(nc, dbg=None, stop_after=99):
    k = K(nc, dbg, stop_after)
    sc = k.sc
    x = k.din("x", [S, D])
    mem = k.din("mem", [256, D])
    w_in = k.din("w_in", [D, IN_W])
    w_q_up = k.din("w_q_up", [512, 1536])
    w_kv_up = k.din("w_kv_up", [512, 2048])
    w_out = k.din("w_out", [D, D])
    w_mq = k.din("w_mq", [D, D])
    w_mk = k.din("w_mk", [D, D])
    w_mv = k.din("w_mv", [D, D])
    w_mo = k.din("w_mo", [D, D])
    w_up = k.din("w_up", [D, 2 * DFF])
    w_dn = k.din("w_dn", [DFF, D])
    consts = k.din("consts", [128, 1408])
    gcols = k.din("gcols", [128, 72])
    gpost = k.din("gpost", [128, 3 * D + 256])
    convp = k.din("convp", [128, 4 * 88])
    w2aug = k.din("w2aug", [33, 1024])
    ropet = k.din("ropet", [64, 2 * S])
    y_out = k.dout("y", [S, D])

    Wi = k.dscr("Wi", [D, IN_W])
    Wq = k.dscr("Wq", [512, 1536])
    Wkv = k.dscr("Wkv", [512, 2048])
    Wo = k.dscr("Wo", [D, D])
    Wmq = k.dscr("Wmq", [D, D])
    Wmk = k.dscr("Wmk", [D, D])
    Wmv = k.dscr("Wmv", [D, D])
    Wmo = k.dscr("Wmo", [D, D])
    Wup = k.dscr("Wup", [D, 2 * DFF])
    Wdn = k.dscr("Wdn", [DFF, D])
    Wkrr = k.dscr("Wkrr", [D, 64])
    Wqr = k.dscr("Wqr", [512, 512])
    H1T = k.dscr("H1T", [128, KC, S])
    MKT = k.dscr("MKT", [128, KC, 256])
    MV = k.dscr("MV", [2, 128, D])
    SILU = k.dscr("SILU", [S, 1024], F32)
    QBB = k.dscr("QBB", [128, 4, S])
    DECB = k.dscr("DECB", [128, 4, 64], F32)
    KDB = k.dscr("KDB", [S, 512])
    VG = k.dscr("VG", [S, 1024])
    OP = k.dscr("OP", [S, 1024], F32)
    silu_tl = [Tl(f"SILU{t}") for t in range(NT)]
    qbb_tl = [Tl(f"QBB{t}") for t in range(NT)]
    decb_tl = [Tl(f"DECB{t}") for t in range(NT)]
    kdb_tl = [Tl(f"KDB{t}") for t in range(NT)]
    vg_tl = [Tl(f"VG{t}") for t in range(NT)]
    op_tl = [Tl(f"OP{t}") for t in range(NT)]
    KPE = k.dscr("KPE", [64, S])
    QTN = k.dscr("QTN", [8, 128, S])
    QTR = k.dscr("QTR", [8, 64, S])
    KTN = k.dscr("KTN", [8, 128, S])
    VS = k.dscr("VS", [8, 128, 32, 128])
    OGT = k.dscr("OGT", [128, 8, S])
    OMT = k.dscr("OMT", [128, 8, S])
    kpe_tl = [Tl(f"KPE{t}") for t in range(NT)]
    qtn_tl = [Tl(f"QTN{t}") for t in range(NT)]
    qtr_tl = [Tl(f"QTR{t}") for t in range(NT)]
    ktn_tl = [Tl(f"KTN{t}") for t in range(NT)]
    vs_tl = [Tl(f"VS{t}") for t in range(NT)]
    ogt_tl = [Tl(f"OGT{t}") for t in range(NT)]
    omt_tl = [Tl(f"OMT{t}") for t in range(NT)]
    X2 = k.dscr("X2", [S, D], F32)
    H3T = k.dscr("H3T", [128, KC, S])
    AT = k.dscr("AT", [128, 44, S])
    x2_tl = [Tl(f"X2{t}") for t in range(NT)]
    h3t_tl = [Tl(f"H3T{t}") for t in range(NT)]
    at_tl = [Tl(f"AT{t}") for t in range(NT)]
    yout_tl = Tl("yout")
    mkt_tl = Tl("MKT")
    mv_tl = Tl("MV")
    h1t_tl = [Tl(f"H1T{t}") for t in range(NT)]
    wt = {n: Tl(n, const=True) for n in ("Wi", "Wq", "Wkv", "Wo", "Wmq", "Wmk", "Wmv", "Wmo", "Wup", "Wdn", "Wkrr", "Wqr")}

    def cast_chunks(dst, src, name, rows, rchunk):
        tl = wt[name]
        key = Tl("k_" + name)
        key.persist = True
        opsl = []
        n = (rows + rchunk - 1) // rchunk

        def issue(r0):
            r1 = min(rows, r0 + rchunk)
            opsl.append(sc.dma("pool", dst[r0:r1, :], src[r0:r1, :], key, r=(), w=(), max_dma_last_dim=8192))
            if len(opsl) == n:
                for o in opsl:
                    o.done = (key.dsem, key.dcnt)
                tl.lw = opsl[-1]
        return [(lambda r0=r0: issue(r0)) for r0 in range(0, rows, rchunk)]

    def cast_w(dst, src, name, rows, cols, rchunk):
        for th in cast_chunks(dst, src, name, rows, rchunk):
            th()

    with ExitStack() as es:
        cast_w(Wmk, w_mk, "Wmk", D, D, 1024)
        cast_w(Wmv, w_mv, "Wmv", D, D, 1024)
        cast_w(Wi, w_in, "Wi", D, IN_W, 512)
        cast_w(Wq, w_q_up, "Wq", 512, 1536, 512)
        cast_w(Wkv, w_kv_up, "Wkv", 512, 2048, 512)


        cst = Buf(es, nc, "cst0", [128, 1408], F32)
        k.ld(cst[:], consts, cst)
        identb = Buf(es, nc, "identb0", [128, 128], BF16)
        k.cp(identb[:], cst[:, 0:128], [cst.tl], [identb.tl])
        gc = Buf(es, nc, "gc0", [128, 72], F32)
        k.ld(gc[:], gcols, gc)
        xs = mk(es, nc, "p1xs", [128, D], BF16, n=4)
        stt_ = mk(es, nc, "p1st", [128, 4], F32, n=4)
        pT = mk(es, nc, "p1pT", [128, 512], BF16, n=2, psum=True)

        xb = mk(es, nc, "p1xb", [128, D], F32, n=6)
        hT = mk(es, nc, "p1hT", [128, KC, TT], BF16, n=2)
        for t in range(NT):
            blocks = []
            for b in range(4):
                xbuf = xb.next()
                k.ld(xbuf[:], x[t * 512 + b * 128: t * 512 + (b + 1) * 128, :], xbuf)
                blocks.append(xbuf)
            h = hT.next()
            norm_T(k, blocks, xs.tiles, stt_.tiles, gc, 0, identb, pT, h)
            k.st(H1T[:, :, t * 512:(t + 1) * 512], h[:], h, w=[h1t_tl[t]], q="act")

        tkr = Buf(es, nc, "tkr", [128, KC, 64], F32)
        k.ld(tkr[:], w_in[:, 4128:4192].rearrange("(kc p) c -> p kc c", p=128), tkr)
        rkr = Buf(es, nc, "rkr", [128, KC, 64], BF16)
        k.ts(rkr[:, :, 0:32], tkr[:, :, 32:64], -1.0, None, ALU.mult, None, [tkr.tl], [rkr.tl])
        k.cp(rkr[:, :, 32:64], tkr[:, :, 0:32], [tkr.tl], [rkr.tl], eng="act")
        k.st(Wkrr.rearrange("(kc p) c -> p kc c", p=128), rkr[:], rkr, w=[wt["Wkrr"]], q="act")
        tq = Buf(es, nc, "tq", [128, 4, 8, 64], F32)
        wqv = w_q_up.rearrange("(kc p) (h c) -> p kc h c", p=128, c=192)
        for kc in range(4):
            k.ld(tq[:, kc, :, :], wqv[:, kc, :, 128:192], tq)
        rq = Buf(es, nc, "rq", [128, 4, 8, 64], BF16)
        for kc in range(4):
            k.ts(rq[:, kc, :, 0:32], tq[:, kc, :, 32:64], -1.0, None, ALU.mult, None, [tq.tl], [rq.tl])
            k.cp(rq[:, kc, :, 32:64], tq[:, kc, :, 0:32], [tq.tl], [rq.tl], eng="act")
        for kc in range(4):
            k.st(Wqr.rearrange("(kc p) (h c) -> p kc h c", p=128, c=64)[:, kc, :, :], rq[:, kc, :, :], rq, w=[wt["Wqr"]], q="act")

        mT = Buf(es, nc, "p0mT", [128, KC, 256], BF16)
        blocks = []
        for b in range(2):
            xbuf = xb.next()
            k.ld(xbuf[:], mem[b * 128:(b + 1) * 128, :], xbuf)
            blocks.append(xbuf)
        norm_T(k, blocks, xs.tiles, stt_.tiles, gc, 48, identb, pT, mT)
        wb = mk(es, nc, "p0wb", [128, KC, 512], BF16, n=2)
        ps = mk(es, nc, "p0ps", [128, 512], F32, n=3, psum=True)
        mkt = Buf(es, nc, "p0mkt", [128, KC, 256], BF16)
        mvs = Buf(es, nc, "p0mvs", [128, 2, D], BF16)
        for g in range(4):
            w_ = wb.next()
            ldw(k, wt, w_, Wmk, "Wmk", 0, KC, g * 512, 512)
            for j in range(4):
                p_ = ps.next()
                for kc in range(KC):
                    k.mm(p_[:, 0:256], w_[:, kc, j * 128:(j + 1) * 128], mT[:, kc, :], kc == 0, kc == KC - 1,
                         [w_.tl, mT.tl], [p_.tl])
                k.cp(mkt[:, g * 4 + j, :], p_[:, 0:256], [p_.tl], [mkt.tl], eng=("act" if j % 2 else "dve"))
        for g in range(4):
            w_ = wb.next()
            ldw(k, wt, w_, Wmv, "Wmv", 0, KC, g * 512, 512)
            for kb in range(2):
                p_ = ps.next()
                for kc in range(KC):
                    k.mm(p_[:], mT[:, kc, kb * 128:(kb + 1) * 128], w_[:, kc, :], kc == 0, kc == KC - 1,
                         [w_.tl, mT.tl], [p_.tl])
                k.cp(mvs[:, kb, g * 512:(g + 1) * 512], p_[:], [p_.tl], [mvs.tl], eng=("act" if kb % 2 else "dve"))
        k.st(MKT, mkt[:], mkt, w=[mkt_tl], q="act")
        k.st(MV.rearrange("kb p d -> p kb d"), mvs[:], mvs, w=[mv_tl], q="act")
        sc.barrier()
        sc.emit()
    if stop_after <= 1:
        return k

    with ExitStack() as es:
        cst = Buf(es, nc, "cst2", [128, 1408], F32)
        k.ld(cst[:], consts, cst)
        cst.tl.const = True
        C1 = (cst[:, 128:256], cst[:, 512:640])
        C2 = (cst[:, 256:384], cst[:, 640:768])
        C3 = (cst[:, 384:512], cst[:, 768:896])
        MSK = (cst[:, 896:1024], cst[:, 1024:1152])
        w2f = Buf(es, nc, "w2f", [33, 1024], F32)
        k.ld(w2f[:], w2aug, w2f)
        w2b = Buf(es, nc, "w2b", [33, 1024], BF16)
        k.cp(w2b[:], w2f[:], [w2f.tl], [w2b.tl])
        w2b.tl.const = True
        wgg = Buf(es, nc, "wgg", [128, KC, 32], BF16)
        ldw(k, wt, wgg, Wi, "Wi", 0, KC, 3072, 32)
        wgg.tl.const = True
        ggaug = Buf(es, nc, "ggaug", [33, TT], BF16)
        k.memset(ggaug[32:33, :], 1.0, [ggaug.tl])
        hT = mk(es, nc, "p2hT", [128, KC, TT], BF16, n=1)
        wb = mk(es, nc, "p2wb", [128, KC, 512], BF16, n=2)
        ps = mk(es, nc, "p2ps", [128, 512], F32, n=3, psum=True)
        ops_ = mk(es, nc, "p2o", [128, 1024], F32, n=1, psum=True)
        kvps = mk(es, nc, "p2kv", [128, 1024], F32, n=1, psum=True)
        qT = Buf(es, nc, "qT", [128, 4, TT], F32)
        kT = Buf(es, nc, "kT", [128, 4, TT], F32)
        kTM = Buf(es, nc, "kTM", [128, 4, 512], F32)
        sp = [[Buf(es, nc, f"sp{d_}{b}", [128, 512], F32) for b in range(4)] for d_ in range(2)]
        etmp = mk(es, nc, "etmp", [128, 512], F32, n=2)
        Eb = mk(es, nc, "Eb", [128, 512], F32, n=6)
        qi = [Buf(es, nc, f"qi{d_}", [128, 4, TT], BF16) for d_ in range(2)]
        ki = [Buf(es, nc, f"ki{d_}", [128, 4, TT], BF16) for d_ in range(2)]
        qb = [Buf(es, nc, f"qb{d_}", [128, 4, TT], BF16) for d_ in range(2)]
        kdec = [[Buf(es, nc, f"kdec{d_}{b}", [128, 512], BF16) for b in range(4)] for d_ in range(2)]
        vv = [Buf(es, nc, f"vv{b}", [128, 1024], BF16) for b in range(4)]
        decf = Buf(es, nc, "decf", [128, 4, 8], F32)
        decb = Buf(es, nc, "decb", [128, 4, 8], F32)
        sstage = mk(es, nc, "sstage", [128, 512], F32, n=2)
        ostage = mk(es, nc, "ostage", [128, 1024], F32, n=2)
        attsb = mk(es, nc, "attsb", [128, 128], BF16, n=4)
        Sf = Buf(es, nc, "Sf", [128, 1024], F32)
        Sfb = Buf(es, nc, "Sfb", [128, 1024], BF16)
        k.memset(Sf[:], 0.0, [Sf.tl])
        k.memset(Sfb[:], 0.0, [Sfb.tl])
        for t in range(NT):
            tok0 = t * TT
            h = hT.next()
            k.ld(h[:], H1T[:, :, tok0:tok0 + TT], h, r=[h1t_tl[t]])
            p_ = ps.next()
            for kc in range(KC):
                k.mm(p_[0:32, :], wgg[:, kc, :], h[:, kc, :], kc == 0, kc == KC - 1, [wgg.tl, h.tl], [p_.tl])
            k.cp(ggaug[0:32, :], p_[0:32, :], [p_.tl], [ggaug.tl], eng="act")
            for d_ in range(2):
                for b in range(4):
                    p_ = ps.next()
                    k.mm(p_[:], ggaug[0:33, b * 128:(b + 1) * 128], w2b[0:33, d_ * 512:(d_ + 1) * 512], True, True,
                         [ggaug.tl, w2b.tl], [p_.tl])
                    e_ = etmp.next()
                    k.act(e_[:], p_[:], AF.Exp, [p_.tl], [e_.tl], scale=-1.0)
                    k.act(sp[d_][b][:], e_[:], AF.Ln, [e_.tl], [sp[d_][b].tl], bias=1.0)
            w_ = wb.next()
            ldw(k, wt, w_, Wi, "Wi", 0, KC, 0, 512)
            for hh in range(4):
                p_ = ps.next()
                for kc in range(KC):
                    k.mm(p_[:], w_[:, kc, hh * 128:(hh + 1) * 128], h[:, kc, :], kc == 0, kc == KC - 1, [w_.tl, h.tl], [p_.tl])
                k.act(qT[:, hh, :], p_[:], AF.Copy, [p_.tl], [qT.tl], scale=float(128 ** -0.5))
            w_ = wb.next()
            ldw(k, wt, w_, Wi, "Wi", 0, KC, 512, 512)
            for hh in range(4):
                p_ = ps.next()
                for kc in range(KC):
                    k.mm(p_[:], w_[:, kc, hh * 128:(hh + 1) * 128], h[:, kc, :], kc == 0, kc == KC - 1, [w_.tl, h.tl], [p_.tl])
                k.cp(kT[:, hh, :], p_[:], [p_.tl], [kT.tl])
            for b in range(4):
                p_ = ps.next()
                for kc in range(KC):
                    k.mm(p_[:], h[:, kc, b * 128:(b + 1) * 128], w_[:, kc, :], kc == 0, kc == KC - 1, [w_.tl, h.tl], [p_.tl])
                k.cp(kTM[:, b, :], p_[:], [p_.tl], [kTM.tl], eng="act")
            for g in range(2):
                w_ = wb.next()
                ldw(k, wt, w_, Wi, "Wi", 0, KC, 1024 + g * 512, 512)
                for b in range(4):
                    p_ = ps.next()
                    for kc in range(KC):
                        k.mm(p_[:], h[:, kc, b * 128:(b + 1) * 128], w_[:, kc, :], kc == 0, kc == KC - 1, [w_.tl, h.tl], [p_.tl])
                    k.cp(vv[b][:, g * 512:(g + 1) * 512], p_[:], [p_.tl], [vv[b].tl], eng=("act" if b % 2 else "dve"))
            gr_w = []
            for g in range(2):
                w_ = wb.next()
                ldw(k, wt, w_, Wi, "Wi", 0, KC, 2048 + g * 512, 512)
                gr_w.append(w_)

            def gr_batch(g, b, h=h, tok0=tok0, t=t, gr_w=gr_w):
                w_ = gr_w[g]
                p_ = ps.next()
                for kc in range(KC):
                    k.mm(p_[:], h[:, kc, b * 128:(b + 1) * 128], w_[:, kc, :], kc == 0, kc == KC - 1, [w_.tl, h.tl], [p_.tl])
                s_ = sstage.next()
                k.act(s_[:], p_[:], AF.Silu, [p_.tl], [s_.tl])
                k.st(SILU[tok0 + b * 128: tok0 + (b + 1) * 128, g * 512:(g + 1) * 512], s_[:], s_, w=[silu_tl[t]])
            gr_todo = [(g, b) for g in range(2) for b in range(4)]
            for d_ in range(2):
                for hh in range(4):
                    p1 = ps.next()
                    for b in range(4):
                        k.mm(p1[:, b * 128:(b + 1) * 128], sp[d_][b][:, hh * 128:(hh + 1) * 128], C1[d_], True, True,
                             [sp[d_][b].tl], [p1.tl])
                    e1 = Eb.next()
                    k.act(e1[:], p1[:], AF.Exp, [p1.tl], [e1.tl])
                    e1i = Eb.next()
                    k.act(e1i[:], p1[:], AF.Exp, [p1.tl], [e1i.tl], scale=-1.0)
                    p2_ = ps.next()
                    for b in range(4):
                        k.mm(p2_[:, b * 128:(b + 1) * 128], sp[d_][b][:, hh * 128:(hh + 1) * 128], C2[d_], True, True,
                             [sp[d_][b].tl], [p2_.tl])
                    e2 = Eb.next()
                    k.act(e2[:], p2_[:], AF.Exp, [p2_.tl], [e2.tl])
                    k.tt(qi[d_][:, hh, :], qT[:, hh, :], e1[:], ALU.mult, [qT.tl, e1.tl], [qi[d_].tl])
                    k.tt(ki[d_][:, hh, :], kT[:, hh, :], e1i[:], ALU.mult, [kT.tl, e1i.tl], [ki[d_].tl])
                    k.tt(qb[d_][:, hh, :], qT[:, hh, :], e2[:], ALU.mult, [qT.tl, e2.tl], [qb[d_].tl])
                    if d_ == 0:
                        k.cp(decf[:, hh, :], e2[:, 63:512:64], [e2.tl], [decf.tl])
                    else:
                        k.cp(decb[:, hh, :], e2[:, 0:512:64], [e2.tl], [decb.tl])
                for b in range(4):
                    p3 = ps.next()
                    k.mm(p3[:], C3[d_], sp[d_][b][:], True, True, [sp[d_][b].tl], [p3.tl])
                    e3 = Eb.next()
                    k.act(e3[:], p3[:], AF.Exp, [p3.tl], [e3.tl])
                    k.tt(kdec[d_][b][:], kTM[:, b, :], e3[:], ALU.mult, [kTM.tl, e3.tl], [kdec[d_][b].tl])
            k.st(QBB[:, :, tok0:tok0 + TT], qb[1][:], qb[1], w=[qbb_tl[t]])
            k.st(DECB[:, :, t * 8:(t + 1) * 8], decb[:], decb, w=[decb_tl[t]])
            for b in range(4):
                k.st(KDB[tok0 + b * 128: tok0 + (b + 1) * 128, :], kdec[1][b][:], kdec[1][b], w=[kdb_tl[t]])
                k.st(VG[tok0 + b * 128: tok0 + (b + 1) * 128, :], vv[b][:], vv[b], w=[vg_tl[t]])
            for b in range(4):
                o_ = ops_.next()
                blk = slice(b * 128, (b + 1) * 128)
                for hh in range(4):
                    hs = slice(hh * 256, (hh + 1) * 256)
                    for d_ in range(2):
                        a_ = ps.next()
                        k.mm(a_[:, 0:128], ki[d_][:, hh, blk], qi[d_][:, hh, blk], True, True, [ki[d_].tl, qi[d_].tl], [a_.tl])
                        as_ = attsb.next()
                        k.tt(as_[:], a_[:, 0:128], MSK[d_], ALU.mult, [a_.tl], [as_.tl])
                        k.mm(o_[:, hs], as_[:], vv[b][:, hs], d_ == 0 and hh % 2 == 0, False, [as_.tl, vv[b].tl], [o_.tl])
                for c in range(2):
                    rows = slice(c * 64, (c + 1) * 64)
                    cs = slice(b * 128 + c * 64, b * 128 + (c + 1) * 64)
                    for hh in range(4):
                        hs = slice(hh * 256, (hh + 1) * 256)
                        k.mm(o_[rows, hs], qb[0][:, hh, cs], Sfb[:, hs], False, True, [qb[0].tl, Sfb.tl], [o_.tl])
                    kv_ = kvps.next()
                    for hh in range(4):
                        hs = slice(hh * 256, (hh + 1) * 256)
                        k.mm(kv_[:, hs], kdec[0][b][rows, hh * 128:(hh + 1) * 128], vv[b][rows, hs], True, True,
                             [kdec[0][b].tl, vv[b].tl], [kv_.tl])
                    for hh in range(4):
                        hs = slice(hh * 256, (hh + 1) * 256)
                        k.stt(Sf[:, hs], Sf[:, hs], decf[:, hh, b * 2 + c:b * 2 + c + 1], kv_[:, hs], ALU.mult, ALU.add,
                              [Sf.tl, decf.tl, kv_.tl], [Sf.tl])
                    k.cp(Sfb[:], Sf[:], [Sf.tl], [Sfb.tl], eng="act")
                    if gr_todo:
                        gr_batch(*gr_todo.pop(0))
                og = ostage.next()
                k.cp(og[:], o_[:], [o_.tl], [og.tl], eng="act")
                k.st(OP[tok0 + b * 128: tok0 + (b + 1) * 128, :], og[:], og, w=[op_tl[t]])
        sc.barrier()
        sc.emit()
    if stop_after <= 2:
        return k

    with ExitStack() as es:
        late = cast_chunks(Wo, w_out, "Wo", D, 512) + cast_chunks(Wmq, w_mq, "Wmq", D, 512) + cast_chunks(Wmo, w_mo, "Wmo", D, 512)
        cst = Buf(es, nc, "cst3", [128, 1408], F32)
        k.ld(cst[:], consts, cst)
        cst.tl.const = True
        ones32 = cst[:, 1152:1280]
        gc = Buf(es, nc, "gc3", [128, 72], F32)
        k.ld(gc[:], gcols, gc)
        gc.tl.const = True
        wmla = Buf(es, nc, "wmla", [128, KC, 1152], BF16)
        ldw(k, wt, wmla, Wi, "Wi", 0, KC, 3104, 1088)
        ldw(k, wt, wmla, Wkrr, "Wkrr", 0, KC, 0, 64, col_off=1088)
        wmla.tl.const = True
        WqS = Buf(es, nc, "WqS", [128, 4, 8, 192], BF16)
        WqrS = Buf(es, nc, "WqrS", [128, 4, 8, 64], BF16)
        WkvS = Buf(es, nc, "WkvS", [128, 4, 8, 256], BF16)
        for c in range(4):
            k.ld(WqS[:, c, :, :], Wq[c * 128:(c + 1) * 128, :].rearrange("p (h c) -> p h c", c=192), WqS, r=[wt["Wq"]])
            k.ld(WqrS[:, c, :, :], Wqr[c * 128:(c + 1) * 128, :].rearrange("p (h c) -> p h c", c=64), WqrS, r=[wt["Wqr"]])
            k.ld(WkvS[:, c, :, :], Wkv[c * 128:(c + 1) * 128, :].rearrange("p (h c) -> p h c", c=256), WkvS, r=[wt["Wkv"]])
        for b_ in (WqS, WqrS, WkvS):
            b_.tl.const = True
        hT = mk(es, nc, "p3hT", [128, KC, TT], BF16, n=2)
        ps = mk(es, nc, "p3ps", [128, 512], F32, n=6, psum=True)
        cqb = Buf(es, nc, "cqb", [128, 4, TT], BF16)
        ckvb = Buf(es, nc, "ckvb", [128, 4, TT], BF16)
        sq = Buf(es, nc, "sq", [128, 4, TT], F32)
        rqbc = Buf(es, nc, "rqbc", [128, TT], F32)
        rkbc = Buf(es, nc, "rkbc", [128, TT], F32)
        rtm = Buf(es, nc, "rtm", [128, 16], F32)
        cs_ = mk(es, nc, "cossin", [64, 2, TT], F32, n=2)
        t1 = mk(es, nc, "ropet1", [64, TT], F32, n=2)
        t2 = mk(es, nc, "ropet2", [64, TT], F32, n=2)
        kpe_st = mk(es, nc, "kpe_st", [64, TT], BF16, n=2)
        qn_st = mk(es, nc, "qn_st", [128, 8, TT], BF16, n=2)
        qr_st = mk(es, nc, "qr_st", [64, 8, TT], BF16, n=2)
        kn_st = mk(es, nc, "kn_st", [128, 8, TT], BF16, n=2)
        v_st = mk(es, nc, "v_st", [128, 8, 128], BF16, n=4)
        QSC = float(192 ** -0.5)

        def rms_fm(src_off, gcol_off, dst_bf, rbc, h):
            for c in range(4):
                p_ = ps.next()
                for kc in range(KC):
                    k.mm(p_[:], wmla[:, kc, src_off + c * 128: src_off + (c + 1) * 128], h[:, kc, :], kc == 0, kc == KC - 1,
                         [wmla.tl, h.tl], [p_.tl])
                k.act(dst_bf[:, c, :], p_[:], AF.Copy, [p_.tl], [dst_bf.tl], scale=gc[:, gcol_off + c:gcol_off + c + 1])
                k.act(sq[:, c, :], p_[:], AF.Square, [p_.tl], [sq.tl])
            pss = ps.next()
            for c in range(4):
                k.mm(pss[:], ones32, sq[:, c, :], c == 0, c == 3, [sq.tl], [pss.tl])
            k.ts(rbc[:], pss[:], 1.0 / 512, EPS, ALU.mult, ALU.add, [pss.tl], [rbc.tl])
            k.act(rbc[:], rbc[:], AF.Sqrt, [rbc.tl], [rbc.tl])
            k.recip(rbc[:], rbc[:], [rbc.tl], [rbc.tl])

        for t in range(NT):
            tok0 = t * TT
            h = hT.next()
            k.ld(h[:], H1T[:, :, tok0:tok0 + TT], h, r=[h1t_tl[t]])
            cs = cs_.next()
            k.ld(cs[:, 0, :], ropet[:, tok0:tok0 + TT], cs)
            k.ld(cs[:, 1, :], ropet[:, S + tok0:S + tok0 + TT], cs)
            for _ in range(2):
                if late:
                    late.pop(0)()
            rms_fm(0, 64, cqb, rqbc, h)
            rms_fm(512, 68, ckvb, rkbc, h)
            for b in range(4):
                p_ = ps.next()
                for c in range(4):
                    k.mm(p_[:, 0:2], sq[:, c, b * 128:(b + 1) * 128], ones32[:, 0:2], c == 0, c == 3, [sq.tl], [p_.tl])
                k.ts(rtm[:, 4 * b + 1:4 * b + 2], p_[:, 0:1], 1.0 / 512, EPS, ALU.mult, ALU.add, [p_.tl], [rtm.tl])
            k.act(rtm[:], rtm[:], AF.Sqrt, [rtm.tl], [rtm.tl])
            k.recip(rtm[:], rtm[:], [rtm.tl], [rtm.tl])
            pr = ps.next()
            for kc in range(KC):
                k.mm(pr[0:64, :], wmla[:, kc, 1024:1088], h[:, kc, :], kc == 0, kc == KC - 1, [wmla.tl, h.tl], [pr.tl])
            pq = ps.next()
            for kc in range(KC):
                k.mm(pq[0:64, :], wmla[:, kc, 1088:1152], h[:, kc, :], kc == 0, kc == KC - 1, [wmla.tl, h.tl], [pq.tl])
            a1 = t1.next()
            a2 = t2.next()
            k.tt(a1[:], pr[0:64, :], cs[:, 0, :], ALU.mult, [pr.tl, cs.tl], [a1.tl])
            k.tt(a2[:], pq[0:64, :], cs[:, 1, :], ALU.mult, [pq.tl, cs.tl], [a2.tl])
            kp = kpe_st.next()
            k.tt(kp[:], a1[:], a2[:], ALU.add, [a1.tl, a2.tl], [kp.tl])
            k.st(KPE[0:64, tok0:tok0 + TT], kp[:], kp, w=[kpe_tl[t]])
            qn = qn_st.next()
            qr = qr_st.next()
            for hh in range(8):
                pn = ps.next()
                for c in range(4):
                    k.mm(pn[:], WqS[:, c, hh, 0:128], cqb[:, c, :], c == 0, c == 3, [WqS.tl, cqb.tl], [pn.tl])
                k.stt(qn[:, hh, :], pn[:], QSC, rqbc[:], ALU.mult, ALU.mult, [pn.tl, rqbc.tl], [qn.tl])
                pr = ps.next()
                for c in range(4):
                    k.mm(pr[0:64, :], WqS[:, c, hh, 128:192], cqb[:, c, :], c == 0, c == 3, [WqS.tl, cqb.tl], [pr.tl])
                pq = ps.next()
                for c in range(4):
                    k.mm(pq[0:64, :], WqrS[:, c, hh, :], cqb[:, c, :], c == 0, c == 3, [WqrS.tl, cqb.tl], [pq.tl])
                a1 = t1.next()
                a2 = t2.next()
                k.tt(a1[:], pr[0:64, :], cs[:, 0, :], ALU.mult, [pr.tl, cs.tl], [a1.tl])
                k.tt(a2[:], pq[0:64, :], cs[:, 1, :], ALU.mult, [pq.tl, cs.tl], [a2.tl])
                k.tt(a1[:], a1[:], a2[:], ALU.add, [a1.tl, a2.tl], [a1.tl], eng="pool")
                k.stt(qr[:, hh, :], a1[:], QSC, rqbc[0:64, :], ALU.mult, ALU.mult, [a1.tl, rqbc.tl], [qr.tl])
            k.st(QTN[:, :, tok0:tok0 + TT].rearrange("h p t -> p h t"), qn[:], qn, w=[qtn_tl[t]])
            k.st(QTR[:, :, tok0:tok0 + TT].rearrange("h p t -> p h t"), qr[:], qr, w=[qtr_tl[t]])
            kn = kn_st.next()
            for hh in range(8):
                pk = ps.next()
                for c in range(4):
                    k.mm(pk[:], WkvS[:, c, hh, 0:128], ckvb[:, c, :], c == 0, c == 3, [WkvS.tl, ckvb.tl], [pk.tl])
                k.tt(kn[:, hh, :], pk[:], rkbc[:], ALU.mult, [pk.tl, rkbc.tl], [kn.tl])
            k.st(KTN[:, :, tok0:tok0 + TT].rearrange("h p t -> p h t"), kn[:], kn, w=[ktn_tl[t]])
            for b in range(4):
                vs = v_st.next()
                for half in range(2):
                    pv = ps.next()
                    for c in range(4):
                        k.mm(pv[:], ckvb[:, c, b * 128:(b + 1) * 128], WkvS[:, c, half * 4:(half + 1) * 4, 128:256],
                             c == 0, c == 3, [WkvS.tl, ckvb.tl], [pv.tl])
                    k.act(vs[:, half * 4:(half + 1) * 4, :], pv[:].rearrange("p (h d) -> p h d", d=128), AF.Copy, [pv.tl, rtm.tl], [vs.tl],
                          scale=rtm[:, 4 * b + 1:4 * b + 2])
                k.st(VS[:, :, 4 * t + b, :].rearrange("h p d -> p h d"), vs[:], vs, w=[vs_tl[t]])
        while late:
            late.pop(0)()
        sc.barrier()
        sc.emit()
    if stop_after <= 3:
        return k

    with ExitStack() as es:
        late = cast_chunks(Wup, w_up, "Wup", D, 128) + cast_chunks(Wdn, w_dn, "Wdn", DFF, 352)
        cst = Buf(es, nc, "cst5", [128, 1408], F32)
        k.ld(cst[:], consts, cst)
        onesb = Buf(es, nc, "onesb5", [128, 128], BF16)
        k.cp(onesb[:], cst[:, 1152:1280], [cst.tl], [onesb.tl])
        onesb.tl.const = True
        identb = Buf(es, nc, "identb4", [128, 128], BF16)
        k.cp(identb[:], cst[:, 0:128], [cst.tl], [identb.tl])
        identb.tl.const = True
        gout = Buf(es, nc, "gout", [128, 256], F32)
        k.ld(gout[:], gpost[:, 3 * D:3 * D + 256], gout)
        gout.tl.const = True
        kpe = Buf(es, nc, "kpe5", [128, S], BF16)
        k.memset(kpe[64:128, :], 0.0, [kpe.tl])
        k.ld(kpe[0:64, :], KPE[0:64, :], kpe, r=kpe_tl)
        kpe.tl.const = True
        ktn = mk(es, nc, "ktn5", [128, S], BF16, n=2)
        vsb = mk(es, nc, "vs5", [128, 32, 128], BF16, n=2)
        qnb = mk(es, nc, "qn5", [128, TT], BF16, n=4)
        qrb = mk(es, nc, "qr5", [128, TT], BF16, n=4)
        for q_0 in qrb.tiles:
            k.memset(q_0[64:128, :], 0.0, [q_0.tl])
        ps = mk(es, nc, "p5ps", [128, 512], F32, n=3, psum=True)
        oTp = mk(es, nc, "p5oT", [128, 512], F32, n=1, psum=True)
        dnp = mk(es, nc, "p5dn", [128, 512], F32, n=1, psum=True)
        ptb = mk(es, nc, "p5pt", [128, 512], BF16, n=6)
        rden = mk(es, nc, "p5rd", [128, 512], F32, n=2)
        omb = mk(es, nc, "p5om", [128, 512], BF16, n=3)
        qbb = mk(es, nc, "p4qbb", [128, 4, TT], BF16, n=2)
        kdb = mk(es, nc, "p4kdb", [128, 512], BF16, n=8)
        vvb = mk(es, nc, "p4vv", [128, 1024], BF16, n=8)
        opb = mk(es, nc, "p4op", [128, 1024], F32, n=6)
        silb = mk(es, nc, "p4sil", [128, 1024], F32, n=6)
        dcb = mk(es, nc, "p4dec", [128, 4, 8], F32, n=2)
        oi_ps = mk(es, nc, "p4oi", [128, 512], F32, n=1, psum=True)
        kv_ps = mk(es, nc, "p4kv", [128, 512], F32, n=1, psum=True)
        pT = mk(es, nc, "p4pT", [128, 1024], BF16, n=1, psum=True)
        Sb = Buf(es, nc, "Sb", [128, 1024], F32)
        Sbb = Buf(es, nc, "Sbb", [128, 1024], BF16)
        k.memset(Sb[:], 0.0, [Sb.tl])
        k.memset(Sbb[:], 0.0, [Sbb.tl])
        junk = Buf(es, nc, "p4junk", [128, 256], BF16)
        ssb = mk(es, nc, "p4ss", [128, 4], F32, n=2)
        tmpn = mk(es, nc, "p4tmpn", [128, 1024], F32, n=2)
        ogb = mk(es, nc, "p4og", [128, 1024], BF16, n=2)
        ogT = mk(es, nc, "p4ogT", [128, 8, TT], BF16, n=2)

        def gla_bwd():
            for t in reversed(range(NT)):
                tok0 = t * TT
                q_ = qbb.next()
                k.ld(q_[:], QBB[:, :, tok0:tok0 + TT], q_, r=[qbb_tl[t]])
                dc = dcb.next()
                k.ld(dc[:], DECB[:, :, t * 8:(t + 1) * 8], dc, r=[decb_tl[t]])
                kd, vb, ob, sb = {}, {}, {}, {}
                for b in reversed(range(4)):
                    rs = slice(tok0 + b * 128, tok0 + (b + 1) * 128)
                    kd[b] = kdb.next()
                    k.ld(kd[b][:], KDB[rs, :], kd[b], r=[kdb_tl[t]])
                    vb[b] = vvb.next()
                    k.ld(vb[b][:], VG[rs, :], vb[b], r=[vg_tl[t]])
                    ob[b] = opb.next()
                    k.ld(ob[b][:], OP[rs, :], ob[b], r=[op_tl[t]])
                    sb[b] = silb.next()
                    k.ld(sb[b][:], SILU[rs, :], sb[b], r=[silu_tl[t]])
                yield
                oT_ = ogT.next()
                for b in reversed(range(4)):
                    o_ = ob[b]
                    for hp in range(2):
                        hps = slice(hp * 512, (hp + 1) * 512)
                        oi = oi_ps.next()
                        for c in (1, 0):
                            rows = slice(c * 64, (c + 1) * 64)
                            cs = slice(b * 128 + c * 64, b * 128 + (c + 1) * 64)
                            for hl in range(2):
                                hh = hp * 2 + hl
                                k.mm(oi[rows, hl * 256:(hl + 1) * 256], q_[:, hh, cs], Sbb[:, hh * 256:(hh + 1) * 256], True, True,
                                     [q_.tl, Sbb.tl], [oi.tl])
                            kv_ = kv_ps.next()
                            for hl in range(2):
                                hh = hp * 2 + hl
                                k.mm(kv_[:, hl * 256:(hl + 1) * 256], kd[b][rows, hh * 128:(hh + 1) * 128],
                                     vb[b][rows, hh * 256:(hh + 1) * 256], True, True, [kd[b].tl, vb[b].tl], [kv_.tl])
                            yield
                            for hl in range(2):
                                hh = hp * 2 + hl
                                hs = slice(hh * 256, (hh + 1) * 256)
                                k.stt(Sb[:, hs], Sb[:, hs], dc[:, hh, b * 2 + c:b * 2 + c + 1], kv_[:, hl * 256:(hl + 1) * 256],
                                      ALU.mult, ALU.add, [Sb.tl, dc.tl, kv_.tl], [Sb.tl])
                            yield
                            k.cp(Sbb[:, hps], Sb[:, hps], [Sb.tl], [Sbb.tl], eng="act")
                            yield
                        k.tt(o_[:, hps], oi[:], o_[:, hps], ALU.add, [oi.tl, o_.tl], [o_.tl])
                        yield
                    ss = ssb.next()
                    for hh in range(4):
                        hs = slice(hh * 256, (hh + 1) * 256)
                        k.act(junk[:], o_[:, hs], AF.Square, [o_.tl], [junk.tl, ss.tl], accum=ss[:, hh:hh + 1])
                    yield
                    k.ts(ss[:], ss[:], 1.0 / 256, EPS, ALU.mult, ALU.add, [ss.tl], [ss.tl])
                    yield
                    k.act(ss[:], ss[:], AF.Sqrt, [ss.tl], [ss.tl])
                    yield
                    k.recip(ss[:], ss[:], [ss.tl], [ss.tl])
                    tn = tmpn.next()
                    for hh in range(4):
                        hs = slice(hh * 256, (hh + 1) * 256)
                        k.stt(tn[:, hs], o_[:, hs], ss[:, hh:hh + 1], gout[:], ALU.mult, ALU.mult, [o_.tl, ss.tl], [tn.tl])
                    yield
                    og = ogb.next()
                    k.tt(og[:], tn[:], sb[b][:], ALU.mult, [tn.tl, sb[b].tl], [og.tl], eng="pool")
                    yield
                    yield
                    p_ = pT.next()
                    for c8 in range(8):
                        k.tr(p_[:, c8 * 128:(c8 + 1) * 128], og[:, c8 * 128:(c8 + 1) * 128], identb[:], [og.tl], [p_.tl])
                    yield
                    k.cp(oT_[:, :, b * 128:(b + 1) * 128], p_[:].rearrange("p (c t) -> p c t", t=128), [p_.tl], [oT_.tl], eng="act")
                    yield
                k.st(OGT[:, :, tok0:tok0 + TT], oT_[:], oT_, w=[ogt_tl[t]])

        gen = gla_bwd()
        gen_done = [False]

        def step_gen():
            if not gen_done[0]:
                try:
                    next(gen)
                except StopIteration:
                    gen_done[0] = True

        it_no = 0
        for hh in range(8):
            kt_ = ktn.next()
            k.ld(kt_[:], KTN[hh, :, :], kt_, r=ktn_tl)
            vs_ = vsb.next()
            k.ld(vs_[:], VS[hh, :, :, :], vs_, r=vs_tl)
            for t in range(NT):
                tok0 = t * TT
                if late and (hh * NT + t) % 2 == 0:
                    late.pop(0)()
                qn = qnb.next()
                k.ld(qn[:], QTN[hh, :, tok0:tok0 + TT], qn, r=[qtn_tl[t]])
                qr = qrb.next()
                k.ld(qr[0:64, :], QTR[hh, :, tok0:tok0 + TT], qr, r=[qtr_tl[t]])
                oT = oTp.next()
                dn = dnp.next()
                pend = []

                def score(kb):
                    ks = slice(kb * 128, (kb + 1) * 128)
                    sT = ps.next()
                    k.mm(sT[:], kt_[:, ks], qn[:], True, False, [kt_.tl, qn.tl], [sT.tl])
                    k.mm(sT[:], kpe[:, ks], qr[:, :], False, True, [kpe.tl, qr.tl], [sT.tl])
                    pt = ptb.next()
                    k.act(pt[:], sT[:], AF.Exp, [sT.tl], [pt.tl])
                    pend.append((kb, pt))

                def pv():
                    kb, pt = pend.pop(0)
                    k.mm(oT[:], vs_[:, kb, :], pt[:], kb == 0, kb == 31, [vs_.tl, pt.tl], [oT.tl])
                    k.mm(dn[:], onesb[:], pt[:], kb == 0, kb == 31, [pt.tl], [dn.tl])

                for kb in range(32):
                    score(kb)
                    if len(pend) > 2:
                        pv()
                    it_no += 1
                    if it_no % 2 == 0:
                        step_gen()
                while pend:
                    pv()
                rd = rden.next()
                k.recip(rd[:], dn[:], [dn.tl], [rd.tl])
                om = omb.next()
                k.tt(om[:], oT[:], rd[:], ALU.mult, [oT.tl, rd.tl], [om.tl])
                k.st(OMT[:, hh, tok0:tok0 + TT], om[:], om, w=[omt_tl[t]])
        while not gen_done[0]:
            step_gen()
        while late:
            late.pop(0)()
        sc.barrier()
        sc.emit()
    if stop_after <= 5:
        return k

    TC = 512
    NB = TC // 128
    with ExitStack() as es:
        cst = Buf(es, nc, "cst6", [128, 256], F32)
        k.ld(cst[:, 0:128], consts[:, 0:128], cst)
        k.ld(cst[:, 128:256], consts[:, 1152:1280], cst)
        identb = Buf(es, nc, "identb6", [128, 128], BF16)
        k.cp(identb[:], cst[:, 0:128], [cst.tl], [identb.tl])
        identb.tl.const = True
        onesb = Buf(es, nc, "onesb6", [128, 128], BF16)
        k.cp(onesb[:], cst[:, 128:256], [cst.tl], [onesb.tl])
        onesb.tl.const = True
        gc = Buf(es, nc, "gc6", [128, 72], F32)
        k.ld(gc[:], gcols, gc)
        gc.tl.const = True
        gp = Buf(es, nc, "gp6", [128, 2 * D], F32)
        k.ld(gp[:], gpost[:, 0:2 * D], gp)
        gp.tl.const = True
        mkt = Buf(es, nc, "mkt6", [128, KC, 256], BF16)
        k.ld(mkt[:], MKT, mkt, r=[mkt_tl])
        mkt.tl.const = True
        mvs = Buf(es, nc, "mvs6", [128, 2, D], BF16)
        k.ld(mvs[:], MV.rearrange("kb p d -> p kb d"), mvs, r=[mv_tl])
        mvs.tl.const = True
        actAr = mk(es, nc, "actA", [128, KC, TC], BF16, n=2)
        ssp = [Buf(es, nc, f"ssp{b}", [128, 4], F32) for b in range(NB)]
        junk2 = Buf(es, nc, "p6junk2", [128, 512], BF16)
        actB = Buf(es, nc, "actB", [128, KC, TC], BF16)
        wb = mk(es, nc, "p6wb", [128, KC, 512], BF16, n=2)
        ysb = [Buf(es, nc, f"ysb{b}", [128, D], F32) for b in range(NB)]
        xres = mk(es, nc, "xres", [128, D], F32, n=4)
        xs = mk(es, nc, "p6xs", [128, D], BF16, n=NB)
        stt_ = mk(es, nc, "p6st", [128, 4], F32, n=NB)
        pT = mk(es, nc, "p6pT", [128, 512], BF16, n=2, psum=True)
        ps = mk(es, nc, "p6ps", [128, 512], F32, n=6, psum=True)
        ptb = mk(es, nc, "p6pt", [128, TC], BF16, n=4)
        rdb = mk(es, nc, "p6rd", [128, TC], F32, n=2)
        MSC = float(512 ** -0.5)

        def proj_tm(W, name, src):
            for g in range(4):
                w_ = wb.next()
                ldw(k, wt, w_, W, name, 0, KC, g * 512, 512)
                for b in range(NB):
                    p_ = ps.next()
                    for kc in range(KC):
                        k.mm(p_[:], src[:, kc, b * 128:(b + 1) * 128], w_[:, kc, :], kc == 0, kc == KC - 1, [w_.tl, src.tl], [p_.tl])
                    k.act(ysb[b][:, g * 512:(g + 1) * 512], p_[:], AF.Copy, [p_.tl], [ysb[b].tl])
                    k.act(junk2[:], p_[:], AF.Square, [p_.tl], [junk2.tl, ssp[b].tl], accum=ssp[b][:, g:g + 1])

        def post_res(xr, goff):
            for b in range(NB):
                st = stt_.next()
                k.sc.op("dve", lambda e, o_=st[:, 0:1], i_=ssp[b][:, 0:4]: e.reduce_sum(out=o_, in_=i_, axis=AX.X), [ssp[b].tl], [st.tl])
                k.rstd_from_ss(st[:, 2:3], st[:, 0:1], st[:, 1:2], D, [st.tl], [st.tl])
                k.stt(ysb[b][:], ysb[b][:], st[:, 2:3], gp[:, goff:goff + D], ALU.mult, ALU.mult, [ysb[b].tl, st.tl], [ysb[b].tl])
                k.tt(xr[b][:], xr[b][:], ysb[b][:], ALU.add, [xr[b].tl, ysb[b].tl], [xr[b].tl], eng=("pool" if b % 2 else "dve"))

        for t in range(S // TC):
            tok0 = t * TC
            t5 = tok0 // TT
            actA = actAr.next()
            k.ld(actA[:, 0:8, :], OGT[:, :, tok0:tok0 + TC], actA, r=[ogt_tl[t5]])
            k.ld(actA[:, 8:16, :], OMT[:, :, tok0:tok0 + TC], actA, r=[omt_tl[t5]])
            xr = []
            for b in range(NB):
                xb_ = xres.next()
                k.ld(xb_[:], x[tok0 + b * 128: tok0 + (b + 1) * 128, :], xb_)
                xr.append(xb_)
            proj_tm(Wo, "Wo", actA)
            post_res(xr, 0)
            norm_T(k, xr, xs.tiles, stt_.tiles, gc, 16, identb, pT, actA)
            for g in range(4):
                w_ = wb.next()
                ldw(k, wt, w_, Wmq, "Wmq", 0, KC, g * 512, 512)
                for j in range(4):
                    p_ = ps.next()
                    for kc in range(KC):
                        k.mm(p_[:, 0:TC], w_[:, kc, j * 128:(j + 1) * 128], actA[:, kc, :], kc == 0, kc == KC - 1, [w_.tl, actA.tl], [p_.tl])
                    k.act(actB[:, g * 4 + j, :], p_[:, 0:TC], AF.Copy, [p_.tl], [actB.tl], scale=MSC)
            for hh in range(4):
                pts = []
                for kb in range(2):
                    sT = ps.next()
                    for dc in range(4):
                        k.mm(sT[:, 0:TC], mkt[:, hh * 4 + dc, kb * 128:(kb + 1) * 128], actB[:, hh * 4 + dc, :], dc == 0, dc == 3,
                             [actB.tl], [sT.tl])
                    pt = ptb.next()
                    k.act(pt[:], sT[:, 0:TC], AF.Exp, [sT.tl], [pt.tl])
                    pts.append(pt)
                dn = ps.next()
                for kb in range(2):
                    k.mm(dn[:, 0:TC], onesb[:], pts[kb][:], kb == 0, kb == 1, [pts[kb].tl], [dn.tl])
                rd = rdb.next()
                k.recip(rd[:], dn[:, 0:TC], [dn.tl], [rd.tl])
                for dvc in range(4):
                    po = ps.next()
                    for kb in range(2):
                        k.mm(po[:, 0:TC], mvs[:, kb, hh * 512 + dvc * 128: hh * 512 + (dvc + 1) * 128], pts[kb][:], kb == 0, kb == 1,
                             [pts[kb].tl], [po.tl])
                    k.tt(actA[:, hh * 4 + dvc, :], po[:, 0:TC], rd[:], ALU.mult, [po.tl, rd.tl], [actA.tl])
            proj_tm(Wmo, "Wmo", actA)
            post_res(xr, D)
            for b in range(NB):
                k.st(X2[tok0 + b * 128: tok0 + (b + 1) * 128, :], xr[b][:], xr[b], w=[x2_tl[t5]])
            norm_T(k, xr, xs.tiles, stt_.tiles, gc, 32, identb, pT, actB)
            k.st(H3T[:, :, tok0:tok0 + TC], actB[:], actB, w=[h3t_tl[t5]])
        sc.barrier()
        sc.emit()
    if stop_after <= 6:
        return k

    with ExitStack() as es:
        cvp = Buf(es, nc, "cvp", [128, 4, 88], F32)
        k.ld(cvp[:], convp.rearrange("p (a j) -> p a j", j=88), cvp)
        cvp.tl.const = True
        carry = [Buf(es, nc, f"carry{i}", [128, 2], F32) for i in range(88)]
        for i in range(88):
            k.memset(carry[i][:], 0.0, [carry[i].tl], eng="pool")
        hT = mk(es, nc, "p7hT", [128, KC, TT], BF16, n=2)
        wb = mk(es, nc, "p7wb", [128, KC, 1024], BF16, n=2)
        ps = mk(es, nc, "p7ps", [128, 512], F32, n=6, psum=True)
        ub = mk(es, nc, "p7ub", [128, 514], F32, n=6)
        cb = mk(es, nc, "p7cb", [128, 512], F32, n=6)
        gb = mk(es, nc, "p7gb", [128, 512], F32, n=3)
        atb = mk(es, nc, "p7at", [128, 4, 512], BF16, n=2)

        def conv(u, idx, out, eng0):
            w0 = cvp[:, 0, idx:idx + 1]
            w1 = cvp[:, 1, idx:idx + 1]
            w2 = cvp[:, 2, idx:idx + 1]
            bb = cvp[:, 3, idx:idx + 1]
            k.ts(out[:], u[:, 1:513], w1, bb, ALU.mult, ALU.add, [u.tl], [out.tl], eng=eng0)
            k.stt(out[:], u[:, 0:512], w0, out[:], ALU.mult, ALU.add, [u.tl, out.tl], [out.tl])
            k.stt(out[:], u[:, 2:514], w2, out[:], ALU.mult, ALU.add, [u.tl, out.tl], [out.tl])

        for t in range(NT):
            tok0 = t * TT
            h = hT.next()
            k.ld(h[:], H3T[:, :, tok0:tok0 + TT], h, r=[h3t_tl[t]])
            for j4 in range(11):
                w_ = wb.next()
                ldw(k, wt, w_, Wup, "Wup", 0, KC, j4 * 512, 512)
                ldw(k, wt, w_, Wup, "Wup", 0, KC, DFF + j4 * 512, 512, col_off=512)
                at = atb.next()
                for jj in range(4):
                    j = j4 * 4 + jj
                    us = []
                    for part in range(2):
                        p_ = ps.next()
                        for kc in range(KC):
                            k.mm(p_[:], w_[:, kc, part * 512 + jj * 128: part * 512 + (jj + 1) * 128], h[:, kc, :], kc == 0, kc == KC - 1,
                                 [w_.tl, h.tl], [p_.tl])
                        u = ub.next()
                        cy = carry[part * 44 + j]
                        k.cp(u[:, 0:2], cy[:], [cy.tl], [u.tl], eng="pool")
                        k.cp(u[:, 2:514], p_[:], [p_.tl], [u.tl], eng="act")
                        k.cp(cy[:], u[:, 512:514], [u.tl], [cy.tl], eng="pool")
                        us.append(u)
                    cg = cb.next()
                    conv(us[0], j, cg, "dve")
                    cv = cb.next()
                    conv(us[1], 44 + j, cv, "pool")
                    g_ = gb.next()
                    k.act(g_[:], cg[:], AF.Gelu_apprx_tanh, [cg.tl], [g_.tl])
                    k.tt(at[:, jj, :], g_[:], cv[:], ALU.mult, [g_.tl, cv.tl], [at.tl], eng="pool")
                if t == 0:
                    k.st(AT[:, j4 * 4:(j4 + 1) * 4, 0:511], at[:, :, 1:512], at, w=[at_tl[0]])
                else:
                    k.st(AT[:, j4 * 4:(j4 + 1) * 4, tok0 - 1:tok0 + 511], at[:, :, :], at, w=[at_tl[t], at_tl[t - 1]])
        cl = Buf(es, nc, "p7cl", [128, 88], F32)
        cl2 = Buf(es, nc, "p7cl2", [128, 88], F32)
        for i in range(88):
            cy = carry[i]
            k.stt(cl[:, i:i + 1], cy[:, 1:2], cvp[:, 1, i:i + 1], cvp[:, 3, i:i + 1], ALU.mult, ALU.add, [cy.tl], [cl.tl])
            k.stt(cl2[:, i:i + 1], cy[:, 0:1], cvp[:, 0, i:i + 1], cl[:, i:i + 1], ALU.mult, ALU.add, [cy.tl, cl.tl], [cl2.tl])
        gl = Buf(es, nc, "p7gl", [128, 44], F32)
        k.act(gl[:], cl2[:, 0:44], AF.Gelu_apprx_tanh, [cl2.tl], [gl.tl])
        al = Buf(es, nc, "p7al", [128, 44], BF16)
        k.tt(al[:], gl[:], cl2[:, 44:88], ALU.mult, [gl.tl, cl2.tl], [al.tl])
        for q4 in range(4):
            k.st(AT[:, q4 * 11:(q4 + 1) * 11, 4095:4096], al[:, q4 * 11:(q4 + 1) * 11].rearrange("p (j o) -> p j o", o=1), al, w=[at_tl[7]],
                 allow_slow_non_contiguous=True)
        sc.barrier()
        sc.emit()
    if stop_after <= 7:
        return k

    with ExitStack() as es:
        gp = Buf(es, nc, "gp8", [128, D], F32)
        k.ld(gp[:], gpost[:, 2 * D:3 * D], gp)
        gp.tl.const = True
        atl = mk(es, nc, "p8at", [128, 44, TT], BF16, n=2)
        xres = mk(es, nc, "p8x", [128, D], F32, n=4)
        wdb = mk(es, nc, "p8wd", [128, 1024], BF16, n=8)
        acc = mk(es, nc, "p8acc", [128, 1024], F32, n=4, psum=True)
        junk = mk(es, nc, "p8junk", [128, D], BF16, n=1)
        ysb = mk(es, nc, "p8y", [128, D], F32, n=4)
        stt_ = mk(es, nc, "p8st", [128, 4], F32, n=4)
        def load_at(i):
            a2 = atl.next()
            deps = [at_tl[i]] + ([at_tl[i + 1]] if i + 1 < NT else [])
            k.ld(a2[:], AT[:, :, i * TT:(i + 1) * TT], a2, r=deps)
            return a2

        for it in range(NT):
            tok0 = it * TT
            ys = []
            if it == 0:
                a_next = load_at(0)
            a_ = a_next
            xr = []
            for b in range(4):
                ys.append(ysb.next())
            for half in range(2):
                ac = [acc.next() for _ in range(4)]
                for j in range(44):
                    if j == 8 and half == 0 and it + 1 < NT:
                        a_next = load_at(it + 1)
                    if j == 8 and half == 1:
                        for b in range(4):
                            x_ = xres.next()
                            k.ld(x_[:], X2[tok0 + b * 128: tok0 + (b + 1) * 128, :], x_, r=[x2_tl[it]])
                            xr.append(x_)
                    wd = wdb.next()
                    k.ld(wd[:], Wdn[j * 128:(j + 1) * 128, half * 1024:(half + 1) * 1024], wd, r=[wt["Wdn"]])
                    for b in range(4):
                        for g in range(2):
                            k.mm(ac[b][:, g * 512:(g + 1) * 512], a_[:, j, b * 128:(b + 1) * 128], wd[:, g * 512:(g + 1) * 512], j == 0, j == 43,
                                 [a_.tl, wd.tl], [ac[b].tl])
                for b in range(4):
                    k.cp(ys[b][:, half * 1024:(half + 1) * 1024], ac[b][:], [ac[b].tl], [ys[b].tl], eng=("act" if b % 2 == 0 else "dve"))
            for b in range(4):
                rs = slice(tok0 + b * 128, tok0 + (b + 1) * 128)
                x_ = xr[b]
                st = stt_.next()
                jk = junk.next()
                k.act(jk[:], ys[b][:], AF.Square, [ys[b].tl], [jk.tl, st.tl], accum=st[:, 0:1])
                k.rstd_from_ss(st[:, 2:3], st[:, 0:1], st[:, 1:2], D, [st.tl], [st.tl])
                k.stt(ys[b][:], ys[b][:], st[:, 2:3], gp[:], ALU.mult, ALU.mult, [ys[b].tl, st.tl], [ys[b].tl])
                k.tt(ys[b][:], ys[b][:], x_[:], ALU.add, [ys[b].tl, x_.tl], [ys[b].tl], eng="pool")
                k.st(y_out[rs, :], ys[b][:], ys[b], w=[yout_tl])
        sc.barrier()
        sc.emit()
    return k


def ldw(k, wt, buf, W, name, r0, nkc, c0, ncols, col_off=0):
    src = W[r0:r0 + nkc * 128, c0:c0 + ncols].rearrange("(kc p) c -> p kc c", p=128)
    return k.ld(buf[:, 0:nkc, col_off:col_off + ncols], src, buf, r=[wt[name]])


def norm_T(k, blocks, xs_list, st_list, gc, gc_off, identb, pT, dstT):
    nb = len(blocks)
    for b, xbuf in enumerate(blocks):
        xs = xs_list[b]
        st = st_list[b]
        k.act(xs[:], xbuf[:], AF.Square, [xbuf.tl], [xs.tl, st.tl], accum=st[:, 0:1])
        k.rstd_from_ss(st[:, 2:3], st[:, 0:1], st[:, 1:2], D, [st.tl], [st.tl])
        k.ts(xs[:], xbuf[:], st[:, 2:3], None, ALU.mult, None, [xbuf.tl, st.tl], [xs.tl])
    for kc in range(KC):
        p_ = pT.next()
        for b in range(nb):
            k.tr(p_[:, b * 128:(b + 1) * 128], xs_list[b][:, kc * 128:(kc + 1) * 128], identb[:],
                 [xs_list[b].tl, identb.tl], [p_.tl])
        if kc % 2 == 0:
            k.act(dstT[:, kc, 0:nb * 128], p_[:, 0:nb * 128], AF.Copy, [p_.tl], [dstT.tl],
                  scale=gc[:, gc_off + kc:gc_off + kc + 1])
        else:
            k.ts(dstT[:, kc, 0:nb * 128], p_[:, 0:nb * 128], gc[:, gc_off + kc:gc_off + kc + 1], None,
                 ALU.mult, None, [p_.tl, gc.tl], [dstT.tl])


def _consts():
    a = np.arange(128)
    j = a[:, None]
    i = a[None, :]
    same = (j // 64) == (i // 64)
    c = np.float32(-1.0 / 16.0)
    Lf = lambda ii, jj: (((ii // 64) == (jj // 64)) & (jj <= ii)).astype(np.float32)
    Lb = lambda ii, jj: (((ii // 64) == (jj // 64)) & (jj >= ii)).astype(np.float32)
    reff = (i // 64) * 64 + 32
    refb = (i // 64) * 64 + 31
    out = np.zeros((128, 1408), np.float32)
    out[:, 0:128] = np.eye(128, dtype=np.float32)
    out[:, 128:256] = c * (Lf(i, j) - Lf(reff, j))
    out[:, 256:384] = c * Lf(i, j)
    out[:, 384:512] = c * (same & (j > i)).astype(np.float32)
    out[:, 512:640] = c * (Lb(i, j) - Lb(refb, j))
    out[:, 640:768] = c * Lb(i, j)
    out[:, 768:896] = c * (same & (j < i)).astype(np.float32)
    out[:, 896:1024] = (same & (j <= i)).astype(np.float32)
    out[:, 1024:1152] = (same & (j > i)).astype(np.float32)
    out[:, 1152:1280] = 1.0
    return out


def _col(v, n):
    return np.ascontiguousarray(np.asarray(v, np.float32).reshape(n, 128).T)


def make_in_maps(inp):
    f = lambda n: np.ascontiguousarray(np.asarray(inp[n], np.float32)[0])
    shared = {
        "w_in": f("w_in"), "w_q_up": f("mla_w_q_up"), "w_kv_up": f("mla_w_kv_up"), "w_out": f("w_out"),
        "w_mq": f("w_mem_q"), "w_mk": f("w_mem_k"), "w_mv": f("w_mem_v"), "w_mo": f("w_mem_o"),
        "w_up": f("w_ffn_up"), "w_dn": f("w_ffn_down"),
    }
    shared["consts"] = _consts()
    shared["gcols"] = np.ascontiguousarray(np.concatenate([
        _col(f("norm_mix_pre"), 16), _col(f("norm_mem_pre"), 16), _col(f("norm_ffn_pre"), 16),
        _col(f("mem_kv_norm"), 16), _col(f("mla_q_norm"), 4), _col(f("mla_kv_norm"), 4)], axis=1))
    gp = np.concatenate([f("norm_mix_post"), f("norm_mem_post"), f("norm_ffn_post"), f("gla_out_norm")])
    shared["gpost"] = np.ascontiguousarray(np.broadcast_to(gp[None, :], (128, gp.shape[0])))
    cw = f("ffn_conv_w")
    shared["convp"] = np.ascontiguousarray(np.concatenate(
        [_col(cw[0], 88), _col(cw[1], 88), _col(cw[2], 88), _col(f("ffn_conv_b"), 88)], axis=1))
    w2 = np.zeros((33, 1024), np.float32)
    w2[0:16, 0:512] = f("gla_gate_w2_fwd")
    w2[32, 0:512] = f("gla_gate_b_fwd")
    w2[16:32, 512:1024] = f("gla_gate_w2_bwd")
    w2[32, 512:1024] = f("gla_gate_b_bwd")
    shared["w2aug"] = w2
    half = 32
    freqs = (np.float32(10000.0) ** (-np.arange(half, dtype=np.float32) / np.float32(half))).astype(np.float32)
    ang = (np.arange(S, dtype=np.float32)[None, :] * freqs[:, None]).astype(np.float32)
    ang = np.concatenate([ang, ang], axis=0)
    shared["ropet"] = np.ascontiguousarray(np.concatenate([np.cos(ang), np.sin(ang)], axis=1).astype(np.float32))
    xs = [np.asarray(inp["x_prompt"], np.float32)[b] for b in range(2)] + \
         [np.asarray(inp["x_sample"], np.float32)[b] for b in range(4)]
    ms = [np.asarray(inp["mem_prompt"], np.float32)[b] for b in range(2)] + \
         [np.asarray(inp["mem_sample"], np.float32)[b] for b in range(4)]
    zeros = {n_: np.zeros_like(v) for n_, v in shared.items()}
    zx = np.zeros((S, D), np.float32)
    zm = np.zeros((256, D), np.float32)
    maps = []
    for c in range(8):
        if c in SPARE_CORES:
            m = dict(zeros)
            m["x"] = zx
            m["mem"] = zm
        else:
            s = SEQ_CORES.index(c)
            m = dict(shared)
            m["x"] = np.ascontiguousarray(xs[s])
            m["mem"] = np.ascontiguousarray(ms[s])
        maps.append(m)
    return maps


def kernel(**inputs):
    nc = bass.Bass("TRN2", target_bir_lowering=False)
    build(nc)
    maps = make_in_maps(inputs)
    res = run_bass_kernel_spmd(nc, maps, core_ids=list(range(8)))
    ys = [np.asarray(res.results[c]["y"], np.float32) for c in SEQ_CORES]
    return (np.stack(ys[0:2], axis=0), np.stack(ys[2:6], axis=0))
```
